# Optimizing a Trainium2 kernel written in Bass

```python
import math
import jax, jax.numpy as jnp
from jax import lax
import numpy as np

D_MODEL = 1024
BATCH = 2
SEQ = 8192
DEPTH = 4
DEC_BATCH = 128
DEC_SEQ = 1
PAST_LEN = 8192
PAGE_SIZE = 128

N_META = 16
HEAD_DIM = 64
ATT_WIDTH = D_MODEL // 2
ATT_HEADS = ATT_WIDTH // HEAD_DIM
ATT_KV_HEADS = ATT_HEADS // 4
KV_WIDTH = ATT_KV_HEADS * HEAD_DIM
WINDOW = 128
ATT_BLOCK = 128
ROPE_THETA = 10000.0
S5_WIDTH = D_MODEL // 4
S5_GROUP_CH = 16
S5_GROUPS = S5_WIDTH // S5_GROUP_CH
S5_STATE = 64
SSD_WIDTH = D_MODEL // 4
SSD_HEAD_DIM = 64
SSD_HEADS = SSD_WIDTH // SSD_HEAD_DIM
SSD_GROUPS = 2
SSD_STATE = 64
SSD_CONV = 4
SSD_CHUNK = 128
SSD_CONV_DIM = SSD_WIDTH + 2 * SSD_GROUPS * SSD_STATE
MIX_WIDTH = ATT_WIDTH + S5_WIDTH + SSD_WIDTH
N_IN = ATT_WIDTH + 2 * KV_WIDTH + S5_WIDTH + SSD_WIDTH + SSD_CONV_DIM + SSD_HEADS
FFN_HIDDEN = ((8 * D_MODEL + 3 * 256 - 1) // (3 * 256)) * 256
NORM_EPS = 1e-6

kernel_name = "hymba_swa_s5_ssd_decoder_step"

F32 = jnp.float32


def rmsnorm(x, g):
    xf = x.astype(F32)
    y = xf * lax.rsqrt(jnp.mean(xf * xf, axis=-1, keepdims=True) + NORM_EPS)
    return (y * g.astype(F32)).astype(x.dtype)


def rope(x, pos):
    half = HEAD_DIM // 2
    inv = ROPE_THETA ** (-jnp.arange(half, dtype=F32) / half)
    ang = pos.astype(F32)[:, None] * inv[None, :]
    cos = jnp.cos(ang)[:, None, :]
    sin = jnp.sin(ang)[:, None, :]
    xf = x.astype(F32)
    x1, x2 = xf[..., :half], xf[..., half:]
    return jnp.concatenate([x1 * cos - x2 * sin, x2 * cos + x1 * sin], axis=-1).astype(x.dtype)


def sink_attention(q, k, v, qpos, kpos, sinks):
    *lead, tq, nh, hd = q.shape
    kvh = k.shape[-2]
    rep = nh // kvh
    qg = q.reshape(*lead, tq, kvh, rep, hd)
    s = jnp.einsum('...qgrd,...kgd->...grqk', qg, k, preferred_element_type=F32) * (hd ** -0.5)
    delta = qpos[..., :, None] - kpos[..., None, :]
    ok = (delta >= 0) & (delta <= WINDOW) & (kpos[..., None, :] >= 0)
    s = jnp.where(ok[..., None, None, :, :], s, -jnp.inf)
    sink = sinks.astype(F32).reshape(kvh, rep)[:, :, None, None]
    m = jnp.maximum(jnp.max(s, axis=-1, keepdims=True), sink)
    p = jnp.exp(s - m)
    den = jnp.sum(p, axis=-1, keepdims=True) + jnp.exp(sink - m)
    w = (p / den).astype(v.dtype)
    o = jnp.einsum('...grqk,...kgd->...qgrd', w, v)
    return o.reshape(*lead, tq, nh * hd)


def attn_prompt(q, k, v, sinks):
    b, L = q.shape[:2]
    front = ATT_BLOCK - N_META
    pad = lambda t: jnp.pad(t, ((0, 0), (front, 0), (0, 0), (0, 0)))
    qp, kp, vp = pad(q), pad(k), pad(v)
    Lp = L + front
    nb = Lp // ATT_BLOCK
    pos = jnp.arange(Lp, dtype=jnp.int32) - front
    qb = qp.reshape(b, nb, ATT_BLOCK, ATT_HEADS, HEAD_DIM)
    kb = kp.reshape(b, nb, ATT_BLOCK, ATT_KV_HEADS, HEAD_DIM)
    vb = vp.reshape(b, nb, ATT_BLOCK, ATT_KV_HEADS, HEAD_DIM)
    two = lambda t: jnp.concatenate(
        [jnp.pad(t[:, :-1], ((0, 0), (1, 0), (0, 0), (0, 0), (0, 0))), t], axis=2)
    kk, vv = two(kb), two(vb)
    pb = pos.reshape(nb, ATT_BLOCK)
    kpos = jnp.concatenate([jnp.pad(pb[:-1], ((1, 0), (0, 0)), constant_values=-1), pb], axis=1)
    o = sink_attention(qb, kk, vv, pb, kpos, sinks)
    return o.reshape(b, Lp, ATT_WIDTH)[:, front:]


def attn_sample(q, k, v, ck, cv, sinks):
    T = q.shape[1]
    W = ck.shape[1]
    qpos = PAST_LEN + jnp.arange(T, dtype=jnp.int32)
    kpos = jnp.concatenate([PAST_LEN - W + jnp.arange(W, dtype=jnp.int32), qpos])
    kk = jnp.concatenate([ck.astype(k.dtype), k], axis=1)
    vv = jnp.concatenate([cv.astype(v.dtype), v], axis=1)
    o = sink_attention(q, kk, vv, qpos, kpos, sinks)
    return o, kk[:, -W:], vv[:, -W:]


def s5_scan(u, h0_re, h0_im, a_re, a_im, log_dt, b_re, b_im, c_re, c_im, d):
    dt = jnp.exp(log_dt.astype(F32))[:, None]
    ar, ai = a_re.astype(F32), a_im.astype(F32)
    mag = jnp.exp(dt * ar)
    abar_re, abar_im = mag * jnp.cos(dt * ai), mag * jnp.sin(dt * ai)
    den = ar * ar + ai * ai
    xr = abar_re - 1.0
    f_re = (xr * ar + abar_im * ai) / den
    f_im = (abar_im * ar - xr * ai) / den
    br, bi = b_re.astype(F32), b_im.astype(F32)
    bb_re = f_re[..., None] * br - f_im[..., None] * bi
    bb_im = f_re[..., None] * bi + f_im[..., None] * br
    bu_re = jnp.einsum('gnc,btgc->btgn', bb_re, u)
    bu_im = jnp.einsum('gnc,btgc->btgn', bb_im, u)
    a_re_t = jnp.broadcast_to(abar_re, bu_re.shape)
    a_im_t = jnp.broadcast_to(abar_im, bu_re.shape)

    def combine(e1, e2):
        a1r, a1i, b1r, b1i = e1
        a2r, a2i, b2r, b2i = e2
        return (a2r * a1r - a2i * a1i, a2r * a1i + a2i * a1r,
                a2r * b1r - a2i * b1i + b2r, a2r * b1i + a2i * b1r + b2i)

    cr, ci, hr, hi = lax.associative_scan(combine, (a_re_t, a_im_t, bu_re, bu_im), axis=1)
    h0r = h0_re.astype(F32)[:, None]
    h0i = h0_im.astype(F32)[:, None]
    hr = hr + cr * h0r - ci * h0i
    hi = hi + cr * h0i + ci * h0r
    y = (jnp.einsum('gcn,btgn->btgc', c_re.astype(F32), hr)
         - jnp.einsum('gcn,btgn->btgc', c_im.astype(F32), hi)
         + d.astype(F32).reshape(S5_GROUPS, S5_GROUP_CH) * u)
    return y, hr[:, -1], hi[:, -1]


def causal_conv(xbc, conv_state, w, bias):
    T = xbc.shape[1]
    xp = jnp.concatenate([conv_state.astype(xbc.dtype), xbc], axis=1)
    out = bias
    for j in range(SSD_CONV):
        out = out + xp[:, j:j + T] * w[j]
    return jax.nn.silu(out), xp[:, -(SSD_CONV - 1):]


def ssd_scan(x, dt, A, Bm, Cm, Dh, h0, front):
    b, T = x.shape[:2]
    total = front + T
    q = min(SSD_CHUNK, total)
    back = (-total) % q
    padt = lambda t: jnp.pad(t, [(0, 0), (front, back)] + [(0, 0)] * (t.ndim - 2))
    x, dt, Bm, Cm = padt(x), padt(dt), padt(Bm), padt(Cm)
    nc = (total + back) // q
    rep = SSD_HEADS // SSD_GROUPS
    ch = lambda t: t.reshape(b, nc, q, *t.shape[2:])
    Bc = ch(jnp.repeat(Bm, rep, axis=2))
    Cc = ch(jnp.repeat(Cm, rep, axis=2))
    xd = ch(x * dt[..., None])
    cs = jnp.cumsum(ch(dt * A), axis=2)
    seg = cs[:, :, :, None, :] - cs[:, :, None, :, :]
    causal = jnp.tril(jnp.ones((q, q), dtype=bool))[None, None, :, :, None]
    Lmat = jnp.exp(jnp.where(causal, seg, -jnp.inf))
    scores = jnp.einsum('bcihn,bcjhn->bcijh', Cc, Bc) * Lmat
    y_diag = jnp.einsum('bcijh,bcjhp->bcihp', scores, xd)
    decay = jnp.exp(cs[:, :, -1:, :] - cs)
    states = jnp.einsum('bcjhn,bcjh,bcjhp->bchpn', Bc, decay, xd)
    chunk_decay = jnp.exp(cs[:, :, -1, :])

    def step(h, inp):
        dcy, st = inp
        return dcy[:, :, None, None] * h + st, h

    h_last, h_prev = lax.scan(step, h0.astype(F32),
                              (jnp.moveaxis(chunk_decay, 1, 0), jnp.moveaxis(states, 1, 0)))
    h_prev = jnp.moveaxis(h_prev, 0, 1)
    y_off = jnp.einsum('bcihn,bchpn,bcih->bcihp', Cc, h_prev, jnp.exp(cs))
    y = (y_diag + y_off).reshape(b, nc * q, SSD_HEADS, SSD_HEAD_DIM) + Dh.astype(F32)[:, None] * x
    return y[:, front:front + T], h_last


def token_mixers(hn, lp, st, prompt):
    ck, cv, s5r0, s5i0, conv0, ssd0 = st
    b, T, _ = hn.shape
    sizes = (ATT_WIDTH, KV_WIDTH, KV_WIDTH, S5_WIDTH, SSD_WIDTH, SSD_CONV_DIM, SSD_HEADS)
    offs = [int(o) for o in np.cumsum(sizes)[:-1]]
    proj = hn @ lp['w_in']
    q, k, v, u, z, xbc, dtr = jnp.split(proj, offs, axis=-1)

    pos = jnp.arange(T, dtype=jnp.int32) + (0 if prompt else PAST_LEN)
    q = rope(q.reshape(b, T, ATT_HEADS, HEAD_DIM), pos)
    k = rope(k.reshape(b, T, ATT_KV_HEADS, HEAD_DIM), pos)
    v = v.reshape(b, T, ATT_KV_HEADS, HEAD_DIM)
    if prompt:
        o_att = attn_prompt(q, k, v, lp['attn_sinks'])
        nk, nv = k[:, -WINDOW:], v[:, -WINDOW:]
    else:
        o_att, nk, nv = attn_sample(q, k, v, ck, cv, lp['attn_sinks'])
    o_att = rmsnorm(o_att, lp['attn_out_g'])

    y5, s5r, s5i = s5_scan(u.astype(F32).reshape(b, T, S5_GROUPS, S5_GROUP_CH), s5r0, s5i0,
                           lp['s5_a_re'], lp['s5_a_im'], lp['s5_log_dt'], lp['s5_b_re'], lp['s5_b_im'],
                           lp['s5_c_re'], lp['s5_c_im'], lp['s5_d'])
    y5 = jax.nn.gelu(y5.reshape(b, T, S5_WIDTH))
    y5 = y5 * jax.nn.sigmoid(y5 @ lp['s5_glu_w'].astype(F32) + lp['s5_glu_b'].astype(F32))
    o_s5 = rmsnorm(y5, lp['s5_out_g']).astype(hn.dtype)

    xbc, nconv = causal_conv(xbc, conv0, lp['ssd_conv_w'], lp['ssd_conv_b'])
    xs, bm, cm = jnp.split(xbc, [SSD_WIDTH, SSD_WIDTH + SSD_GROUPS * SSD_STATE], axis=-1)
    dt = jax.nn.softplus(dtr.astype(F32) + lp['ssd_dt_bias'].astype(F32))
    A = -jnp.exp(lp['ssd_a_log'].astype(F32))
    front = SSD_CHUNK - N_META if prompt else 0
    yc, hssd = ssd_scan(xs.astype(F32).reshape(b, T, SSD_HEADS, SSD_HEAD_DIM), dt, A,
                        bm.astype(F32).reshape(b, T, SSD_GROUPS, SSD_STATE),
                        cm.astype(F32).reshape(b, T, SSD_GROUPS, SSD_STATE),
                        lp['ssd_d'], ssd0, front)
    yc = yc.reshape(b, T, SSD_WIDTH) * jax.nn.silu(z.astype(F32))
    o_ssd = rmsnorm(yc, lp['ssd_norm_g']).astype(hn.dtype)

    mixed = jnp.concatenate([o_att, o_s5, o_ssd], axis=-1) @ lp['w_out']
    return mixed, (nk, nv, s5r, s5i, nconv, hssd)


def swiglu(h, wg, wu, wd):
    return (jax.nn.silu(h @ wg) * (h @ wu)) @ wd


def decoder_layer(x, lp, st, prompt):
    mixed, new_st = token_mixers(rmsnorm(x, lp['ln1_g']), lp, st, prompt)
    x = x + mixed
    x = x + swiglu(rmsnorm(x, lp['ln2_g']), lp['w_gate'], lp['w_up'], lp['w_down'])
    return x, new_st


def setup_inputs(seed: int = 0) -> dict:
    key = jax.random.key(seed)
    ks = iter(jax.random.split(key, 48))
    nrm = lambda shape, scale: jax.random.normal(next(ks), shape, F32) * scale
    unif = lambda shape, lo, hi: jax.random.uniform(next(ks), shape, F32, lo, hi)
    w_c = min(WINDOW, PAST_LEN)
    n_idx = jnp.arange(S5_STATE, dtype=F32)
    dt0 = jnp.exp(unif((DEPTH, SSD_HEADS), math.log(1e-3), math.log(1e-1)))
    return {
        'x_prompt': nrm((BATCH, SEQ, D_MODEL), 1.0),
        'x_sample': nrm((DEC_BATCH, DEC_SEQ, D_MODEL), 1.0),
        'cache_k': nrm((DEPTH, DEC_BATCH, w_c, ATT_KV_HEADS, HEAD_DIM), 1.0),
        'cache_v': nrm((DEPTH, DEC_BATCH, w_c, ATT_KV_HEADS, HEAD_DIM), 1.0),
        'state_s5_re': nrm((DEPTH, DEC_BATCH, S5_GROUPS, S5_STATE), 0.1),
        'state_s5_im': nrm((DEPTH, DEC_BATCH, S5_GROUPS, S5_STATE), 0.1),
        'state_ssd_conv': nrm((DEPTH, DEC_BATCH, SSD_CONV - 1, SSD_CONV_DIM), 1.0),
        'state_ssd': nrm((DEPTH, DEC_BATCH, SSD_HEADS, SSD_HEAD_DIM, SSD_STATE), 0.1),
        'meta_tokens': nrm((N_META, D_MODEL), 1.0),
        'ln1_g': 1.0 + nrm((DEPTH, D_MODEL), 0.02),
        'w_in': nrm((DEPTH, D_MODEL, N_IN), D_MODEL ** -0.5),
        'attn_sinks': nrm((DEPTH, ATT_HEADS), 0.5),
        'attn_out_g': 1.0 + nrm((DEPTH, ATT_WIDTH), 0.02),
        's5_a_re': -0.5 + nrm((DEPTH, S5_GROUPS, S5_STATE), 0.01),
        's5_a_im': math.pi * n_idx + nrm((DEPTH, S5_GROUPS, S5_STATE), 0.01),
        's5_log_dt': unif((DEPTH, S5_GROUPS), math.log(1e-3), math.log(1e-1)),
        's5_b_re': nrm((DEPTH, S5_GROUPS, S5_STATE, S5_GROUP_CH), (2 * S5_GROUP_CH) ** -0.5),
        's5_b_im': nrm((DEPTH, S5_GROUPS, S5_STATE, S5_GROUP_CH), (2 * S5_GROUP_CH) ** -0.5),
        's5_c_re': nrm((DEPTH, S5_GROUPS, S5_GROUP_CH, S5_STATE), (2 * S5_STATE) ** -0.5),
        's5_c_im': nrm((DEPTH, S5_GROUPS, S5_GROUP_CH, S5_STATE), (2 * S5_STATE) ** -0.5),
        's5_d': nrm((DEPTH, S5_WIDTH), 1.0),
        's5_glu_w': nrm((DEPTH, S5_WIDTH, S5_WIDTH), S5_WIDTH ** -0.5),
        's5_glu_b': nrm((DEPTH, S5_WIDTH), 0.01),
        's5_out_g': 1.0 + nrm((DEPTH, S5_WIDTH), 0.02),
        'ssd_conv_w': nrm((DEPTH, SSD_CONV, SSD_CONV_DIM), SSD_CONV ** -0.5),
        'ssd_conv_b': nrm((DEPTH, SSD_CONV_DIM), 0.01),
        'ssd_dt_bias': dt0 + jnp.log(-jnp.expm1(-dt0)),
        'ssd_a_log': jnp.log(unif((DEPTH, SSD_HEADS), 1.0, 16.0)),
        'ssd_d': 1.0 + nrm((DEPTH, SSD_HEADS), 0.02),
        'ssd_norm_g': 1.0 + nrm((DEPTH, SSD_WIDTH), 0.02),
        'w_out': nrm((DEPTH, MIX_WIDTH, D_MODEL), MIX_WIDTH ** -0.5),
        'ln2_g': 1.0 + nrm((DEPTH, D_MODEL), 0.02),
        'w_gate': nrm((DEPTH, D_MODEL, FFN_HIDDEN), D_MODEL ** -0.5),
        'w_up': nrm((DEPTH, D_MODEL, FFN_HIDDEN), D_MODEL ** -0.5),
        'w_down': nrm((DEPTH, FFN_HIDDEN, D_MODEL), FFN_HIDDEN ** -0.5),
        'lnf_g': 1.0 + nrm((D_MODEL,), 0.02),
    }


def reference(x_prompt, x_sample, cache_k, cache_v, state_s5_re, state_s5_im, state_ssd_conv, state_ssd,
              meta_tokens, ln1_g, w_in, attn_sinks, attn_out_g, s5_a_re, s5_a_im, s5_log_dt,
              s5_b_re, s5_b_im, s5_c_re, s5_c_im, s5_d, s5_glu_w, s5_glu_b, s5_out_g,
              ssd_conv_w, ssd_conv_b, ssd_dt_bias, ssd_a_log, ssd_d, ssd_norm_g, w_out,
              ln2_g, w_gate, w_up, w_down, lnf_g):
    b = x_prompt.shape[0]
    meta = jnp.broadcast_to(meta_tokens[None].astype(x_prompt.dtype), (b, N_META, D_MODEL))
    xp = jnp.concatenate([meta, x_prompt], axis=1)
    xs = x_sample
    named = (('ln1_g', ln1_g), ('w_in', w_in), ('attn_sinks', attn_sinks), ('attn_out_g', attn_out_g),
             ('s5_a_re', s5_a_re), ('s5_a_im', s5_a_im), ('s5_log_dt', s5_log_dt),
             ('s5_b_re', s5_b_re), ('s5_b_im', s5_b_im), ('s5_c_re', s5_c_re), ('s5_c_im', s5_c_im),
             ('s5_d', s5_d), ('s5_glu_w', s5_glu_w), ('s5_glu_b', s5_glu_b), ('s5_out_g', s5_out_g),
             ('ssd_conv_w', ssd_conv_w), ('ssd_conv_b', ssd_conv_b), ('ssd_dt_bias', ssd_dt_bias),
             ('ssd_a_log', ssd_a_log), ('ssd_d', ssd_d), ('ssd_norm_g', ssd_norm_g), ('w_out', w_out),
             ('ln2_g', ln2_g), ('w_gate', w_gate), ('w_up', w_up), ('w_down', w_down))
    new_p = [[] for _ in range(6)]
    new_s = [[] for _ in range(6)]
    for l in range(DEPTH):
        lp = {name: arr[l] for name, arr in named}
        st_p = (None, None,
                jnp.zeros((b, S5_GROUPS, S5_STATE), F32), jnp.zeros((b, S5_GROUPS, S5_STATE), F32),
                jnp.zeros((b, SSD_CONV - 1, SSD_CONV_DIM), xp.dtype),
                jnp.zeros((b, SSD_HEADS, SSD_HEAD_DIM, SSD_STATE), F32))
        xp, sp = decoder_layer(xp, lp, st_p, True)
        st_s = (cache_k[l], cache_v[l], state_s5_re[l], state_s5_im[l], state_ssd_conv[l], state_ssd[l])
        xs, ss = decoder_layer(xs, lp, st_s, False)
        for i in range(6):
            new_p[i].append(sp[i])
            new_s[i].append(ss[i])
    y_prompt = rmsnorm(xp, lnf_g)[:, N_META:]
    y_sample = rmsnorm(xs, lnf_g)
    k_p, v_p, s5re_p, s5im_p, conv_p, ssd_p = [jnp.stack(t) for t in new_p]
    k_s, v_s, s5re_s, s5im_s, conv_s, ssd_s = [jnp.stack(t) for t in new_s]
    return (y_prompt, y_sample, k_p, v_p, s5re_p, s5im_p, conv_p, ssd_p,
            k_s, v_s, s5re_s, s5im_s, conv_s, ssd_s)
```

```python
import numpy as np
import concourse.bass as bass
import concourse.mybir as mybir
F32=mybir.dt.float32; BF16=mybir.dt.bfloat16; I32=mybir.dt.int32
AF=mybir.ActivationFunctionType; ALU=mybir.AluOpType; AX=mybir.AxisListType

class Buf:
    def __init__(self, k, name, t):
        self.k=k; self.name=name; self.t=t
        self.lw=[]
        self.rd=[]
        self.is_dram=False
        self.ldsem=None; self.ldcnt=0
        self.stsem=None; self.stcnt=0
    def __getitem__(self, idx): return self.t[idx]

class KB:
    def __init__(self):
        self.nc = bass.Bass("TRN2", target_bir_lowering=False)
        nc=self.nc
        self.eng={'pe':nc.tensor,'act':nc.scalar,'dve':nc.vector,'pool':nc.gpsimd,'sp':nc.sync}
        self.sem={}; self.cnt={}; self.seen={}
        self._ctx=[]
        for e in ['pe','act','dve','pool']:
            self.sem[e]=self._enter(nc.semaphore("s_"+e)); self.cnt[e]=0
        for e in self.eng: self.seen[e]={}
        self.pending={e:[] for e in self.eng}
        self.nsem=4; self.dmasems={}; self.d2d=None; self.d2dcnt=0; self.outs=[]
    def _enter(self, cm):
        v=cm.__enter__(); self._ctx.append(cm); return v
    def close(self):
        for cm in reversed(self._ctx): cm.__exit__(None,None,None)
    def newsem(self, name):
        self.nsem+=1
        return self._enter(self.nc.semaphore(name+"_%d"%self.nsem))
    def sb(self, name, shape, dt=F32):
        return Buf(self, name, self._enter(self.nc.sbuf_tensor(name, list(shape), dt)))
    def ps(self, name, shape, dt=F32):
        return Buf(self, name, self._enter(self.nc.psum_tensor(name, list(shape), dt)))
    def dram(self, name, shape, dt=F32, kind="Internal"):
        b=Buf(self, name, self.nc.dram_tensor(name, list(shape), dt, kind=kind).ap()); b.is_dram=True
        if kind=="ExternalOutput": self.outs.append(b)
        return b
    def need(self, e, ev):
        s,v=ev
        if s in self.dmasems:
            v=max(v,self.dmasems[s][0])
        key=id(s)
        if self.seen[e].get(key,0)>=v: return
        self.seen[e][key]=v
        self.eng[e].wait_ge(s,v)
    def op(self, e, fn, rd=(), wr=(), sig=True):
        for b in rd:
            for ev in b.lw: self.need(e,ev)
        for b in wr:
            for ev in b.lw: self.need(e,ev)
            for ev in b.rd: self.need(e,ev)
        ins=fn(self.eng[e])
        if sig:
            self.cnt[e]+=1
            ins.then_inc(self.sem[e],1)
            ev=(self.sem[e],self.cnt[e])
            for b in self.pending[e]: b.rd.append(ev)
            self.pending[e]=[]
            for b in wr: b.lw[:]=[ev]; b.rd.clear()
            for b in rd:
                if b not in wr: b.rd.append(ev)
        else:
            for b in rd: self.pending[e].append(b)
        return ins
    def dma(self, out_ap, in_ap, dst, src, q='sp', **kw):
        for ev in src.lw: self.need(q,ev)
        for ev in dst.lw: self.need(q,ev)
        for ev in dst.rd: self.need(q,ev)
        if not dst.is_dram:
            if dst.ldsem is None: dst.ldsem=self.newsem("ld_"+dst.name)
            dst.ldcnt+=16; sem=dst.ldsem; val=dst.ldcnt
        elif not src.is_dram:
            if src.stsem is None: src.stsem=self.newsem("st_"+src.name)
            src.stcnt+=16; sem=src.stsem; val=src.stcnt
        else:
            if self.d2d is None: self.d2d=self.newsem("d2d")
            self.d2dcnt+=16; sem=self.d2d; val=self.d2dcnt
        ins=self.eng[q].dma_start(out=out_ap, in_=in_ap, **kw)
        ins.then_inc(sem,16)
        ev=(sem,val); self.dmasems[sem]=[val]
        dst.lw[:]=[x for x in dst.lw if x[0] is not sem]+[ev]
        dst.rd.clear()
        src.rd[:]=[x for x in src.rd if x[0] is not sem]+[ev]
        return ins
    def finish(self, q='sp'):
        for b in self.outs:
            for ev in b.lw: self.need(q,ev)

EPS=1e-6
TWO_PI_SAFE=6.283185

def sincos(k, turns, s_out, c_out, tmp_i, tmp_f, shape_key):
    k.op('dve',lambda e:e.tensor_copy(out=tmp_i[:],in_=turns[:]),rd=[turns],wr=[tmp_i])
    k.op('dve',lambda e:e.tensor_copy(out=tmp_f[:],in_=tmp_i[:]),rd=[tmp_i],wr=[tmp_f])
    k.op('dve',lambda e:e.tensor_tensor(out=tmp_f[:],in0=turns[:],in1=tmp_f[:],op=ALU.subtract),rd=[turns,tmp_f],wr=[tmp_f])
    k.op('act',lambda e:e.activation(out=s_out[:],in_=tmp_f[:],func=AF.Sin,scale=TWO_PI_SAFE),rd=[tmp_f],wr=[s_out])
    k.op('dve',lambda e:e.tensor_scalar(out=turns[:],in0=turns[:],scalar1=0.25,scalar2=None,op0=ALU.add),rd=[turns],wr=[turns])
    k.op('dve',lambda e:e.tensor_copy(out=tmp_i[:],in_=turns[:]),rd=[turns],wr=[tmp_i])
    k.op('dve',lambda e:e.tensor_copy(out=tmp_f[:],in_=tmp_i[:]),rd=[tmp_i],wr=[tmp_f])
    k.op('dve',lambda e:e.tensor_tensor(out=tmp_f[:],in0=turns[:],in1=tmp_f[:],op=ALU.subtract),rd=[turns,tmp_f],wr=[tmp_f])
    k.op('act',lambda e:e.activation(out=c_out[:],in_=tmp_f[:],func=AF.Sin,scale=TWO_PI_SAFE),rd=[tmp_f],wr=[c_out])


NCW = 1924
NC = 76
G = 256

class Arena:
    def __init__(self, k, words):
        self.k = k; self.words = words
        self.t = k._enter(k.nc.sbuf_tensor("arena", [128, words], F32))
        self.off = 0; self.marks = []
    def buf(self, name, shape, dt=F32):
        n = int(np.prod(shape[1:]))
        w = n if dt in (F32, I32) else (n + 1) // 2
        w = (w + 7) // 8 * 8
        assert self.off + w <= self.words, (name, self.off, w, self.words)
        ap = self.t[:, self.off:self.off + w]
        if dt != F32:
            ap = ap.bitcast(dt)
        ap = ap[:, 0:n]
        if len(shape) == 3:
            ap = ap.rearrange("p (a b) -> p a b", a=shape[1])
        elif len(shape) == 4:
            ap = ap.rearrange("p (a b c) -> p a b c", a=shape[1], b=shape[2])
        if shape[0] < 128:
            ap = ap[0:shape[0]]
        self.off += w
        self.hw = max(getattr(self, 'hw', 0), self.off)
        return Buf(self.k, name, ap)
    def mark(self): self.marks.append(self.off)
    def release(self): self.off = self.marks.pop()


def barrier(k):
    evs = [(k.sem[f], k.cnt[f]) for f in ['pe', 'act', 'dve', 'pool'] if k.cnt[f] > 0]
    evs += [(s, v[0]) for s, v in k.dmasems.items()]
    for e in ['pe', 'act', 'dve', 'pool', 'sp']:
        for ev in evs:
            k.need(e, ev)


class StopBuild(Exception): pass

def build(seq, L=4, dbg=None, stop=None, sample=True):
    try:
        return build_(seq, L, dbg, stop, sample)
    except StopBuild as e:
        k = e.args[0]; k.finish(); k.close(); return k

def build_(seq, L=4, dbg=None, stop=None, sample=True):
    T = seq + 16
    NB = (T + 127) // 128
    TP = NB * 128
    groups = []
    t0 = 0
    while t0 < TP:
        gt = min(G, TP - t0)
        groups.append((t0, gt)); t0 += gt
    k = KB(); nc = k.nc
    def chk(tag):
        if stop == tag: raise StopBuild(k)
    din = lambda n, s: k.dram(n, s, kind="ExternalInput")
    dout = lambda n, s: k.dram(n, s, kind="ExternalOutput")
    xin = din("xin", [TP, 1024]); ident_d = din("ident", [128, 128]); rotm_d = din("rotm", [128, 128])
    ropec_d = din("ropec", [128, TP]); ropes_d = din("ropes", [128, TP])
    amask_d = din("amask", [2, 128, 256]); slt_d = din("slt", [128, 128]); umat_d = din("umat", [128, 128]); negm_d = din("negm", [128, 128])
    iota_d = din("iota", [128, G])
    w_in_d = din("w_in_x", [L, 1024, NCW]); w_out_d = din("w_out", [L, 1024, 1024])
    w_g_d = din("w_gate", [L, 1024, 2816]); w_u_d = din("w_up", [L, 1024, 2816]); w_d_d = din("w_down", [L, 2816, 1024])
    glu_d = din("glu_w", [L, 256, 256]); bre_d = din("bblk_re", [L, 256, 1024]); bim_d = din("bblk_im", [L, 256, 1024])
    cre_d = din("cpad_re", [L, 128, 8, 128]); cim_d = din("cpad_im", [L, 128, 8, 128])
    colp_d = din("colpack", [L, 128, NC]); rowp_d = din("rowpack", [L, 1, 16]); lnf_d = din("lnf_cols", [128, 8])
    y_prompt = dout("y_prompt", [seq, 1024])
    k_p = dout("k_p", [L, 128, 128]); v_p = dout("v_p", [L, 128, 128])
    s5re_p = dout("s5re_p", [L, 8, 128]); s5im_p = dout("s5im_p", [L, 8, 128])
    conv_p = dout("conv_p", [L, 3, 512]); ssd_p = dout("ssd_p", [L, 4, 64, 64])
    dbg_o = dout("dbg", [128, 8, TP]) if dbg else None
    if sample:
        xs_in = din("xs_in", [128, 1024]); cache_k_d = din("cache_k", [L, 128, 128, 128]); cache_v_d = din("cache_v", [L, 128, 128, 128])
        st_s5re_d = din("st_s5re", [L, 128, 1024]); st_s5im_d = din("st_s5im", [L, 128, 1024])
        st_conv_d = din("st_conv", [L, 128, 1536]); st_ssd_d = din("st_ssd", [L, 128, 16384])
        ropec_s_d = din("ropec_s", [128, 128]); ropes_s_d = din("ropes_s", [128, 128])
        y_sample = dout("y_sample", [128, 1024]); k_s_d = dout("k_s", [L, 128, 128, 128]); v_s_d = dout("v_s", [L, 128, 128, 128])
        s5re_s_d = dout("s5re_s", [L, 128, 1024]); s5im_s_d = dout("s5im_s", [L, 128, 1024])
        conv_s_d = dout("conv_s", [L, 128, 1536]); ssd_s_d = dout("ssd_s", [L, 128, 16384])
        xres_s = k.dram("xres_s", [128, 8, 128])
    xres = k.dram("xres", [128, 8, TP]); kscr = k.dram("kscr", [128, TP]); vscr = k.dram("vscr", [128, TP])

    A = Arena(k, 53000)
    identf = A.buf("identf", [128, 128]); identb = A.buf("identb", [128, 128], BF16)
    rotm = A.buf("rotm", [128, 128]); onesb = A.buf("onesb", [128, 128], BF16); onesf = A.buf("onesf", [128, 128])
    slt = A.buf("slt", [128, 128]); umat = A.buf("umat", [128, 128]); negm = A.buf("negm", [128, 128])
    maskf = A.buf("maskf", [128, 2, 256]); maskb = A.buf("maskb", [128, 2, 256], BF16)
    iota = A.buf("iota", [128, G]); lnf = A.buf("lnf", [128, 8])
    colp = A.buf("colp", [128, NC]); rowp = A.buf("rowp", [128, 16])
    xg = A.buf("xg", [128, 8, G]); stg = [A.buf("stg0", [128, 2048]), A.buf("stg1", [128, 2048])]
    k.dma(identf[:], ident_d[:], identf, ident_d); k.dma(rotm[:], rotm_d[:], rotm, rotm_d)
    k.dma(slt[:], slt_d[:], slt, slt_d); k.dma(umat[:], umat_d[:], umat, umat_d); k.dma(negm[:], negm_d[:], negm, negm_d)
    k.dma(maskf[:], amask_d[:].rearrange("a p c -> p a c"), maskf, amask_d)
    k.dma(iota[:], iota_d[:], iota, iota_d); k.dma(lnf[:], lnf_d[:], lnf, lnf_d)
    k.op('dve', lambda e: e.tensor_copy(out=identb[:], in_=identf[:]), rd=[identf], wr=[identb])
    k.op('dve', lambda e: e.tensor_copy(out=maskb[:], in_=maskf[:]), rd=[maskf], wr=[maskb])
    k.op('pool', lambda e: e.memset(onesb[:], 1.0), wr=[onesb])
    k.op('pool', lambda e: e.memset(onesf[:], 1.0), wr=[onesf])
    pG = [k.ps("pG%d" % i, [128, 512]) for i in range(3)]
    pS = [k.ps("pS%d" % i, [128, 2, 256]) for i in range(2)]
    pT = k.ps("pT", [128, 8, 128], BF16)
    pO = k.ps("pO", [128, 512])
    pX = k.ps("pX", [128, 512])
    pgi = [0]
    pM = pG[2]
    def run_chains(gens):
        gens = list(gens)
        while gens:
            for nm, g_ in list(gens):
                cur_chain[0] = nm
                try:
                    next(g_)
                except StopIteration:
                    gens.remove((nm, g_))
        cur_chain[0] = None
    cur_chain = [None]; rotP = [0]; rotW = [0]
    def nextpg():
        if cur_chain[0] == 'B': return pG[1]
        if cur_chain[0] == 'C': return pG[0]
        if cur_chain[0] == 'P':
            rotP[0] = (rotP[0] + 1) % 3
            return [pG[0], pM, pO][rotP[0]]
        if cur_chain[0] == 'W':
            rotW[0] = (rotW[0] + 1) % 2
            return [pG[1], pX][rotW[0]]
        pgi[0] = (pgi[0] + 1) % 2
        return pG[pgi[0]]
    mm = lambda out_ap, lhsT, rhs, rd, wr, start=True, stop=True: k.op('pe', lambda e: e.matmul(out_ap, lhsT=lhsT, rhs=rhs, start=start, stop=stop), rd=rd, wr=wr, sig=stop)
    tr = lambda out_ap, in_ap, idn, rd, wr: k.op('pe', lambda e: e.transpose(out_ap, in_ap, idn[:]), rd=rd + [idn], wr=wr)

    def rmsnorm_T(src, nt, gt, width, dst16, tmp16, rstd, dst_ap=None):
        k.op('act', lambda e: e.activation(out=tmp16[:, 0:nt, 0:gt], in_=src[:, 0:nt, 0:gt], func=AF.Square), rd=[src], wr=[tmp16])
        pg = nextpg()
        for i in range(nt):
            mm(pg[:, 0:gt], onesb[:], tmp16[:, i, 0:gt], [onesb, tmp16], [pg], start=(i == 0), stop=(i == nt - 1))
        k.op('act', lambda e: e.activation(out=rstd[:, 0:gt], in_=pg[:, 0:gt], func=AF.Ln, scale=1.0 / width, bias=epsc[:, 0:1]), rd=[pg, epsc], wr=[rstd])
        k.op('act', lambda e: e.activation(out=rstd[:, 0:gt], in_=rstd[:, 0:gt], func=AF.Exp, scale=-0.5), rd=[rstd], wr=[rstd])
        oap = dst16[:, 0:nt, 0:gt] if dst_ap is None else dst_ap
        k.op('dve', lambda e: e.tensor_tensor(out=oap, in0=src[:, 0:nt, 0:gt], in1=rstd[:, 0:gt].unsqueeze(1).to_broadcast([128, nt, gt]), op=ALU.mult), rd=[src, rstd], wr=[dst16])

    epsc = A.buf("epsc", [128, 1])
    k.op('pool', lambda e: e.memset(epsc[:], EPS), wr=[epsc])
    hT16 = A.buf("hT16", [128, 8, G], BF16); rstd = A.buf("rstd", [128, G])
    xg2 = A.buf("xg2", [128, 8, G]); xgs = [xg, xg2]

    A.mark()
    xtok = A.buf("xtok", [128, 1024]); xtr = A.buf("xtr", [128, 8, 128])
    for b in range(NB):
        k.dma(xtok[:], xin[b * 128:(b + 1) * 128, :], xtok, xin)
        for half in range(2):
            pg = nextpg()
            for j in range(4):
                tr(pg[:, j * 128:(j + 1) * 128], xtok[:, (half * 4 + j) * 128:(half * 4 + j + 1) * 128], identf, [xtok], [pg])
            k.op('dve' if half == 0 else 'act',
                 (lambda e, pg=pg, half=half: e.tensor_copy(out=xtr[:, half * 4:half * 4 + 4, :], in_=pg[:, :].rearrange("p (a b) -> p a b", a=4))) if half == 0 else
                 (lambda e, pg=pg, half=half: e.activation(out=xtr[:, half * 4:half * 4 + 4, :], in_=pg[:, :].rearrange("p (a b) -> p a b", a=4), func=AF.Copy)),
                 rd=[pg], wr=[xtr])
        k.dma(xres[:, :, b * 128:(b + 1) * 128], xtr[:], xres, xtr)
    if sample:
        k.dma(xtok[:], xs_in[:], xtok, xs_in)
        for half in range(2):
            pg = nextpg()
            for j in range(4):
                tr(pg[:, j * 128:(j + 1) * 128], xtok[:, (half * 4 + j) * 128:(half * 4 + j + 1) * 128], identf, [xtok], [pg])
            k.op('dve', lambda e, pg=pg, half=half: e.tensor_copy(out=xtr[:, half * 4:half * 4 + 4, :], in_=pg[:, :].rearrange("p (a b) -> p a b", a=4)), rd=[pg], wr=[xtr])
        k.dma(xres_s[:], xtr[:], xres_s, xtr)
    A.release()
    chk('p0')

    def load_w(dst16, src_ap_fn, nchunks, width, gain_col_fn=None):
        npiece = (width + 2047) // 2048
        pw = width // npiece
        assert pw * npiece == width
        n = 0
        for c in range(nchunks):
            for pc in range(npiece):
                st = stg[n % 2]
                k.dma(st[:, 0:pw], src_ap_fn(c)[:, pc * pw:(pc + 1) * pw], st, w_any)
                use_act = (n % 2 == 1)
                n += 1
                if use_act:
                    if gain_col_fn is None:
                        k.op('act', lambda e, st=st, c=c, pc=pc: e.activation(out=dst16[:, c, pc * pw:(pc + 1) * pw], in_=st[:, 0:pw], func=AF.Copy), rd=[st], wr=[dst16])
                    else:
                        k.op('act', lambda e, st=st, c=c, pc=pc: e.activation(out=dst16[:, c, pc * pw:(pc + 1) * pw], in_=st[:, 0:pw], func=AF.Copy, scale=gain_col_fn(c)), rd=[st, colp], wr=[dst16])
                elif gain_col_fn is None:
                    k.op('dve', lambda e, st=st, c=c, pc=pc: e.tensor_copy(out=dst16[:, c, pc * pw:(pc + 1) * pw], in_=st[:, 0:pw]), rd=[st], wr=[dst16])
                else:
                    k.op('dve', lambda e, st=st, c=c, pc=pc: e.tensor_scalar(out=dst16[:, c, pc * pw:(pc + 1) * pw], in0=st[:, 0:pw], scalar1=gain_col_fn(c), scalar2=None, op0=ALU.mult), rd=[st, colp], wr=[dst16])
    w_any = Buf(k, "w_any", None); w_any.is_dram = True

    for l in range(L):
        barrier(k)
        A.mark()
        Win16 = A.buf("Win16", [128, 8, NCW], BF16); Wout16 = A.buf("Wout16", [128, 8, 1024], BF16)
        glu16 = A.buf("glu16", [128, 2, 256], BF16); Bre16 = A.buf("Bre16", [128, 2, 1024], BF16); Bim16 = A.buf("Bim16", [128, 2, 1024], BF16)
        CRe = A.buf("CRe", [128, 8, 128], BF16); nCRe = A.buf("nCRe", [128, 8, 128], BF16); nCIm = A.buf("nCIm", [128, 8, 128], BF16)
        k.dma(colp[:], colp_d[l], colp, colp_d)
        k.dma(rowp[:], rowp_d[l, 0:1, :].to_broadcast([128, 16]), rowp, rowp_d)
        load_w(Win16, lambda c: w_in_d[l, c * 128:(c + 1) * 128, :], 8, NCW, lambda c: colp[:, c:c + 1])
        load_w(Wout16, lambda c: w_out_d[l, c * 128:(c + 1) * 128, :], 8, 1024, lambda c: colp[:, 8 + c:9 + c])
        load_w(glu16, lambda c: glu_d[l, c * 128:(c + 1) * 128, :], 2, 256)
        load_w(Bre16, lambda c: bre_d[l, c * 128:(c + 1) * 128, :], 2, 1024)
        load_w(Bim16, lambda c: bim_d[l, c * 128:(c + 1) * 128, :], 2, 1024)
        chk('w')
        sp = lambda n, w=8: A.buf(n, [128, w])
        dtv = sp("dtv"); rho = sp("rho"); turns = sp("turns"); ti = A.buf("ti", [128, 8], I32); tf = sp("tf")
        sn = sp("sn"); cs_ = sp("cs_"); abr = sp("abr"); abi = sp("abi"); den = sp("den"); fre = sp("fre"); fim = sp("fim"); t8a = sp("t8a"); t8b = sp("t8b")
        aRe = colp[:, 24:32]; aIm = colp[:, 32:40]; lgdt = colp[:, 40:48]
        k.op('act', lambda e: e.activation(out=dtv[:], in_=lgdt, func=AF.Exp), rd=[colp], wr=[dtv])
        k.op('dve', lambda e: e.tensor_tensor(out=rho[:], in0=dtv[:], in1=aRe, op=ALU.mult), rd=[dtv, colp], wr=[rho])
        k.op('act', lambda e: e.activation(out=rho[:], in_=rho[:], func=AF.Exp), rd=[rho], wr=[rho])
        k.op('dve', lambda e: e.tensor_tensor(out=turns[:], in0=dtv[:], in1=aIm, op=ALU.mult), rd=[dtv, colp], wr=[turns])
        k.op('dve', lambda e: e.tensor_scalar(out=turns[:], in0=turns[:], scalar1=1.0 / (2 * np.pi), scalar2=None, op0=ALU.mult), rd=[turns], wr=[turns])
        k.op('dve', lambda e: e.tensor_copy(out=ti[:], in_=turns[:]), rd=[turns], wr=[ti])
        k.op('dve', lambda e: e.tensor_copy(out=tf[:], in_=ti[:]), rd=[ti], wr=[tf])
        thr = sp("thr")
        k.op('dve', lambda e: e.tensor_tensor(out=thr[:], in0=turns[:], in1=tf[:], op=ALU.subtract), rd=[turns, tf], wr=[thr])
        k.op('dve', lambda e: e.tensor_copy(out=turns[:], in_=thr[:]), rd=[thr], wr=[turns])
        sincos(k, turns, sn, cs_, ti, tf, None)
        k.op('dve', lambda e: e.tensor_tensor(out=abr[:], in0=rho[:], in1=cs_[:], op=ALU.mult), rd=[rho, cs_], wr=[abr])
        k.op('dve', lambda e: e.tensor_tensor(out=abi[:], in0=rho[:], in1=sn[:], op=ALU.mult), rd=[rho, sn], wr=[abi])
        k.op('dve', lambda e: e.tensor_tensor(out=den[:], in0=aRe, in1=aRe, op=ALU.mult), rd=[colp], wr=[den])
        k.op('dve', lambda e: e.tensor_tensor(out=t8a[:], in0=aIm, in1=aIm, op=ALU.mult), rd=[colp], wr=[t8a])
        k.op('dve', lambda e: e.tensor_tensor(out=den[:], in0=den[:], in1=t8a[:], op=ALU.add), rd=[den, t8a], wr=[den])
        k.op('dve', lambda e: e.reciprocal(out=den[:], in_=den[:]), rd=[den], wr=[den])
        xr = sp("xr")
        k.op('dve', lambda e: e.tensor_scalar(out=xr[:], in0=abr[:], scalar1=-1.0, scalar2=None, op0=ALU.add), rd=[abr], wr=[xr])
        k.op('dve', lambda e: e.tensor_tensor(out=t8a[:], in0=xr[:], in1=aRe, op=ALU.mult), rd=[xr, colp], wr=[t8a])
        k.op('dve', lambda e: e.tensor_tensor(out=t8b[:], in0=abi[:], in1=aIm, op=ALU.mult), rd=[abi, colp], wr=[t8b])
        k.op('dve', lambda e: e.tensor_tensor(out=t8a[:], in0=t8a[:], in1=t8b[:], op=ALU.add), rd=[t8a, t8b], wr=[t8a])
        k.op('dve', lambda e: e.tensor_tensor(out=fre[:], in0=t8a[:], in1=den[:], op=ALU.mult), rd=[t8a, den], wr=[fre])
        k.op('dve', lambda e: e.tensor_tensor(out=t8a[:], in0=abi[:], in1=aRe, op=ALU.mult), rd=[abi, colp], wr=[t8a])
        k.op('dve', lambda e: e.tensor_tensor(out=t8b[:], in0=xr[:], in1=aIm, op=ALU.mult), rd=[xr, colp], wr=[t8b])
        k.op('dve', lambda e: e.tensor_tensor(out=t8a[:], in0=t8a[:], in1=t8b[:], op=ALU.subtract), rd=[t8a, t8b], wr=[t8a])
        k.op('dve', lambda e: e.tensor_tensor(out=fim[:], in0=t8a[:], in1=den[:], op=ALU.mult), rd=[t8a, den], wr=[fim])
        finv_re = sp("finv_re"); finv_im = sp("finv_im"); nfinv_im = sp("nfinv_im"); nabi = sp("nabi"); nfim = sp("nfim")
        k.op('dve', lambda e: e.tensor_tensor(out=t8a[:], in0=fre[:], in1=fre[:], op=ALU.mult), rd=[fre], wr=[t8a])
        k.op('dve', lambda e: e.tensor_tensor(out=t8b[:], in0=fim[:], in1=fim[:], op=ALU.mult), rd=[fim], wr=[t8b])
        k.op('dve', lambda e: e.tensor_tensor(out=t8a[:], in0=t8a[:], in1=t8b[:], op=ALU.add), rd=[t8a, t8b], wr=[t8a])
        k.op('dve', lambda e: e.reciprocal(out=t8a[:], in_=t8a[:]), rd=[t8a], wr=[t8a])
        k.op('dve', lambda e: e.tensor_tensor(out=finv_re[:], in0=fre[:], in1=t8a[:], op=ALU.mult), rd=[fre, t8a], wr=[finv_re])
        k.op('dve', lambda e: e.tensor_tensor(out=nfinv_im[:], in0=fim[:], in1=t8a[:], op=ALU.mult), rd=[fim, t8a], wr=[nfinv_im])
        k.op('dve', lambda e: e.tensor_scalar(out=finv_im[:], in0=nfinv_im[:], scalar1=-1.0, scalar2=None, op0=ALU.mult), rd=[nfinv_im], wr=[finv_im])
        k.op('dve', lambda e: e.tensor_scalar(out=nabi[:], in0=abi[:], scalar1=-1.0, scalar2=None, op0=ALU.mult), rd=[abi], wr=[nabi])
        k.op('dve', lambda e: e.tensor_scalar(out=nfim[:], in0=fim[:], scalar1=-1.0, scalar2=None, op0=ALU.mult), rd=[fim], wr=[nfim])
        cosT = A.buf("cosT", [128, 8, G]); sinT = A.buf("sinT", [128, 8, G])
        cG = sp("cG"); sG = sp("sG"); tg = sp("tg")
        negsink = A.buf("negsink", [128, 8]); Atab = A.buf("Atab", [128, 4])
        A.mark()
        cst_re = A.buf("cst_re", [128, 8, 128]); cst_im = A.buf("cst_im", [128, 8, 128]); ctmp = A.buf("ctmp", [128, 8, 128]); ctmp2 = A.buf("ctmp2", [128, 8, 128])
        k.dma(cst_re[:], cre_d[l], cst_re, cre_d); k.dma(cst_im[:], cim_d[l], cst_im, cim_d)
        bc = lambda b_: b_[:].unsqueeze(2).to_broadcast([128, 8, 128])
        k.op('dve', lambda e: e.tensor_tensor(out=ctmp[:], in0=cst_re[:], in1=bc(fre), op=ALU.mult), rd=[cst_re, fre], wr=[ctmp])
        k.op('pool', lambda e: e.tensor_tensor(out=ctmp2[:], in0=cst_im[:], in1=bc(fim), op=ALU.mult), rd=[cst_im, fim], wr=[ctmp2])
        k.op('dve', lambda e: e.tensor_tensor(out=CRe[:], in0=ctmp[:], in1=ctmp2[:], op=ALU.subtract), rd=[ctmp, ctmp2], wr=[CRe])
        k.op('dve', lambda e: e.tensor_tensor(out=nCRe[:], in0=ctmp2[:], in1=ctmp[:], op=ALU.subtract), rd=[ctmp, ctmp2], wr=[nCRe])
        k.op('dve', lambda e: e.tensor_tensor(out=ctmp[:], in0=cst_re[:], in1=bc(fim), op=ALU.mult), rd=[cst_re, fim], wr=[ctmp])
        k.op('pool', lambda e: e.tensor_tensor(out=ctmp2[:], in0=cst_im[:], in1=bc(fre), op=ALU.mult), rd=[cst_im, fre], wr=[ctmp2])
        k.op('dve', lambda e: e.tensor_tensor(out=ctmp[:], in0=ctmp[:], in1=ctmp2[:], op=ALU.add), rd=[ctmp, ctmp2], wr=[ctmp])
        k.op('dve', lambda e: e.tensor_scalar(out=nCIm[:], in0=ctmp[:], scalar1=-1.0, scalar2=None, op0=ALU.mult), rd=[ctmp], wr=[nCIm])
        tb_t = A.buf("tb_t", [128, G]); tb_i = A.buf("tb_i", [128, G], I32); tb_f = A.buf("tb_f", [128, G])
        tbs = A.buf("tbs", [128, G]); tbc = A.buf("tbc", [128, G])
        for j in range(8):
            k.op('dve', lambda e, j=j: e.tensor_scalar(out=tb_t[:], in0=iota[:], scalar1=thr[:, j:j + 1], scalar2=None, op0=ALU.mult), rd=[iota, thr], wr=[tb_t])
            sincos(k, tb_t, tbs, tbc, tb_i, tb_f, None)
            k.op('pool', lambda e, j=j: e.tensor_copy(out=sinT[:, j, :], in_=tbs[:]), rd=[tbs], wr=[sinT])
            k.op('pool', lambda e, j=j: e.tensor_copy(out=cosT[:, j, :], in_=tbc[:]), rd=[tbc], wr=[cosT])
        k.op('dve', lambda e: e.tensor_scalar(out=tg[:], in0=thr[:], scalar1=float(G), scalar2=None, op0=ALU.mult), rd=[thr], wr=[tg])
        sincos(k, tg, sG, cG, ti, tf, None)
        k.op('dve', lambda e: e.tensor_scalar(out=negsink[:], in0=rowp[:, 0:8], scalar1=-1.0, scalar2=None, op0=ALU.mult), rd=[rowp], wr=[negsink])
        k.op('act', lambda e: e.activation(out=Atab[:], in_=rowp[:, 8:12], func=AF.Exp), rd=[rowp], wr=[Atab])
        k.op('dve', lambda e: e.tensor_scalar(out=Atab[:], in0=Atab[:], scalar1=-1.0, scalar2=None, op0=ALU.mult), rd=[Atab], wr=[Atab])

        A.release(); barrier(k)
        chk('prep')
        qk32 = A.buf("qk32", [128, 6, G]); qT16 = A.buf("qT16", [128, 4, G], BF16)
        kdT = [A.buf("kdT%d" % i, [128, 128 + G], BF16) for i in range(2)]; rk32 = A.buf("rk32", [128, 2, G])
        Vtok = A.buf("Vtok", [128, 1 + G // 128, 128], BF16); vT32 = A.buf("vT32", [128, G])
        u32 = A.buf("u32", [128, 2, G]); u16 = A.buf("u16", [128, 2, G], BF16); sz = A.buf("sz", [128, 2, G])
        xpre = A.buf("xpre", [128, 4, 8 + G]); xc32 = A.buf("xc32", [128, 4, G]); xc16 = A.buf("xc16", [128, 4, G], BF16)
        dtT = A.buf("dtT", [4, G]); ropc = A.buf("ropc", [128, G]); rops = A.buf("rops", [128, G]); rt1 = A.buf("rt1", [128, G])
        mixT = A.buf("mixT", [128, 8, G], BF16)
        bur = A.buf("bur", [128, G]); bui = A.buf("bui", [128, G]); wr_ = A.buf("wr_", [128, G]); wi_ = A.buf("wi_", [128, G])
        s5a = A.buf("s5a", [128, G]); s5b = A.buf("s5b", [128, G]); gr = A.buf("gr", [128, G]); gi_ = A.buf("gi_", [128, G])
        prodf = A.buf("prodf", [128, 2 * 4 * G])
        prod = prodf.t.bitcast(BF16).rearrange("p (a b c) -> p a b c", a=4, b=4)
        sq16 = Buf(k, "sq16m", prodf.t.bitcast(BF16)[:, 0:8 * G].rearrange("p (a b) -> p a b", a=8)); sq16.lw = prodf.lw; sq16.rd = prodf.rd
        gend_r = sp("gend_r"); gend_i = sp("gend_i"); gin_r = sp("gin_r"); gin_i = sp("gin_i")
        ys = A.buf("ys", [128, 2, G]); y5 = A.buf("y5", [128, 2, G]); y5h = A.buf("y5h", [128, 2, G], BF16); sg = A.buf("sg", [128, 2, G])
        P16 = [A.buf("P16_%d" % i, [128, 2, 256], BF16) for i in range(2)]; PTs = [A.buf("PTs%d" % i, [128, 4, 128], BF16) for i in range(2)]
        mx = A.buf("mx", [128, 8]); ngm = A.buf("ngm", [128, 8]); rs = A.buf("rs", [128, 8]); es = A.buf("es", [128, 8]); rden = A.buf("rden", [128, 8])
        On = A.buf("On", [128, 8, 64]); junk = A.buf("junk", [128, 512]); ss1 = A.buf("ss1", [128, 1]); o16 = A.buf("o16", [128, 512], BF16)
        dtk = A.buf("dtk", [128, 4]); atok = A.buf("atok", [128, 4]); xd16 = A.buf("xd16", [128, 4, 64], BF16); Btk = A.buf("Btk", [128, 128])
        LA = A.buf("LA", [128, 4, 128]); LT = A.buf("LT", [128, 4, 128]); ecs = A.buf("ecs", [128, 8]); SL16 = A.buf("SL16", [128, 4, 128], BF16)
        Bd16 = A.buf("Bd16", [128, 4, 128], BF16); Yd = A.buf("Yd", [128, 256]); Yt = A.buf("Yt", [128, 4, 64]); ycg = A.buf("ycg", [128, 2, 128])
        hT32 = A.buf("hT32", [128, 4, 64]); hTh = A.buf("hTh", [128, 4, 64], BF16); htmp = A.buf("htmp", [128, 4, 64])
        sq2 = A.buf("sq2", [128, 2, 128], BF16); rstd2 = A.buf("rstd2", [128, 128])
        k.op('pool', lambda e: e.memset(hT32[:], 0.0), wr=[hT32]); k.op('pool', lambda e: e.memset(hTh[:], 0.0), wr=[hTh])
        k.op('pool', lambda e: e.memset(gin_r[:], 0.0), wr=[gin_r]); k.op('pool', lambda e: e.memset(gin_i[:], 0.0), wr=[gin_i])
        k.op('pool', lambda e: e.memset(xpre[:], 0.0), wr=[xpre])
        for i in range(2): k.op('pool', lambda e, i=i: e.memset(kdT[i][:], 0.0), wr=[kdT[i]])
        k.op('pool', lambda e: e.memset(Vtok[:], 0.0), wr=[Vtok])

        def do_norm(gt, xb):
            rmsnorm_T(xb, 8, gt, 1024.0, hT16, sq16, rstd)
        def do_proj(gt, nvalid):
            def proj(ot, rows=128):
                pg = nextpg()
                for kt in range(8):
                    mm(pg[0:rows, 0:gt], Win16[:, kt, ot * 128:ot * 128 + rows], hT16[:, kt, 0:gt], [Win16, hT16], [pg], start=(kt == 0), stop=(kt == 7))
                return pg
            for i in range(6):
                pg = proj(i)
                k.op('act' if i % 2 else 'dve', (lambda e, pg=pg, i=i: e.activation(out=qk32[:, i, 0:gt], in_=pg[:, 0:gt], func=AF.Copy)) if i % 2 else
                     (lambda e, pg=pg, i=i: e.tensor_copy(out=qk32[:, i, 0:gt], in_=pg[:, 0:gt])), rd=[pg], wr=[qk32])
            chk('qk')
            pg = proj(6)
            k.op('act', lambda e, pg=pg: e.activation(out=vT32[:, 0:gt], in_=pg[:, 0:gt], func=AF.Copy), rd=[pg], wr=[vT32])
            yield
            chk('v')
            for m in range(2):
                pg = proj(7 + m)
                k.op('dve', lambda e, pg=pg, m=m: e.tensor_copy(out=u32[:, m, 0:gt], in_=pg[:, 0:gt]), rd=[pg], wr=[u32])
                yield
                k.op('act', lambda e, m=m: e.activation(out=u16[:, m, 0:gt], in_=u32[:, m, 0:gt], func=AF.Copy), rd=[u32], wr=[u16])
                yield
            chk('u')
            for m in range(2):
                pg = proj(9 + m)
                k.op('act', lambda e, pg=pg, m=m: e.activation(out=sz[:, m, 0:gt], in_=pg[:, 0:gt], func=AF.Silu), rd=[pg], wr=[sz])
                yield
            chk('z')
            for j in range(4):
                pg = proj(11 + j)
                k.op('dve', lambda e, pg=pg, j=j: e.tensor_copy(out=xpre[:, j, 8:8 + gt], in_=pg[:, 0:gt]), rd=[pg], wr=[xpre])
                yield
            chk('x')
            pg = proj(15, rows=4)
            k.op('act', lambda e, pg=pg: e.activation(out=dtT[:, 0:gt], in_=pg[0:4, 0:gt], func=AF.Exp, bias=colp[0:4, 74:75]), rd=[pg, colp], wr=[dtT])
            yield
            k.op('act', lambda e: e.activation(out=dtT[:, 0:gt], in_=dtT[:, 0:gt], func=AF.Ln, bias=onesf[0:4, 0:1]), rd=[dtT, onesf], wr=[dtT])
            yield
            if nvalid < gt:
                k.op('dve', lambda e: e.memset(dtT[:, nvalid:gt], 0.0), wr=[dtT])
                yield
        def do_rope(gt, sample=False):
            for i in range(6):
                pg = nextpg()
                mm(pg[:, 0:gt], rotm[:], qk32[:, i, 0:gt], [rotm, qk32], [pg])
                k.op('pool', lambda e, i=i: e.tensor_tensor(out=rt1[:, 0:gt], in0=qk32[:, i, 0:gt], in1=ropc[:, 0:gt], op=ALU.mult), rd=[qk32, ropc], wr=[rt1])
                yield
                k.op('dve', lambda e, pg=pg: e.tensor_tensor(out=junk[:, 0:gt], in0=pg[:, 0:gt], in1=rops[:, 0:gt], op=ALU.mult), rd=[pg, rops], wr=[junk])
                yield
                if i < 4 and sample:
                    k.op('pool', lambda e, i=i: e.tensor_tensor(out=qk32[:, i, 0:gt], in0=rt1[:, 0:gt], in1=junk[:, 0:gt], op=ALU.add), rd=[rt1, junk], wr=[qk32])
                    yield
                elif i < 4:
                    k.op('pool', lambda e, i=i: e.tensor_tensor(out=qT16[:, i, 0:gt], in0=rt1[:, 0:gt], in1=junk[:, 0:gt], op=ALU.add), rd=[rt1, junk], wr=[qT16])
                    yield
                else:
                    k.op('pool', lambda e, i=i: e.tensor_tensor(out=rk32[:, i - 4, 0:gt], in0=rt1[:, 0:gt], in1=junk[:, 0:gt], op=ALU.add), rd=[rt1, junk], wr=[rk32])
                    yield
                    k.op('act', lambda e, i=i: e.activation(out=kdT[i - 4][:, 128:128 + gt], in_=rk32[:, i - 4, 0:gt], func=AF.Copy), rd=[rk32], wr=[kdT[i - 4]])
                    yield
        def attn_den():
            k.op('dve', lambda e: e.tensor_tensor(out=es[:], in0=ngm[:], in1=negsink[:], op=ALU.subtract), rd=[ngm, negsink], wr=[es])
            k.op('act', lambda e: e.activation(out=es[:], in_=es[:], func=AF.Exp), rd=[es], wr=[es])
            k.op('dve', lambda e: e.tensor_tensor(out=rden[:], in0=rs[:], in1=es[:], op=ALU.add), rd=[rs, es], wr=[rden])
            k.op('dve', lambda e: e.reciprocal(out=rden[:], in_=rden[:]), rd=[rden], wr=[rden])
        def attn_tail(c0):
            k.op('pool', lambda e: e.memset(ss1[:], 0.0), wr=[ss1])
            k.op('act', lambda e: e.activation(out=junk[:, 0:512], in_=On[:].rearrange("p h d -> p (h d)"), func=AF.Square, accum_out=ss1[:, 0:1]), rd=[On], wr=[junk, ss1])
            k.op('act', lambda e: e.activation(out=ss1[:], in_=ss1[:], func=AF.Ln, scale=1.0 / 512, bias=epsc[:, 0:1]), rd=[ss1, epsc], wr=[ss1])
            k.op('act', lambda e: e.activation(out=ss1[:], in_=ss1[:], func=AF.Exp, scale=-0.5), rd=[ss1], wr=[ss1])
            k.op('dve', lambda e: e.tensor_scalar(out=o16[:], in0=On[:].rearrange("p h d -> p (h d)"), scalar1=ss1[:, 0:1], scalar2=None, op0=ALU.mult), rd=[On, ss1], wr=[o16])
            for c in range(4):
                tr(pT[:, c, :], o16[:, c * 128:(c + 1) * 128], identb, [o16], [pT])
            k.op('act', lambda e, c0=c0: e.activation(out=mixT[:, 0:4, c0:c0 + 128], in_=pT[:, 0:4, :], func=AF.Copy), rd=[pT], wr=[mixT])
        def s5_epi(gt):
            k.op('act', lambda e: e.activation(out=y5[:, :, 0:gt], in_=ys[:, :, 0:gt], func=AF.Gelu_apprx_tanh), rd=[ys], wr=[y5])
            k.op('pool', lambda e: e.tensor_copy(out=y5h[:, :, 0:gt], in_=y5[:, :, 0:gt]), rd=[y5], wr=[y5h])
            for m2 in range(2):
                pg = nextpg()
                for m in range(2):
                    mm(pg[:, 0:gt], glu16[:, m, m2 * 128:(m2 + 1) * 128], y5h[:, m, 0:gt], [glu16, y5h], [pg], start=(m == 0), stop=(m == 1))
                k.op('act', lambda e, pg=pg, m2=m2: e.activation(out=sg[:, m2, 0:gt], in_=pg[:, 0:gt], func=AF.Sigmoid, bias=colp[:, 50 + m2:51 + m2]), rd=[pg, colp], wr=[sg])
            k.op('pool', lambda e: e.tensor_tensor(out=y5[:, :, 0:gt], in0=y5[:, :, 0:gt], in1=sg[:, :, 0:gt], op=ALU.mult), rd=[y5, sg], wr=[y5])
            rmsnorm_T(y5, 2, gt, 256.0, mixT, sq16, rstd, dst_ap=mixT[:, 4:6, 0:gt])
        def ssd_epi(c0):
            pg2 = nextpg()
            for m in range(2):
                tr(pg2[:, m * 128:(m + 1) * 128], Yt[:, 2 * m:2 * m + 2, :].rearrange("p h d -> p (h d)"), identf, [Yt], [pg2])
            for m in range(2):
                k.op('dve', lambda e, m=m, pg2=pg2, c0=c0: e.scalar_tensor_tensor(out=ycg[:, m, :], in0=xc32[:, m, c0:c0 + 128], scalar=colp[:, 72 + m:73 + m], in1=pg2[:, m * 128:(m + 1) * 128], op0=ALU.mult, op1=ALU.add), rd=[xc32, colp, pg2], wr=[ycg])
            k.op('pool', lambda e, c0=c0: e.tensor_tensor(out=ycg[:], in0=ycg[:], in1=sz[:, :, c0:c0 + 128], op=ALU.mult), rd=[ycg, sz], wr=[ycg])
            k.op('act', lambda e: e.activation(out=sq2[:], in_=ycg[:], func=AF.Square), rd=[ycg], wr=[sq2])
            pg3 = nextpg()
            for m in range(2):
                mm(pg3[:, 0:128], onesb[:], sq2[:, m, :], [onesb, sq2], [pg3], start=(m == 0), stop=(m == 1))
            k.op('act', lambda e, pg3=pg3: e.activation(out=rstd2[:], in_=pg3[:, 0:128], func=AF.Ln, scale=1.0 / 256, bias=epsc[:, 0:1]), rd=[pg3, epsc], wr=[rstd2])
            k.op('act', lambda e: e.activation(out=rstd2[:], in_=rstd2[:], func=AF.Exp, scale=-0.5), rd=[rstd2], wr=[rstd2])
            k.op('dve', lambda e, c0=c0: e.tensor_tensor(out=mixT[:, 6:8, c0:c0 + 128], in0=ycg[:], in1=rstd2[:].unsqueeze(1).to_broadcast([128, 2, 128]), op=ALU.mult), rd=[ycg, rstd2], wr=[mixT])
        def wout(gt, xb=None):
            xb = xg if xb is None else xb
            for dt_ in range(8):
                pg = nextpg()
                for kt in range(8):
                    mm(pg[:, 0:gt], Wout16[:, kt, dt_ * 128:(dt_ + 1) * 128], mixT[:, kt, 0:gt], [Wout16, mixT], [pg], start=(kt == 0), stop=(kt == 7))
                k.op('dve', lambda e, pg=pg, dt_=dt_: e.tensor_tensor(out=xb[:, dt_, 0:gt], in0=xb[:, dt_, 0:gt], in1=pg[:, 0:gt], op=ALU.add), rd=[xb, pg], wr=[xb])
                yield

        def proj_chain(gidx_):
            t0, gt = groups[gidx_]; nblk = gt // 128; nvalid = min(gt, T - t0)
            if gidx_ > 0:
                do_norm(gt, xgs[gidx_ % 2])
                yield
            k.dma(ropc[:, 0:gt], ropec_d[:, t0:t0 + gt], ropc, ropec_d); k.dma(rops[:, 0:gt], ropes_d[:, t0:t0 + gt], rops, ropes_d)
            for _ in do_proj(gt, nvalid): yield
            for _ in do_rope(gt): yield
            k.dma(kscr[0:64, t0:t0 + gt], rk32[0:64, 0, 0:gt], kscr, rk32); k.dma(kscr[64:128, t0:t0 + gt], rk32[0:64, 1, 0:gt], kscr, rk32)
            k.dma(vscr[:, t0:t0 + gt], vT32[:, 0:gt], vscr, vT32)
            for bi in range(nblk):
                pg = nextpg()
                tr(pg[:, 0:128], vT32[:, bi * 128:(bi + 1) * 128], identf, [vT32], [pg])
                k.op('dve', lambda e, pg=pg, bi=bi: e.tensor_copy(out=Vtok[:, 1 + bi, :], in_=pg[:, 0:128]), rd=[pg], wr=[Vtok])
            yield
        def wout_chain(gidx_):
            t0, gt = groups[gidx_]; xb = xgs[gidx_ % 2]
            for _ in wout(gt, xb):
                yield
            k.dma(xres[:, :, t0:t0 + gt], xb[:, :, 0:gt], xres, xb)
            yield
        def load_g(gidx_):
            t0_, gt_ = groups[gidx_]; xb_ = xgs[gidx_ % 2]
            k.dma(xb_[:, :, 0:gt_], xres[:, :, t0_:t0_ + gt_], xb_, xres)
        load_g(0)
        do_norm(groups[0][1], xgs[0])
        if len(groups) > 1: load_g(1)
        run_chains([('P', proj_chain(0))])
        for gidx, (t0, gt) in enumerate(groups):
            nblk = gt // 128
            nvalid = min(gt, T - t0)
            xb = xgs[gidx % 2]
            chk('rope')
            def chain_A():
                for bi in range(nblk):
                    c0 = bi * 128
                    first = 1 if (gidx == 0 and bi == 0) else 0
                    k.op('pool', lambda e: e.memset(rs[:], 0.0), wr=[rs])
                    for hp in range(4):
                        gi = hp // 2; ps = pS[0]; p16 = P16[hp % 2]; pts = PTs[hp % 2]
                        for hh in range(2):
                            r0 = hh * 64
                            mm(ps[:, hh, :], qT16[r0:r0 + 64, hp, c0:c0 + 128], kdT[gi][r0:r0 + 64, c0:c0 + 256], [qT16, kdT[gi]], [ps], start=True, stop=False)
                            mm(ps[:, hh, :], identb[:], maskb[:, first, :], [identb, maskb], [ps], start=False, stop=True)
                        k.op('dve', lambda e, ps=ps, hp=hp: e.tensor_reduce(out=mx[:, 2 * hp:2 * hp + 2], in_=ps[:], axis=AX.X, op=ALU.max), rd=[ps], wr=[mx])
                        k.op('dve', lambda e, hp=hp: e.scalar_tensor_tensor(out=ngm[:, 2 * hp:2 * hp + 2], in0=mx[:, 2 * hp:2 * hp + 2], scalar=-0.125, in1=negsink[:, 2 * hp:2 * hp + 2], op0=ALU.mult, op1=ALU.min), rd=[mx, negsink], wr=[ngm])
                        for hh in range(2):
                            h = 2 * hp + hh
                            k.op('act', lambda e, ps=ps, p16=p16, hh=hh, h=h: e.activation(out=p16[:, hh, :], in_=ps[:, hh, :], func=AF.Exp, bias=ngm[:, h:h + 1], scale=0.125, accum_out=rs[:, h:h + 1]), rd=[ps, ngm], wr=[p16, rs])
                        yield
                        for hh in range(2):
                            for half in range(2):
                                tr(pT[:, (hp % 2) * 4 + hh * 2 + half, :], p16[:, hh, half * 128:(half + 1) * 128], identb, [p16], [pT])
                        k.op('dve', lambda e, hp=hp, pts=pts: e.tensor_copy(out=pts[:], in_=pT[:, (hp % 2) * 4:(hp % 2) * 4 + 4, :]), rd=[pT], wr=[pts])
                        yield
                        for hh in range(2):
                            h = 2 * hp + hh
                            for half in range(2):
                                mm(pO[:, h * 64:(h + 1) * 64], pts[:, hh * 2 + half, :], Vtok[:, bi + half, gi * 64:(gi + 1) * 64], [pts, Vtok], [pO], start=(half == 0), stop=(half == 1))
                    yield
                    attn_den()
                    k.op('dve', lambda e: e.tensor_tensor(out=On[:], in0=pO[:, :].rearrange("p (h d) -> p h d", h=8), in1=rden[:].unsqueeze(2).to_broadcast([128, 8, 64]), op=ALU.mult), rd=[pO, rden], wr=[On])
                    attn_tail(c0)
                for i in range(2):
                    k.op('pool', lambda e, i=i: e.tensor_copy(out=kdT[i][:, 0:128], in_=kdT[i][:, gt:gt + 128]), rd=[kdT[i]], wr=[kdT[i]])
                k.op('pool', lambda e: e.tensor_copy(out=Vtok[:, 0, :], in_=Vtok[:, nblk, :]), rd=[Vtok], wr=[Vtok])
                yield
            def chain_B():
                for j in range(8):
                    jq = j % 4
                    pg1 = nextpg()
                    mm(pg1[:, 0:gt], Bre16[:, j // 4, j * 128:(j + 1) * 128], u16[:, j // 4, 0:gt], [Bre16, u16], [pg1])
                    mm(pg1[:, 256:256 + gt], Bim16[:, j // 4, j * 128:(j + 1) * 128], u16[:, j // 4, 0:gt], [Bim16, u16], [pg1])
                    k.op('act', lambda e, pg1=pg1: e.activation(out=bur[:, 0:gt], in_=pg1[:, 0:gt], func=AF.Copy), rd=[pg1], wr=[bur])
                    k.op('act', lambda e, pg1=pg1: e.activation(out=bui[:, 0:gt], in_=pg1[:, 256:256 + gt], func=AF.Copy), rd=[pg1], wr=[bui])
                    cj = cosT[:, j, 0:gt]; sj = sinT[:, j, 0:gt]
                    k.op('dve', lambda e, cj=cj: e.tensor_tensor(out=s5a[:, 0:gt], in0=bur[:, 0:gt], in1=cj, op=ALU.mult), rd=[bur, cosT], wr=[s5a])
                    k.op('pool', lambda e, sj=sj: e.tensor_tensor(out=s5b[:, 0:gt], in0=bui[:, 0:gt], in1=sj, op=ALU.mult), rd=[bui, sinT], wr=[s5b])
                    k.op('dve', lambda e: e.tensor_tensor(out=wr_[:, 0:gt], in0=s5a[:, 0:gt], in1=s5b[:, 0:gt], op=ALU.add), rd=[s5a, s5b], wr=[wr_])
                    yield
                    k.op('dve', lambda e, cj=cj: e.tensor_tensor(out=s5a[:, 0:gt], in0=bui[:, 0:gt], in1=cj, op=ALU.mult), rd=[bui, cosT], wr=[s5a])
                    k.op('pool', lambda e, sj=sj: e.tensor_tensor(out=s5b[:, 0:gt], in0=bur[:, 0:gt], in1=sj, op=ALU.mult), rd=[bur, sinT], wr=[s5b])
                    k.op('dve', lambda e: e.tensor_tensor(out=wi_[:, 0:gt], in0=s5a[:, 0:gt], in1=s5b[:, 0:gt], op=ALU.subtract), rd=[s5a, s5b], wr=[wi_])
                    rb = rho[:, j:j + 1].to_broadcast([128, gt])
                    k.op('dve', lambda e, rb=rb, j=j: e.tensor_tensor_scan(out=gr[:, 0:gt], data0=rb, data1=wr_[:, 0:gt], initial=gin_r[:, j:j + 1], op0=ALU.mult, op1=ALU.add), rd=[rho, wr_, gin_r], wr=[gr])
                    k.op('dve', lambda e, rb=rb, j=j: e.tensor_tensor_scan(out=gi_[:, 0:gt], data0=rb, data1=wi_[:, 0:gt], initial=gin_i[:, j:j + 1], op0=ALU.mult, op1=ALU.add), rd=[rho, wi_, gin_i], wr=[gi_])
                    yield
                    ec = nvalid - 1 if nvalid < gt else gt - 1
                    k.op('pool', lambda e, j=j, ec=ec: e.tensor_copy(out=gend_r[:, j:j + 1], in_=gr[:, ec:ec + 1]), rd=[gr], wr=[gend_r])
                    k.op('pool', lambda e, j=j, ec=ec: e.tensor_copy(out=gend_i[:, j:j + 1], in_=gi_[:, ec:ec + 1]), rd=[gi_], wr=[gend_i])
                    k.op('dve', lambda e, j=j, jq=jq, cj=cj: e.tensor_tensor(out=prod[:, jq, 0, 0:gt], in0=gr[:, 0:gt], in1=cj, op=ALU.mult), rd=[gr, cosT], wr=[prodf])
                    k.op('pool', lambda e, j=j, jq=jq, sj=sj: e.tensor_tensor(out=prod[:, jq, 1, 0:gt], in0=gi_[:, 0:gt], in1=sj, op=ALU.mult), rd=[gi_, sinT], wr=[prodf])
                    k.op('dve', lambda e, j=j, jq=jq, sj=sj: e.tensor_tensor(out=prod[:, jq, 2, 0:gt], in0=gr[:, 0:gt], in1=sj, op=ALU.mult), rd=[gr, sinT], wr=[prodf])
                    k.op('pool', lambda e, j=j, jq=jq, cj=cj: e.tensor_tensor(out=prod[:, jq, 3, 0:gt], in0=gi_[:, 0:gt], in1=cj, op=ALU.mult), rd=[gi_, cosT], wr=[prodf])
                    yield
                    if jq == 3:
                        m = j // 4
                        pgc = nextpg()
                        n = 0
                        for jj in range(4):
                            for q, Cm in enumerate([CRe, nCRe, nCIm, nCIm]):
                                mm(pgc[:, 0:gt], Cm[:, 4 * m + jj, :], prod[:, jj, q, 0:gt], [Cm, prodf], [pgc], start=(n == 0), stop=(n == 15)); n += 1
                        k.op('dve', lambda e, pgc=pgc, m=m: e.scalar_tensor_tensor(out=ys[:, m, 0:gt], in0=u32[:, m, 0:gt], scalar=colp[:, 48 + m:49 + m], in1=pgc[:, 0:gt], op0=ALU.mult, op1=ALU.add), rd=[u32, colp, pgc], wr=[ys])
                k.op('dve', lambda e: e.tensor_tensor(out=t8a[:], in0=cG[:], in1=gend_r[:], op=ALU.mult), rd=[cG, gend_r], wr=[t8a])
                k.op('dve', lambda e: e.tensor_tensor(out=t8b[:], in0=sG[:], in1=gend_i[:], op=ALU.mult), rd=[sG, gend_i], wr=[t8b])
                k.op('dve', lambda e: e.tensor_tensor(out=gin_r[:], in0=t8a[:], in1=t8b[:], op=ALU.subtract), rd=[t8a, t8b], wr=[gin_r])
                k.op('dve', lambda e: e.tensor_tensor(out=t8a[:], in0=sG[:], in1=gend_r[:], op=ALU.mult), rd=[sG, gend_r], wr=[t8a])
                k.op('dve', lambda e: e.tensor_tensor(out=t8b[:], in0=cG[:], in1=gend_i[:], op=ALU.mult), rd=[cG, gend_i], wr=[t8b])
                k.op('dve', lambda e: e.tensor_tensor(out=gin_i[:], in0=t8a[:], in1=t8b[:], op=ALU.add), rd=[t8a, t8b], wr=[gin_i])
                s5_epi(gt)
                yield
            def chain_C():
                for j in range(4):
                    for tap in range(4):
                        wcol = colp[:, 52 + j * 4 + tap:53 + j * 4 + tap]
                        src = xpre[:, j, 5 + tap:5 + tap + gt]
                        if tap == 0:
                            k.op('pool', lambda e, src=src, wcol=wcol, j=j: e.tensor_scalar(out=xc32[:, j, 0:gt], in0=src, scalar1=wcol, scalar2=colp[:, 68 + j:69 + j], op0=ALU.mult, op1=ALU.add), rd=[xpre, colp], wr=[xc32])
                        else:
                            k.op('dve', lambda e, src=src, wcol=wcol, j=j: e.scalar_tensor_tensor(out=xc32[:, j, 0:gt], in0=src, scalar=wcol, in1=xc32[:, j, 0:gt], op0=ALU.mult, op1=ALU.add), rd=[xpre, colp, xc32], wr=[xc32])
                if nvalid == gt:
                    k.op('pool', lambda e: e.tensor_copy(out=xpre[:, :, 5:8], in_=xpre[:, :, 5 + gt:8 + gt]), rd=[xpre], wr=[xpre])
                k.op('act', lambda e: e.activation(out=xc32[:, :, 0:gt], in_=xc32[:, :, 0:gt], func=AF.Silu), rd=[xc32], wr=[xc32])
                k.op('act', lambda e: e.activation(out=xc16[:, :, 0:gt], in_=xc32[:, :, 0:gt], func=AF.Copy), rd=[xc32], wr=[xc16])
                for bi in range(nblk):
                    c0 = bi * 128
                    pg = nextpg()
                    for m in range(2):
                        tr(pg[:, m * 128:(m + 1) * 128], xc32[:, m, c0:c0 + 128], identf, [xc32], [pg])
                    tr(pg[:, 256:384], xc32[:, 2, c0:c0 + 128], identf, [xc32], [pg])
                    k.op('pe', lambda e, pg=pg, c0=c0: e.transpose(pg[:, 384:388], dtT[0:4, c0:c0 + 128], identf[0:4, 0:4]), rd=[dtT, identf], wr=[pg])
                    k.op('dve', lambda e, pg=pg: e.tensor_copy(out=dtk[:], in_=pg[:, 384:388]), rd=[pg], wr=[dtk])
                    k.op('dve', lambda e: e.tensor_tensor(out=atok[:], in0=dtk[:], in1=Atab[:], op=ALU.mult), rd=[dtk, Atab], wr=[atok])
                    k.op('dve', lambda e, pg=pg: e.tensor_tensor(out=xd16[:], in0=pg[:, 0:256].rearrange("p (h d) -> p h d", h=4), in1=dtk[:].unsqueeze(2).to_broadcast([128, 4, 64]), op=ALU.mult), rd=[pg, dtk], wr=[xd16])
                    k.op('dve', lambda e, pg=pg: e.tensor_copy(out=Btk[:], in_=pg[:, 256:384]), rd=[pg], wr=[Btk])
                    k.op('pool', lambda e: e.tensor_tensor(out=LA[:], in0=slt[:].unsqueeze(1).to_broadcast([128, 4, 128]), in1=atok[:].unsqueeze(2).to_broadcast([128, 4, 128]), op=ALU.mult), rd=[slt, atok], wr=[LA])
                    yield
                    pm = pM
                    pmv = pm[:, :].rearrange("p (h i) -> p h i", h=4)
                    for h in range(4):
                        mm(pmv[:, h, :], LA[:, h, :], umat[:], [LA, umat], [pm], start=True, stop=False)
                        mm(pmv[:, h, :], identf[:], negm[:], [identf, negm], [pm], start=False, stop=True)
                    k.op('act', lambda e, pmv=pmv: e.activation(out=LT[:], in_=pmv, func=AF.Exp), rd=[pm], wr=[LT])
                    pgs = nextpg()
                    mm(pgs[:, 0:4], umat[:], atok[:], [umat, atok], [pgs], start=True, stop=True)
                    mm(pgs[:, 4:8], onesf[:], atok[:], [onesf, atok], [pgs], start=True, stop=True)
                    k.op('act', lambda e, pgs=pgs: e.activation(out=ecs[:], in_=pgs[:, 0:8], func=AF.Exp), rd=[pgs], wr=[ecs])
                    yield
                    psc = pX[:, 0:256].rearrange("p (a b) -> p a b", a=2)
                    for gi in range(2):
                        mm(psc[:, gi, :], xc16[gi * 64:(gi + 1) * 64, 2, c0:c0 + 128], xc16[gi * 64:(gi + 1) * 64, 3, c0:c0 + 128], [xc16], [pX])
                    k.op('dve', lambda e, psc=psc: e.tensor_tensor(out=SL16[:].rearrange("p (g r) i -> p g r i", g=2), in0=psc.unsqueeze(2).to_broadcast([128, 2, 2, 128]), in1=LT[:].rearrange("p (g r) i -> p g r i", g=2), op=ALU.mult), rd=[pX, LT], wr=[SL16])
                    yield
                    py = pS[1][:, :, :].rearrange("p a b -> p (a b)")
                    for h in range(4):
                        mm(py[:, h * 64:(h + 1) * 64], SL16[:, h, :], xd16[:, h, :], [SL16, xd16], [pS[1]])
                    for h in range(4):
                        g = h // 2
                        mm(py[:, 256 + h * 64:256 + (h + 1) * 64], xc16[g * 64:(g + 1) * 64, 3, c0:c0 + 128], hTh[g * 64:(g + 1) * 64, h, :], [xc16, hTh], [pS[1]])
                    k.op('dve', lambda e: e.tensor_copy(out=Yd[:], in_=py[:, 0:256]), rd=[pS[1]], wr=[Yd])
                    k.op('dve', lambda e: e.tensor_tensor(out=Yt[:], in0=py[:, 256:512].rearrange("p (h d) -> p h d", h=4), in1=ecs[:, 0:4].unsqueeze(2).to_broadcast([128, 4, 64]), op=ALU.mult), rd=[pS[1], ecs], wr=[Yt])
                    k.op('pool', lambda e: e.tensor_tensor(out=Yt[:], in0=Yt[:], in1=Yd[:].rearrange("p (h d) -> p h d", h=4), op=ALU.add), rd=[Yt, Yd], wr=[Yt])
                    yield
                    ssd_epi(c0)
                    yield
                    k.op('pool', lambda e: e.tensor_tensor(out=Bd16[:], in0=Btk[:].unsqueeze(1).to_broadcast([128, 4, 128]), in1=LT[:, :, 127:128].to_broadcast([128, 4, 128]), op=ALU.mult), rd=[Btk, LT], wr=[Bd16])
                    pst = pX[:, 256:512].rearrange("p (h d) -> p h d", h=4)
                    for h in range(4):
                        mm(pst[:, h, :], Bd16[:, h, :], xd16[:, h, :], [Bd16, xd16], [pX])
                    k.op('dve', lambda e: e.tensor_tensor(out=htmp[:], in0=hT32[:], in1=ecs[:, 4:8].unsqueeze(2).to_broadcast([128, 4, 64]), op=ALU.mult), rd=[hT32, ecs], wr=[htmp])
                    k.op('dve', lambda e, pst=pst: e.tensor_tensor(out=hT32[:], in0=htmp[:], in1=pst, op=ALU.add), rd=[htmp, pX], wr=[hT32])
                    k.op('act', lambda e: e.activation(out=hTh[:], in_=hT32[:], func=AF.Copy), rd=[hT32], wr=[hTh])
                yield
            run_chains([('A', chain_A()), ('B', chain_B()), ('C', chain_C())])
            chk('ssd')
            chs = [('W', wout_chain(gidx))]
            if gidx + 1 < len(groups):
                chs.append(('P', proj_chain(gidx + 1)))
            run_chains(chs)
            if gidx + 2 < len(groups):
                load_g(gidx + 2)
            if dbg == ('mix', l) :
                k.dma(dbg_o[:, :, t0:t0 + gt], xb[:, :, 0:gt], dbg_o, xb)
            if gidx == len(groups) - 1:
                ec = nvalid - 1
                hr = dtv; hi = turns
                k.op('dve', lambda e: e.tensor_tensor(out=t8a[:], in0=cosT[:, :, ec], in1=gend_r[:], op=ALU.mult), rd=[cosT, gend_r], wr=[t8a])
                k.op('dve', lambda e: e.tensor_tensor(out=t8b[:], in0=sinT[:, :, ec], in1=gend_i[:], op=ALU.mult), rd=[sinT, gend_i], wr=[t8b])
                k.op('dve', lambda e: e.tensor_tensor(out=hr[:], in0=t8a[:], in1=t8b[:], op=ALU.subtract), rd=[t8a, t8b], wr=[hr])
                k.op('dve', lambda e: e.tensor_tensor(out=t8a[:], in0=sinT[:, :, ec], in1=gend_r[:], op=ALU.mult), rd=[sinT, gend_r], wr=[t8a])
                k.op('dve', lambda e: e.tensor_tensor(out=t8b[:], in0=cosT[:, :, ec], in1=gend_i[:], op=ALU.mult), rd=[cosT, gend_i], wr=[t8b])
                k.op('dve', lambda e: e.tensor_tensor(out=hi[:], in0=t8a[:], in1=t8b[:], op=ALU.add), rd=[t8a, t8b], wr=[hi])
                ore = tf; oim = den
                k.op('dve', lambda e: e.tensor_tensor(out=t8a[:], in0=fre[:], in1=hr[:], op=ALU.mult), rd=[fre, hr], wr=[t8a])
                k.op('dve', lambda e: e.tensor_tensor(out=t8b[:], in0=fim[:], in1=hi[:], op=ALU.mult), rd=[fim, hi], wr=[t8b])
                k.op('dve', lambda e: e.tensor_tensor(out=ore[:], in0=t8a[:], in1=t8b[:], op=ALU.subtract), rd=[t8a, t8b], wr=[ore])
                k.op('dve', lambda e: e.tensor_tensor(out=t8a[:], in0=fre[:], in1=hi[:], op=ALU.mult), rd=[fre, hi], wr=[t8a])
                k.op('dve', lambda e: e.tensor_tensor(out=t8b[:], in0=fim[:], in1=hr[:], op=ALU.mult), rd=[fim, hr], wr=[t8b])
                k.op('dve', lambda e: e.tensor_tensor(out=oim[:], in0=t8a[:], in1=t8b[:], op=ALU.add), rd=[t8a, t8b], wr=[oim])
                jv = junk[:, 0:512]
                pg = nextpg()
                k.op('pe', lambda e, pg=pg: e.transpose(pg[0:8, 0:128], ore[:], identf[:]), rd=[ore, identf], wr=[pg])
                k.op('pe', lambda e, pg=pg: e.transpose(pg[0:8, 128:256], oim[:], identf[:]), rd=[oim, identf], wr=[pg])
                k.op('dve', lambda e, pg=pg: e.tensor_copy(out=junk[0:8, 0:256], in_=pg[0:8, 0:256]), rd=[pg], wr=[junk])
                k.dma(s5re_p[l], junk[0:8, 0:128], s5re_p, junk); k.dma(s5im_p[l], junk[0:8, 128:256], s5im_p, junk)
                pg = nextpg()
                for j in range(4):
                    k.op('pe', lambda e, pg=pg, j=j: e.transpose(pg[0:3, j * 128:(j + 1) * 128], xpre[:, j, 8 + nvalid - 3:8 + nvalid], identf[:]), rd=[xpre, identf], wr=[pg])
                k.op('dve', lambda e, pg=pg: e.tensor_copy(out=junk[0:3, 0:512], in_=pg[0:3, 0:512]), rd=[pg], wr=[junk])
                k.dma(conv_p[l], junk[0:3, 0:512], conv_p, junk)
                pg = nextpg()
                for h in range(4):
                    g = h // 2
                    k.op('pe', lambda e, pg=pg, h=h, g=g: e.transpose(pg[0:64, h * 64:(h + 1) * 64], hT32[g * 64:(g + 1) * 64, h, :], identf[g * 64:(g + 1) * 64, g * 64:(g + 1) * 64]), rd=[hT32, identf], wr=[pg])
                k.op('dve', lambda e, pg=pg: e.tensor_copy(out=junk[0:64, 0:256], in_=pg[0:64, 0:256]), rd=[pg], wr=[junk])
                k.dma(ssd_p[l].rearrange("h p n -> p h n"), junk[0:64, 0:256].rearrange("p (h n) -> p h n", h=4), ssd_p, junk)
        if sample:
            gt = 128
            k.dma(xg[:, :, 0:128], xres_s[:], xg, xres_s)
            k.dma(ropc[:, 0:128], ropec_s_d[:], ropc, ropec_s_d); k.dma(rops[:, 0:128], ropes_s_d[:], rops, ropes_s_d)
            do_norm(128, xg)
            for _ in do_proj(128, 128): pass
            for _ in do_rope(128, sample=True): pass
            st0 = stg[0]; st1 = stg[1]
            S0v = st0[:, 0:2048].rearrange("p (a b) -> p a b", a=32); T1v = st1[:, 0:2048].rearrange("p (a b) -> p a b", a=32)
            pg = nextpg()
            for i in range(4):
                tr(pg[:, i * 128:(i + 1) * 128], qk32[:, i, 0:128], identf, [qk32], [pg])
            k.op('dve', lambda e, pg=pg: e.tensor_copy(out=junk[:, 0:512], in_=pg[:, 0:512]), rd=[pg], wr=[junk])
            pg = nextpg()
            for gi in range(2):
                k.op('pe', lambda e, pg=pg, gi=gi: e.transpose(pg[:, gi * 64:(gi + 1) * 64], rk32[0:64, gi, 0:128], identf[0:64, 0:64]), rd=[rk32, identf], wr=[pg])
            tr(pg[:, 128:256], vT32[:, 0:128], identf, [vT32], [pg])
            k.op('dve', lambda e, pg=pg: e.tensor_copy(out=ycg[:], in_=pg[:, 0:256].rearrange("p (a b) -> p a b", a=2)), rd=[pg], wr=[ycg])
            k.dma(k_s_d[l][:, 127, :], ycg[:, 0, :], k_s_d, ycg); k.dma(v_s_d[l][:, 127, :], ycg[:, 1, :], v_s_d, ycg)
            k.dma(k_s_d[l][:, 0:127, :], cache_k_d[l][:, 1:128, :], k_s_d, cache_k_d); k.dma(v_s_d[l][:, 0:127, :], cache_v_d[l][:, 1:128, :], v_s_d, cache_v_d)
            Sv = prodf[:, 0:1056].rearrange("p (h c) -> p h c", h=8)
            hb = []
            for nm, base in (("st0a", st0), ("st0b", st0), ("st1a", st1), ("st1b", st1)):
                off = 0 if nm.endswith("a") else 1024
                b_ = Buf(k, nm + "_%d" % l, base.t[:, off:off + 1024]); b_.lw = list(base.lw); b_.rd = list(base.rd)
                b_.ldsem = None
                hb.append(b_)
            Kh = [hb[0], hb[1]]; Th = [hb[2], hb[3]]
            kv3 = lambda b_: b_[:, 0:1024].rearrange("p (a b) -> p a b", a=16)
            stp = 0
            for gi in range(2):
                for c in range(8):
                    Kb = Kh[stp % 2]
                    k.dma(kv3(Kb), cache_k_d[l][:, 16 * c:16 * c + 16, gi * 64:(gi + 1) * 64], Kb, cache_k_d)
                    for r in range(4):
                        h = 4 * gi + r
                        Tb = Th[(stp * 4 + r) % 2]
                        k.op('pool' if r % 2 else 'dve', lambda e, h=h, Kb=Kb, Tb=Tb: e.tensor_tensor(out=kv3(Tb), in0=kv3(Kb), in1=junk[:, h * 64:(h + 1) * 64].unsqueeze(1).to_broadcast([128, 16, 64]), op=ALU.mult), rd=[Kb, junk], wr=[Tb])
                        k.op('dve', lambda e, h=h, c=c, Tb=Tb: e.tensor_reduce(out=Sv[:, h, 16 * c:16 * c + 16], in_=kv3(Tb), axis=AX.X, op=ALU.add), rd=[Tb], wr=[prodf])
                    stp += 1
            k.op('pool', lambda e: e.tensor_tensor(out=On[:].rearrange("p (g r) d -> p g r d", g=2), in0=junk[:, 0:512].rearrange("p (g r d) -> p g r d", g=2, r=4), in1=ycg[:, 0, :].rearrange("p (g d) -> p g d", g=2).unsqueeze(2).to_broadcast([128, 2, 4, 64]), op=ALU.mult), rd=[junk, ycg], wr=[On])
            k.op('dve', lambda e: e.tensor_reduce(out=Sv[:, :, 128], in_=On[:], axis=AX.X, op=ALU.add), rd=[On], wr=[prodf])
            k.op('dve', lambda e: e.tensor_reduce(out=mx[:], in_=Sv[:, :, 0:129], axis=AX.X, op=ALU.max), rd=[prodf], wr=[mx])
            k.op('dve', lambda e: e.scalar_tensor_tensor(out=ngm[:], in0=mx[:], scalar=-0.125, in1=negsink[:], op0=ALU.mult, op1=ALU.min), rd=[mx, negsink], wr=[ngm])
            k.op('pool', lambda e: e.memset(rs[:], 0.0), wr=[rs])
            for h in range(8):
                k.op('act', lambda e, h=h: e.activation(out=Sv[:, h, 0:129], in_=Sv[:, h, 0:129], func=AF.Exp, bias=ngm[:, h:h + 1], scale=0.125, accum_out=rs[:, h:h + 1]), rd=[prodf, ngm], wr=[prodf, rs])
            attn_den()
            k.op('pool', lambda e: e.memset(On[:], 0.0), wr=[On])
            for gi in range(2):
                for c in range(8):
                    Kb = Kh[stp % 2]
                    k.dma(kv3(Kb), cache_v_d[l][:, 16 * c:16 * c + 16, gi * 64:(gi + 1) * 64], Kb, cache_v_d)
                    for r in range(4):
                        h = 4 * gi + r
                        Tb = Th[(stp * 4 + r) % 2]
                        k.op('pool' if r % 2 else 'dve', lambda e, h=h, c=c, Kb=Kb, Tb=Tb: e.tensor_tensor(out=kv3(Tb), in0=kv3(Kb), in1=Sv[:, h, 16 * c:16 * c + 16].unsqueeze(2).to_broadcast([128, 16, 64]), op=ALU.mult), rd=[Kb, prodf], wr=[Tb])
                        k.op('dve', lambda e, Tb=Tb, r=r: e.tensor_reduce(out=htmp[:, r, :], in_=kv3(Tb).rearrange("p k d -> p d k"), axis=AX.X, op=ALU.add), rd=[Tb], wr=[htmp])
                    k.op('dve', lambda e, gi=gi: e.tensor_tensor(out=On[:, 4 * gi:4 * gi + 4, :], in0=On[:, 4 * gi:4 * gi + 4, :], in1=htmp[:], op=ALU.add), rd=[On, htmp], wr=[On])
                    stp += 1
            for b_ in hb:
                base = st0 if b_.name.startswith("st0") else st1
                for ev in b_.lw + b_.rd:
                    if ev not in base.rd: base.rd.append(ev)
            for h in range(8):
                g = h // 4
                k.op('dve', lambda e, h=h, g=g: e.scalar_tensor_tensor(out=On[:, h, :], in0=ycg[:, 1, g * 64:(g + 1) * 64], scalar=Sv[:, h, 128:129], in1=On[:, h, :], op0=ALU.mult, op1=ALU.add), rd=[ycg, prodf, On], wr=[On])
            k.op('dve', lambda e: e.tensor_tensor(out=On[:], in0=On[:], in1=rden[:].unsqueeze(2).to_broadcast([128, 8, 64]), op=ALU.mult), rd=[On, rden], wr=[On])
            attn_tail(0)
            h0r = st1[:, 0:1024].rearrange("p (a b) -> p a b", a=8); h0i = st1[:, 1024:2048].rearrange("p (a b) -> p a b", a=8)
            for src_d, dv in ((st_s5re_d, h0r), (st_s5im_d, h0i)):
                k.dma(st0[:, 0:1024], src_d[l], st0, src_d)
                for half in range(2):
                    pg = nextpg()
                    for jj in range(4):
                        j = half * 4 + jj
                        tr(pg[:, jj * 128:(jj + 1) * 128], st0[:, j * 128:(j + 1) * 128], identf, [st0], [pg])
                    k.op('dve', lambda e, pg=pg, dv=dv, half=half: e.tensor_copy(out=dv[:, half * 4:half * 4 + 4, :], in_=pg[:, :].rearrange("p (a b) -> p a b", a=4)), rd=[pg], wr=[st1])
            sc = lambda b_, j: b_[:, j:j + 1]
            for j in range(8):
                jq = j % 4
                pg1 = nextpg(); pg2 = nextpg()
                mm(pg1[:, 0:gt], Bre16[:, j // 4, j * 128:(j + 1) * 128], u16[:, j // 4, 0:gt], [Bre16, u16], [pg1])
                mm(pg2[:, 0:gt], Bim16[:, j // 4, j * 128:(j + 1) * 128], u16[:, j // 4, 0:gt], [Bim16, u16], [pg2])
                k.op('act', lambda e, pg1=pg1: e.activation(out=bur[:, 0:gt], in_=pg1[:, 0:gt], func=AF.Copy), rd=[pg1], wr=[bur])
                k.op('act', lambda e, pg2=pg2: e.activation(out=bui[:, 0:gt], in_=pg2[:, 0:gt], func=AF.Copy), rd=[pg2], wr=[bui])
                V = lambda b_: b_[:, 0:gt]
                ts = lambda o, i0, s1, rd_, wr_b: k.op('dve', lambda e: e.tensor_scalar(out=o, in0=i0, scalar1=s1, scalar2=None, op0=ALU.mult), rd=rd_, wr=[wr_b])
                stt = lambda o, i0, s1, i1, rd_, wr_b: k.op('dve', lambda e: e.scalar_tensor_tensor(out=o, in0=i0, scalar=s1, in1=i1, op0=ALU.mult, op1=ALU.add), rd=rd_, wr=[wr_b])
                ts(V(s5a), h0r[:, j, :], sc(finv_re, j), [st1, finv_re], s5a)
                stt(V(gr), h0i[:, j, :], sc(nfinv_im, j), V(s5a), [st1, nfinv_im, s5a], gr)
                ts(V(s5a), h0i[:, j, :], sc(finv_re, j), [st1, finv_re], s5a)
                stt(V(gi_), h0r[:, j, :], sc(finv_im, j), V(s5a), [st1, finv_im, s5a], gi_)
                stt(V(s5a), V(gr), sc(abr, j), V(bur), [gr, abr, bur], s5a)
                stt(V(wr_), V(gi_), sc(nabi, j), V(s5a), [gi_, nabi, s5a], wr_)
                stt(V(s5b), V(gi_), sc(abr, j), V(bui), [gi_, abr, bui], s5b)
                stt(V(wi_), V(gr), sc(abi, j), V(s5b), [gr, abi, s5b], wi_)
                k.op('pool', lambda e, jq=jq: e.tensor_copy(out=prod[:, jq, 0, 0:gt], in_=wr_[:, 0:gt]), rd=[wr_], wr=[prodf])
                k.op('pool', lambda e, jq=jq: e.tensor_copy(out=prod[:, jq, 1, 0:gt], in_=wi_[:, 0:gt]), rd=[wi_], wr=[prodf])
                ts(V(s5a), V(wr_), sc(fre, j), [wr_, fre], s5a)
                stt(h0r[:, j, :], V(wi_), sc(nfim, j), V(s5a), [wi_, nfim, s5a], st1)
                ts(V(s5a), V(wi_), sc(fre, j), [wi_, fre], s5a)
                stt(h0i[:, j, :], V(wr_), sc(fim, j), V(s5a), [wr_, fim, s5a], st1)
                if jq == 3:
                    m = j // 4
                    pgc = nextpg()
                    n = 0
                    for jj in range(4):
                        for q, Cm in enumerate([CRe, nCIm]):
                            mm(pgc[:, 0:gt], Cm[:, 4 * m + jj, :], prod[:, jj, q, 0:gt], [Cm, prodf], [pgc], start=(n == 0), stop=(n == 7)); n += 1
                    k.op('dve', lambda e, pgc=pgc, m=m: e.scalar_tensor_tensor(out=ys[:, m, 0:gt], in0=u32[:, m, 0:gt], scalar=colp[:, 48 + m:49 + m], in1=pgc[:, 0:gt], op0=ALU.mult, op1=ALU.add), rd=[u32, colp, pgc], wr=[ys])
            s5_epi(128)
            for dv, dst_d in ((h0r, s5re_s_d), (h0i, s5im_s_d)):
                for half in range(2):
                    pg = nextpg()
                    for jj in range(4):
                        tr(pg[:, jj * 128:(jj + 1) * 128], dv[:, half * 4 + jj, :], identf, [st1], [pg])
                    k.op('dve', lambda e, pg=pg, half=half: e.tensor_copy(out=st0[:, half * 512:(half + 1) * 512], in_=pg[:, :]), rd=[pg], wr=[st0])
                k.dma(dst_d[l], st0[:, 0:1024], dst_d, st0)
            k.dma(st0[:, 0:1536], st_conv_d[l], st0, st_conv_d)
            planes = st1[:, 0:1536].rearrange("p (t j b) -> p t j b", t=3, j=4)
            for tap in range(3):
                pg = nextpg()
                for j in range(4):
                    tr(pg[:, j * 128:(j + 1) * 128], st0[:, tap * 512 + j * 128:tap * 512 + (j + 1) * 128], identf, [st0], [pg])
                k.op('dve', lambda e, pg=pg, tap=tap: e.tensor_copy(out=planes[:, tap], in_=pg[:, :].rearrange("p (a b) -> p a b", a=4)), rd=[pg], wr=[st1])
            for j in range(4):
                wc = lambda tap: colp[:, 52 + j * 4 + tap:53 + j * 4 + tap]
                k.op('pool', lambda e, j=j, w0=wc(0): e.tensor_scalar(out=xc32[:, j, 0:gt], in0=planes[:, 0, j], scalar1=w0, scalar2=colp[:, 68 + j:69 + j], op0=ALU.mult, op1=ALU.add), rd=[st1, colp], wr=[xc32])
                for tap in (1, 2):
                    k.op('dve', lambda e, j=j, tap=tap, w=wc(tap): e.scalar_tensor_tensor(out=xc32[:, j, 0:gt], in0=planes[:, tap, j], scalar=w, in1=xc32[:, j, 0:gt], op0=ALU.mult, op1=ALU.add), rd=[st1, colp, xc32], wr=[xc32])
                k.op('dve', lambda e, j=j, w=wc(3): e.scalar_tensor_tensor(out=xc32[:, j, 0:gt], in0=xpre[:, j, 8:8 + gt], scalar=w, in1=xc32[:, j, 0:gt], op0=ALU.mult, op1=ALU.add), rd=[xpre, colp, xc32], wr=[xc32])
            k.op('act', lambda e: e.activation(out=xc32[:, :, 0:gt], in_=xc32[:, :, 0:gt], func=AF.Silu), rd=[xc32], wr=[xc32])
            k.op('act', lambda e: e.activation(out=xc16[:, :, 0:gt], in_=xc32[:, :, 0:gt], func=AF.Copy), rd=[xc32], wr=[xc16])
            k.dma(conv_s_d[l][:, 0:1024], st_conv_d[l][:, 512:1536], conv_s_d, st_conv_d)
            pg = nextpg()
            for j in range(4):
                tr(pg[:, j * 128:(j + 1) * 128], xpre[:, j, 8:8 + 128], identf, [xpre], [pg])
            k.op('dve', lambda e, pg=pg: e.tensor_copy(out=junk[:, 0:512], in_=pg[:, :]), rd=[pg], wr=[junk])
            k.dma(conv_s_d[l][:, 1024:1536], junk[:, 0:512], conv_s_d, junk)
            pg = nextpg()
            for j in range(4):
                tr(pg[:, j * 128:(j + 1) * 128], xc32[:, j, 0:128], identf, [xc32], [pg])
            xtk = LA[:].rearrange("p a b -> p (a b)")
            k.op('dve', lambda e, pg=pg: e.tensor_copy(out=xtk, in_=pg[:, :]), rd=[pg], wr=[LA])
            pg = nextpg()
            k.op('pe', lambda e, pg=pg: e.transpose(pg[:, 0:4], dtT[0:4, 0:128], identf[0:4, 0:4]), rd=[dtT, identf], wr=[pg])
            k.op('dve', lambda e, pg=pg: e.tensor_copy(out=dtk[:], in_=pg[:, 0:4]), rd=[pg], wr=[dtk])
            k.op('dve', lambda e: e.tensor_tensor(out=atok[:], in0=dtk[:], in1=Atab[:], op=ALU.mult), rd=[dtk, Atab], wr=[atok])
            k.op('act', lambda e: e.activation(out=ecs[:, 0:4], in_=atok[:], func=AF.Exp), rd=[atok], wr=[ecs])
            k.op('dve', lambda e: e.tensor_tensor(out=Yd[:].rearrange("p (h d) -> p h d", h=4), in0=xtk[:, 0:256].rearrange("p (h d) -> p h d", h=4), in1=dtk[:].unsqueeze(2).to_broadcast([128, 4, 64]), op=ALU.mult), rd=[LA, dtk], wr=[Yd])
            for h in range(4):
                g = h // 2
                for ph in range(2):
                    o0 = h * 4096 + ph * 2048
                    k.dma(st0[:, 0:2048], st_ssd_d[l][:, o0:o0 + 2048], st0, st_ssd_d)
                    k.op('dve', lambda e, h=h, ph=ph, g=g: e.tensor_tensor(out=T1v, in0=Yd[:, h * 64 + ph * 32:h * 64 + ph * 32 + 32].unsqueeze(2).to_broadcast([128, 32, 64]), in1=xtk[:, 256 + g * 64:256 + (g + 1) * 64].unsqueeze(1).to_broadcast([128, 32, 64]), op=ALU.mult), rd=[Yd, LA], wr=[st1])
                    k.op('dve', lambda e, h=h: e.scalar_tensor_tensor(out=S0v, in0=S0v, scalar=ecs[:, h:h + 1], in1=T1v, op0=ALU.mult, op1=ALU.add), rd=[st0, ecs, st1], wr=[st0])
                    k.dma(ssd_s_d[l][:, o0:o0 + 2048], st0[:, 0:2048], ssd_s_d, st0)
                    k.op('dve', lambda e, g=g: e.tensor_tensor(out=T1v, in0=S0v, in1=xtk[:, 384 + g * 64:384 + (g + 1) * 64].unsqueeze(1).to_broadcast([128, 32, 64]), op=ALU.mult), rd=[st0, LA], wr=[st1])
                    k.op('dve', lambda e, h=h, ph=ph: e.tensor_reduce(out=Yt[:, h, ph * 32:(ph + 1) * 32], in_=T1v, axis=AX.X, op=ALU.add), rd=[st1], wr=[Yt])
            ssd_epi(0)
            for _ in wout(128): pass
            k.dma(xres_s[:], xg[:, :, 0:128], xres_s, xg)
        chk('mixgroups')
        Onf = On[:].rearrange("p h d -> p (h d)")
        k.dma(Onf[:, 0:128], kscr[:, T - 128:T], On, kscr); k.dma(Onf[:, 128:256], vscr[:, T - 128:T], On, vscr)
        pg = nextpg()
        for a in range(2):
            tr(pg[:, a * 128:(a + 1) * 128], Onf[:, a * 128:(a + 1) * 128], identf, [On], [pg])
        k.op('dve', lambda e, pg=pg: e.tensor_copy(out=junk[:, 0:256], in_=pg[:, 0:256]), rd=[pg], wr=[junk])
        k.dma(k_p[l], junk[:, 0:128], k_p, junk); k.dma(v_p[l], junk[:, 128:256], v_p, junk)
        A.release()
        barrier(k)
        chk('mix')
        A.mark()
        Wg16 = A.buf("Wg16", [128, 8, 2816], BF16); Wu16 = A.buf("Wu16", [128, 8, 2816], BF16); Wd16 = A.buf("Wd16", [128, 22, 1024], BF16)
        load_w(Wg16, lambda c: w_g_d[l, c * 128:(c + 1) * 128, :], 8, 2816, lambda c: colp[:, 16 + c:17 + c])
        load_w(Wu16, lambda c: w_u_d[l, c * 128:(c + 1) * 128, :], 8, 2816, lambda c: colp[:, 16 + c:17 + c])
        load_w(Wd16, lambda c: w_d_d[l, c * 128:(c + 1) * 128, :], 22, 1024)
        sq16 = A.buf("sq16f", [128, 8, G], BF16); hTs = [hT16, A.buf("hT16b", [128, 8, G], BF16)]
        actf = A.buf("actf", [128, 11 * G]); actT = Buf(k, "actT", actf.t.bitcast(BF16).rearrange("p (a b) -> p a b", a=22)); actT.lw = actf.lw; actT.rd = actf.rd; sgt = [A.buf("sgt%d" % i, [128, G]) for i in range(2)]
        yT = Buf(k, "yT", actf.t[:, 0:8 * G].rearrange("p (a b) -> p a b", a=8)); yT.lw = actf.lw; yT.rd = actf.rd; yo = A.buf("yo", [128, 1024])
        ffn_items = [('p', t0, gt) for (t0, gt) in groups] + ([('s', 0, 128)] if sample else [])
        def ffn_src(it):
            kind_, t0_, gt_ = it
            return (xres, xres[:, :, t0_:t0_ + gt_]) if kind_ == 'p' else (xres_s, xres_s[:])
        for fi, (kind, t0, gt) in enumerate(ffn_items):
            xsrc, xsl = ffn_src(ffn_items[fi])
            xg = xgs[fi % 2]
            if fi == 0:
                k.dma(xg[:, :, 0:gt], xsl, xg, xsrc)
            if fi + 1 < len(ffn_items):
                xsrcn, xsln = ffn_src(ffn_items[fi + 1]); xgn = xgs[(fi + 1) % 2]
                k.dma(xgn[:, :, 0:ffn_items[fi + 1][2]], xsln, xgn, xsrcn)
            hTc = hTs[fi % 2]
            if fi == 0:
                rmsnorm_T(xg, 8, gt, 1024.0, hTc, sq16, rstd)
            for ht in range(22):
                pg1 = nextpg()
                for kt in range(8):
                    mm(pg1[:, 0:gt], Wg16[:, kt, ht * 128:(ht + 1) * 128], hTc[:, kt, 0:gt], [Wg16, hTc], [pg1], start=(kt == 0), stop=(kt == 7))
                st_ = sgt[ht % 2]
                k.op('act', lambda e, pg1=pg1, st_=st_: e.activation(out=st_[:, 0:gt], in_=pg1[:, 0:gt], func=AF.Silu), rd=[pg1], wr=[st_])
                pg2 = nextpg()
                for kt in range(8):
                    mm(pg2[:, 0:gt], Wu16[:, kt, ht * 128:(ht + 1) * 128], hTc[:, kt, 0:gt], [Wu16, hTc], [pg2], start=(kt == 0), stop=(kt == 7))
                k.op('dve', lambda e, pg2=pg2, st_=st_, ht=ht: e.tensor_tensor(out=actT[:, ht, 0:gt], in0=st_[:, 0:gt], in1=pg2[:, 0:gt], op=ALU.mult), rd=[st_, pg2], wr=[actT])
            if fi + 1 < len(ffn_items):
                rmsnorm_T(xgn, 8, ffn_items[fi + 1][2], 1024.0, hTs[(fi + 1) % 2], sq16, rstd)
            for dt_ in range(8):
                pg = nextpg()
                for ht in range(22):
                    mm(pg[:, 0:gt], Wd16[:, ht, dt_ * 128:(dt_ + 1) * 128], actT[:, ht, 0:gt], [Wd16, actT], [pg], start=(ht == 0), stop=(ht == 21))
                k.op('dve', lambda e, pg=pg, dt_=dt_: e.tensor_tensor(out=xg[:, dt_, 0:gt], in0=xg[:, dt_, 0:gt], in1=pg[:, 0:gt], op=ALU.add), rd=[xg, pg], wr=[xg])
            if l < L - 1:
                k.dma(xsl, xg[:, :, 0:gt], xsrc, xg)
            else:
                k.op('act', lambda e: e.activation(out=sq16[:, :, 0:gt], in_=xg[:, :, 0:gt], func=AF.Square), rd=[xg], wr=[sq16])
                pg = nextpg()
                for i in range(8):
                    mm(pg[:, 0:gt], onesb[:], sq16[:, i, 0:gt], [onesb, sq16], [pg], start=(i == 0), stop=(i == 7))
                k.op('act', lambda e, pg=pg: e.activation(out=rstd[:, 0:gt], in_=pg[:, 0:gt], func=AF.Ln, scale=1.0 / 1024, bias=epsc[:, 0:1]), rd=[pg, epsc], wr=[rstd])
                k.op('act', lambda e: e.activation(out=rstd[:, 0:gt], in_=rstd[:, 0:gt], func=AF.Exp, scale=-0.5), rd=[rstd], wr=[rstd])
                for i in range(8):
                    k.op('dve', lambda e, i=i: e.scalar_tensor_tensor(out=yT[:, i, 0:gt], in0=xg[:, i, 0:gt], scalar=lnf[:, i:i + 1], in1=rstd[:, 0:gt], op0=ALU.mult, op1=ALU.mult), rd=[xg, lnf, rstd], wr=[yT])
                for bi in range(gt // 128):
                    tok0 = t0 + bi * 128
                    lo = max(tok0, 16); hi_ = min(tok0 + 128, T)
                    if kind == 's': lo, hi_ = 0, 128
                    if hi_ <= lo: continue
                    for half in range(2):
                        pg = nextpg()
                        for j in range(4):
                            tr(pg[:, j * 128:(j + 1) * 128], yT[:, half * 4 + j, bi * 128:(bi + 1) * 128], identf, [yT], [pg])
                        k.op('dve' if half == 0 else 'act',
                             (lambda e, pg=pg, half=half: e.tensor_copy(out=yo[:, half * 512:(half + 1) * 512], in_=pg[:, :])) if half == 0 else
                             (lambda e, pg=pg, half=half: e.activation(out=yo[:, half * 512:(half + 1) * 512], in_=pg[:, :], func=AF.Copy)), rd=[pg], wr=[yo])
                    if kind == 's':
                        k.dma(y_sample[:], yo[:], y_sample, yo)
                    else:
                        k.dma(y_prompt[lo - 16:hi_ - 16, :], yo[lo - tok0:hi_ - tok0, :], y_prompt, yo)
        xg = xgs[0]
        A.release()
    k.finish()
    k.close()
    print('arena high-water', A.hw, 'of', A.words)
    return k

def prep_inputs(inp, b, seq, L=4):
    f32=np.float32
    T=seq+16; NB=(T+127)//128; TP=NB*128
    d={}
    xin=np.zeros((TP,1024),f32); xin[:16]=inp['meta_tokens']; xin[16:T]=inp['x_prompt'][b,:seq]
    d['xin']=xin
    d['ident']=np.eye(128,dtype=f32)
    R=np.zeros((128,128),f32)
    for m in range(128):
        if m%64<32: R[m+32,m]=-1.0
        else: R[m-32,m]=1.0
    d['rotm']=R
    half=32
    inv=(10000.0**(-np.arange(half,dtype=np.float32)/half)).astype(f32)
    pos=np.arange(TP,dtype=f32)
    ang=(pos[None,:]*inv[:,None]).astype(f32)
    d['ropec']=np.tile(np.cos(ang).astype(f32),(4,1)); d['ropes']=np.tile(np.sin(ang).astype(f32),(4,1))
    NEG=-240000.0
    i=np.arange(128)[:,None]; j=np.arange(128)[None,:]
    m0=np.concatenate([np.where(j>=i,0.0,NEG),np.where(j<=i,0.0,NEG)],axis=1).astype(f32)
    m1=m0.copy(); m1[:,:128]=NEG
    d['amask']=np.stack([m0,m1])
    kk=np.arange(128)[:,None]; jj=np.arange(128)[None,:]
    d['slt']=(kk>jj).astype(f32); d['umat']=(kk<=jj).astype(f32)
    d['negm']=np.where(jj<kk,-30000.0,0.0).astype(f32)
    d['iota']=np.tile(np.arange(G,dtype=f32)[None,:],(128,1))
    w=inp['w_in'][:L]
    d['w_in_x']=np.ascontiguousarray(np.concatenate([w[:,:,0:512],w[:,:,512:576],w[:,:,512:576],w[:,:,576:640],w[:,:,576:640],w[:,:,640:1796]],axis=2))
    d['w_out']=np.ascontiguousarray(inp['w_out'][:L]); d['w_gate']=np.ascontiguousarray(inp['w_gate'][:L]); d['w_up']=np.ascontiguousarray(inp['w_up'][:L]); d['w_down']=np.ascontiguousarray(inp['w_down'][:L])
    d['glu_w']=np.ascontiguousarray(inp['s5_glu_w'][:L])
    bre=np.zeros((L,256,1024),f32); bim=np.zeros((L,256,1024),f32)
    for g in range(16):
        bre[:,g*16:(g+1)*16,g*64:(g+1)*64]=np.transpose(inp['s5_b_re'][:L,g],(0,2,1))
        bim[:,g*16:(g+1)*16,g*64:(g+1)*64]=np.transpose(inp['s5_b_im'][:L,g],(0,2,1))
    d['bblk_re']=bre; d['bblk_im']=bim
    cre=np.zeros((L,128,8,128),f32); cim=np.zeros((L,128,8,128),f32)
    for g in range(16):
        jch=g//2; r0=(g%2)*64; c0=(g%8)*16
        cre[:,r0:r0+64,jch,c0:c0+16]=np.transpose(inp['s5_c_re'][:L,g],(0,2,1))
        cim[:,r0:r0+64,jch,c0:c0+16]=np.transpose(inp['s5_c_im'][:L,g],(0,2,1))
    d['cpad_re']=cre; d['cpad_im']=cim
    NC=76
    cp=np.zeros((L,128,NC),f32)
    col=lambda v: np.transpose(v.reshape(L,-1,128),(0,2,1))
    cp[:,:,0:8]=col(inp['ln1_g'][:L])
    gmix=np.concatenate([inp['attn_out_g'][:L],inp['s5_out_g'][:L],inp['ssd_norm_g'][:L]],axis=1)
    cp[:,:,8:16]=col(gmix); cp[:,:,16:24]=col(inp['ln2_g'][:L])
    chan=lambda v: np.transpose(v.reshape(L,8,2*64),(0,2,1))
    cp[:,:,24:32]=chan(inp['s5_a_re'][:L]); cp[:,:,32:40]=chan(inp['s5_a_im'][:L])
    cp[:,:,40:48]=chan(np.repeat(inp['s5_log_dt'][:L,:,None],64,axis=2))
    cp[:,:,48:50]=col(inp['s5_d'][:L]); cp[:,:,50:52]=col(inp['s5_glu_b'][:L])
    cw=inp['ssd_conv_w'][:L]
    for j in range(4):
        for tap in range(4):
            cp[:,:,52+j*4+tap]=cw[:,tap,j*128:(j+1)*128]
    cp[:,:,68:72]=col(inp['ssd_conv_b'][:L])
    cp[:,:,72:74]=col(np.repeat(inp['ssd_d'][:L],64,axis=1))
    cp[:,0:4,74]=inp['ssd_dt_bias'][:L]
    d['colpack']=cp
    rp=np.zeros((L,1,16),f32); rp[:,0,0:8]=inp['attn_sinks'][:L]; rp[:,0,8:12]=inp['ssd_a_log'][:L]
    d['rowpack']=rp
    d['lnf_cols']=np.ascontiguousarray(inp['lnf_g'].reshape(8,128).T)
    d['xs_in']=np.ascontiguousarray(inp['x_sample'][:,0,:])
    d['cache_k']=np.ascontiguousarray(inp['cache_k'][:L]).reshape(L,128,128,128); d['cache_v']=np.ascontiguousarray(inp['cache_v'][:L]).reshape(L,128,128,128)
    d['st_s5re']=np.ascontiguousarray(inp['state_s5_re'][:L]).reshape(L,128,1024); d['st_s5im']=np.ascontiguousarray(inp['state_s5_im'][:L]).reshape(L,128,1024)
    d['st_conv']=np.ascontiguousarray(inp['state_ssd_conv'][:L]).reshape(L,128,1536); d['st_ssd']=np.ascontiguousarray(inp['state_ssd'][:L]).reshape(L,128,16384)
    angs=(np.float32(8192.0)*inv).astype(f32)
    d['ropec_s']=np.tile(np.tile(np.cos(angs).astype(f32),4)[:,None],(1,128)).astype(f32); d['ropes_s']=np.tile(np.tile(np.sin(angs).astype(f32),4)[:,None],(1,128)).astype(f32)
    return d


_CACHE = {}

def kernel(**inputs):
    from concourse.bass_utils import run_bass_kernel_spmd
    inp = {k_: np.asarray(v) for k_, v in inputs.items()}
    seq = inp['x_prompt'].shape[1]; L = inp['w_in'].shape[0]; B = inp['x_prompt'].shape[0]
    key = (seq, L)
    if key not in _CACHE:
        _CACHE[key] = build(seq, L)
    kb = _CACHE[key]
    per_seq = [prep_inputs(inp, b, seq, L) for b in range(B)]
    for b in range(1, B):
        for n in per_seq[0]:
            if n != 'xin':
                per_seq[b][n] = per_seq[0][n]
    n_cores = 8
    in_maps = [per_seq[c % B] for c in range(n_cores)]
    res = run_bass_kernel_spmd(kb.nc, in_maps, core_ids=list(range(n_cores))).results
    f32 = np.float32
    y_prompt = np.stack([res[b]['y_prompt'] for b in range(B)]).astype(f32)
    st = lambda n: np.stack([res[b][n] for b in range(B)], axis=1)
    k_p = st('k_p').reshape(L, B, 128, 2, 64); v_p = st('v_p').reshape(L, B, 128, 2, 64)
    s5re = st('s5re_p').reshape(L, B, 16, 64); s5im = st('s5im_p').reshape(L, B, 16, 64)
    conv_p = st('conv_p'); ssd_p = st('ssd_p')
    DB = inp['x_sample'].shape[0]
    r0 = res[0]
    y_sample = r0['y_sample'].reshape(DB, 1, 1024).astype(f32)
    k_s = r0['k_s'].reshape(L, DB, 128, 2, 64); v_s = r0['v_s'].reshape(L, DB, 128, 2, 64)
    s5re_s = r0['s5re_s'].reshape(L, DB, 16, 64); s5im_s = r0['s5im_s'].reshape(L, DB, 16, 64)
    conv_s = r0['conv_s'].reshape(L, DB, 3, 512); ssd_s = r0['ssd_s'].reshape(L, DB, 4, 64, 64)
    return (y_prompt, y_sample, k_p, v_p, s5re, s5im, conv_p, ssd_p, k_s, v_s, s5re_s, s5im_s, conv_s, ssd_s)
```

```python
import numpy as np
import concourse.bass as bass
import concourse.mybir as mybir
F32=mybir.dt.float32; BF16=mybir.dt.bfloat16; I32=mybir.dt.int32
AF=mybir.ActivationFunctionType; ALU=mybir.AluOpType; AX=mybir.AxisListType

class Buf:
    def __init__(self, k, name, t):
        self.k=k; self.name=name; self.t=t
        self.lw=[]
        self.rd=[]
        self.is_dram=False
        self.ldsem=None; self.ldcnt=0
        self.stsem=None; self.stcnt=0
    def __getitem__(self, idx): return self.t[idx]

class KB:
    def __init__(self):
        self.nc = bass.Bass("TRN2", target_bir_lowering=False)
        nc=self.nc
        self.eng={'pe':nc.tensor,'act':nc.scalar,'dve':nc.vector,'pool':nc.gpsimd,'sp':nc.sync}
        self.sem={}; self.cnt={}; self.seen={}
        self._ctx=[]
        for e in ['pe','act','dve','pool']:
            self.sem[e]=self._enter(nc.semaphore("s_"+e)); self.cnt[e]=0
        for e in self.eng: self.seen[e]={}
        self.pending={e:[] for e in self.eng}
        self.nsem=4; self.dmasems={}; self.d2d=None; self.d2dcnt=0; self.outs=[]
    def _enter(self, cm):
        v=cm.__enter__(); self._ctx.append(cm); return v
    def close(self):
        for cm in reversed(self._ctx): cm.__exit__(None,None,None)
    def newsem(self, name):
        self.nsem+=1
        return self._enter(self.nc.semaphore(name+"_%d"%self.nsem))
    def sb(self, name, shape, dt=F32):
        return Buf(self, name, self._enter(self.nc.sbuf_tensor(name, list(shape), dt)))
    def ps(self, name, shape, dt=F32):
        return Buf(self, name, self._enter(self.nc.psum_tensor(name, list(shape), dt)))
    def dram(self, name, shape, dt=F32, kind="Internal"):
        b=Buf(self, name, self.nc.dram_tensor(name, list(shape), dt, kind=kind).ap()); b.is_dram=True
        if kind=="ExternalOutput": self.outs.append(b)
        return b
    def need(self, e, ev):
        s,v=ev
        if s in self.dmasems:
            v=max(v,self.dmasems[s][0])
        key=id(s)
        if self.seen[e].get(key,0)>=v: return
        self.seen[e][key]=v
        self.eng[e].wait_ge(s,v)
    def op(self, e, fn, rd=(), wr=(), sig=True):
        for b in rd:
            for ev in b.lw: self.need(e,ev)
        for b in wr:
            for ev in b.lw: self.need(e,ev)
            for ev in b.rd: self.need(e,ev)
        ins=fn(self.eng[e])
        if sig:
            self.cnt[e]+=1
            ins.then_inc(self.sem[e],1)
            ev=(self.sem[e],self.cnt[e])
            for b in self.pending[e]: b.rd.append(ev)
            self.pending[e]=[]
            for b in wr: b.lw[:]=[ev]; b.rd.clear()
            for b in rd:
                if b not in wr: b.rd.append(ev)
        else:
            for b in rd: self.pending[e].append(b)
        return ins
    def dma(self, out_ap, in_ap, dst, src, q='sp', **kw):
        for ev in src.lw: self.need(q,ev)
        for ev in dst.lw: self.need(q,ev)
        for ev in dst.rd: self.need(q,ev)
        if not dst.is_dram:
            if dst.ldsem is None: dst.ldsem=self.newsem("ld_"+dst.name)
            dst.ldcnt+=16; sem=dst.ldsem; val=dst.ldcnt
        elif not src.is_dram:
            if src.stsem is None: src.stsem=self.newsem("st_"+src.name)
            src.stcnt+=16; sem=src.stsem; val=src.stcnt
        else:
            if self.d2d is None: self.d2d=self.newsem("d2d")
            self.d2dcnt+=16; sem=self.d2d; val=self.d2dcnt
        ins=self.eng[q].dma_start(out=out_ap, in_=in_ap, **kw)
        ins.then_inc(sem,16)
        ev=(sem,val); self.dmasems[sem]=[val]
        dst.lw[:]=[x for x in dst.lw if x[0] is not sem]+[ev]
        dst.rd.clear()
        src.rd[:]=[x for x in src.rd if x[0] is not sem]+[ev]
        return ins
    def finish(self, q='sp'):
        for b in self.outs:
            for ev in b.lw: self.need(q,ev)

EPS=1e-6
TWO_PI_SAFE=6.283185

def sincos(k, turns, s_out, c_out, tmp_i, tmp_f, shape_key):
    k.op('dve',lambda e:e.tensor_copy(out=tmp_i[:],in_=turns[:]),rd=[turns],wr=[tmp_i])
    k.op('dve',lambda e:e.tensor_copy(out=tmp_f[:],in_=tmp_i[:]),rd=[tmp_i],wr=[tmp_f])
    k.op('dve',lambda e:e.tensor_tensor(out=tmp_f[:],in0=turns[:],in1=tmp_f[:],op=ALU.subtract),rd=[turns,tmp_f],wr=[tmp_f])
    k.op('act',lambda e:e.activation(out=s_out[:],in_=tmp_f[:],func=AF.Sin,scale=TWO_PI_SAFE),rd=[tmp_f],wr=[s_out])
    k.op('dve',lambda e:e.tensor_scalar(out=turns[:],in0=turns[:],scalar1=0.25,scalar2=None,op0=ALU.add),rd=[turns],wr=[turns])
    k.op('dve',lambda e:e.tensor_copy(out=tmp_i[:],in_=turns[:]),rd=[turns],wr=[tmp_i])
    k.op('dve',lambda e:e.tensor_copy(out=tmp_f[:],in_=tmp_i[:]),rd=[tmp_i],wr=[tmp_f])
    k.op('dve',lambda e:e.tensor_tensor(out=tmp_f[:],in0=turns[:],in1=tmp_f[:],op=ALU.subtract),rd=[turns,tmp_f],wr=[tmp_f])
    k.op('act',lambda e:e.activation(out=c_out[:],in_=tmp_f[:],func=AF.Sin,scale=TWO_PI_SAFE),rd=[tmp_f],wr=[c_out])


NCW = 1924
NC = 76
G = 256

class Arena:
    def __init__(self, k, words):
        self.k = k; self.words = words
        self.t = k._enter(k.nc.sbuf_tensor("arena", [128, words], F32))
        self.off = 0; self.marks = []
    def buf(self, name, shape, dt=F32):
        n = int(np.prod(shape[1:]))
        w = n if dt in (F32, I32) else (n + 1) // 2
        w = (w + 7) // 8 * 8
        assert self.off + w <= self.words, (name, self.off, w, self.words)
        ap = self.t[:, self.off:self.off + w]
        if dt != F32:
            ap = ap.bitcast(dt)
        ap = ap[:, 0:n]
        if len(shape) == 3:
            ap = ap.rearrange("p (a b) -> p a b", a=shape[1])
        elif len(shape) == 4:
            ap = ap.rearrange("p (a b c) -> p a b c", a=shape[1], b=shape[2])
        if shape[0] < 128:
            ap = ap[0:shape[0]]
        self.off += w
        self.hw = max(getattr(self, 'hw', 0), self.off)
        return Buf(self.k, name, ap)
    def mark(self): self.marks.append(self.off)
    def release(self): self.off = self.marks.pop()


def barrier(k):
    evs = [(k.sem[f], k.cnt[f]) for f in ['pe', 'act', 'dve', 'pool'] if k.cnt[f] > 0]
    evs += [(s, v[0]) for s, v in k.dmasems.items()]
    for e in ['pe', 'act', 'dve', 'pool', 'sp']:
        for ev in evs:
            k.need(e, ev)


class StopBuild(Exception): pass

def build(seq, L=4, dbg=None, stop=None, sample=True):
    try:
        return build_(seq, L, dbg, stop, sample)
    except StopBuild as e:
        k = e.args[0]; k.finish(); k.close(); return k

def build_(seq, L=4, dbg=None, stop=None, sample=True):
    T = seq + 16
    NB = (T + 127) // 128
    TP = NB * 128
    groups = []
    t0 = 0
    while t0 < TP:
        gt = min(G, TP - t0)
        groups.append((t0, gt)); t0 += gt
    k = KB(); nc = k.nc
    def chk(tag):
        if stop == tag: raise StopBuild(k)
    din = lambda n, s: k.dram(n, s, kind="ExternalInput")
    dout = lambda n, s: k.dram(n, s, kind="ExternalOutput")
    xin = din("xin", [TP, 1024]); ident_d = din("ident", [128, 128]); rotm_d = din("rotm", [128, 128])
    ropec_d = din("ropec", [128, TP]); ropes_d = din("ropes", [128, TP])
    amask_d = din("amask", [2, 128, 256]); slt_d = din("slt", [128, 128]); umat_d = din("umat", [128, 128]); negm_d = din("negm", [128, 128])
    iota_d = din("iota", [128, G])
    w_in_d = din("w_in_x", [L, 1024, NCW]); w_out_d = din("w_out", [L, 1024, 1024])
    w_g_d = din("w_gate", [L, 1024, 2816]); w_u_d = din("w_up", [L, 1024, 2816]); w_d_d = din("w_down", [L, 2816, 1024])
    glu_d = din("glu_w", [L, 256, 256]); bre_d = din("bblk_re", [L, 256, 1024]); bim_d = din("bblk_im", [L, 256, 1024])
    cre_d = din("cpad_re", [L, 128, 8, 128]); cim_d = din("cpad_im", [L, 128, 8, 128])
    colp_d = din("colpack", [L, 128, NC]); rowp_d = din("rowpack", [L, 1, 16]); lnf_d = din("lnf_cols", [128, 8])
    y_prompt = dout("y_prompt", [seq, 1024])
    k_p = dout("k_p", [L, 128, 128]); v_p = dout("v_p", [L, 128, 128])
    s5re_p = dout("s5re_p", [L, 8, 128]); s5im_p = dout("s5im_p", [L, 8, 128])
    conv_p = dout("conv_p", [L, 3, 512]); ssd_p = dout("ssd_p", [L, 4, 64, 64])
    dbg_o = dout("dbg", [128, 8, TP]) if dbg else None
    if sample:
        xs_in = din("xs_in", [128, 1024]); cache_k_d = din("cache_k", [L, 128, 128, 128]); cache_v_d = din("cache_v", [L, 128, 128, 128])
        st_s5re_d = din("st_s5re", [L, 128, 1024]); st_s5im_d = din("st_s5im", [L, 128, 1024])
        st_conv_d = din("st_conv", [L, 128, 1536]); st_ssd_d = din("st_ssd", [L, 128, 16384])
        ropec_s_d = din("ropec_s", [128, 128]); ropes_s_d = din("ropes_s", [128, 128])
        y_sample = dout("y_sample", [128, 1024]); k_s_d = dout("k_s", [L, 128, 128, 128]); v_s_d = dout("v_s", [L, 128, 128, 128])
        s5re_s_d = dout("s5re_s", [L, 128, 1024]); s5im_s_d = dout("s5im_s", [L, 128, 1024])
        conv_s_d = dout("conv_s", [L, 128, 1536]); ssd_s_d = dout("ssd_s", [L, 128, 16384])
        xres_s = k.dram("xres_s", [128, 8, 128])
    xres = k.dram("xres", [128, 8, TP]); kscr = k.dram("kscr", [128, TP]); vscr = k.dram("vscr", [128, TP])

    A = Arena(k, 53000)
    identf = A.buf("identf", [128, 128]); identb = A.buf("identb", [128, 128], BF16)
    rotm = A.buf("rotm", [128, 128]); onesb = A.buf("onesb", [128, 128], BF16); onesf = A.buf("onesf", [128, 128])
    slt = A.buf("slt", [128, 128]); umat = A.buf("umat", [128, 128]); negm = A.buf("negm", [128, 128])
    maskf = A.buf("maskf", [128, 2, 256]); maskb = A.buf("maskb", [128, 2, 256], BF16)
    iota = A.buf("iota", [128, G]); lnf = A.buf("lnf", [128, 8])
    colp = A.buf("colp", [128, NC]); rowp = A.buf("rowp", [128, 16])
    xg = A.buf("xg", [128, 8, G]); stg = [A.buf("stg0", [128, 2048]), A.buf("stg1", [128, 2048])]
    k.dma(identf[:], ident_d[:], identf, ident_d); k.dma(rotm[:], rotm_d[:], rotm, rotm_d)
    k.dma(slt[:], slt_d[:], slt, slt_d); k.dma(umat[:], umat_d[:], umat, umat_d); k.dma(negm[:], negm_d[:], negm, negm_d)
    k.dma(maskf[:], amask_d[:].rearrange("a p c -> p a c"), maskf, amask_d)
    k.dma(iota[:], iota_d[:], iota, iota_d); k.dma(lnf[:], lnf_d[:], lnf, lnf_d)
    k.op('dve', lambda e: e.tensor_copy(out=identb[:], in_=identf[:]), rd=[identf], wr=[identb])
    k.op('dve', lambda e: e.tensor_copy(out=maskb[:], in_=maskf[:]), rd=[maskf], wr=[maskb])
    k.op('pool', lambda e: e.memset(onesb[:], 1.0), wr=[onesb])
    k.op('pool', lambda e: e.memset(onesf[:], 1.0), wr=[onesf])
    pG = [k.ps("pG%d" % i, [128, 512]) for i in range(3)]
    pS = [k.ps("pS%d" % i, [128, 2, 256]) for i in range(2)]
    pT = k.ps("pT", [128, 8, 128], BF16)
    pO = k.ps("pO", [128, 512])
    pX = k.ps("pX", [128, 512])
    pgi = [0]
    pM = pG[2]
    def run_chains(gens):
        gens = list(gens)
        while gens:
            for nm, g_ in list(gens):
                cur_chain[0] = nm
                try:
                    next(g_)
                except StopIteration:
                    gens.remove((nm, g_))
        cur_chain[0] = None
    cur_chain = [None]; rotP = [0]; rotW = [0]
    def nextpg():
        if cur_chain[0] == 'B': return pG[1]
        if cur_chain[0] == 'C': return pG[0]
        if cur_chain[0] == 'P':
            rotP[0] = (rotP[0] + 1) % 3
            return [pG[0], pM, pO][rotP[0]]
        if cur_chain[0] == 'W':
            rotW[0] = (rotW[0] + 1) % 2
            return [pG[1], pX][rotW[0]]
        pgi[0] = (pgi[0] + 1) % 2
        return pG[pgi[0]]
    mm = lambda out_ap, lhsT, rhs, rd, wr, start=True, stop=True: k.op('pe', lambda e: e.matmul(out_ap, lhsT=lhsT, rhs=rhs, start=start, stop=stop), rd=rd, wr=wr, sig=stop)
    tr = lambda out_ap, in_ap, idn, rd, wr: k.op('pe', lambda e: e.transpose(out_ap, in_ap, idn[:]), rd=rd + [idn], wr=wr)

    def rmsnorm_T(src, nt, gt, width, dst16, tmp16, rstd, dst_ap=None):
        k.op('act', lambda e: e.activation(out=tmp16[:, 0:nt, 0:gt], in_=src[:, 0:nt, 0:gt], func=AF.Square), rd=[src], wr=[tmp16])
        pg = nextpg()
        for i in range(nt):
            mm(pg[:, 0:gt], onesb[:], tmp16[:, i, 0:gt], [onesb, tmp16], [pg], start=(i == 0), stop=(i == nt - 1))
        k.op('act', lambda e: e.activation(out=rstd[:, 0:gt], in_=pg[:, 0:gt], func=AF.Ln, scale=1.0 / width, bias=epsc[:, 0:1]), rd=[pg, epsc], wr=[rstd])
        k.op('act', lambda e: e.activation(out=rstd[:, 0:gt], in_=rstd[:, 0:gt], func=AF.Exp, scale=-0.5), rd=[rstd], wr=[rstd])
        oap = dst16[:, 0:nt, 0:gt] if dst_ap is None else dst_ap
        k.op('dve', lambda e: e.tensor_tensor(out=oap, in0=src[:, 0:nt, 0:gt], in1=rstd[:, 0:gt].unsqueeze(1).to_broadcast([128, nt, gt]), op=ALU.mult), rd=[src, rstd], wr=[dst16])

    epsc = A.buf("epsc", [128, 1])
    k.op('pool', lambda e: e.memset(epsc[:], EPS), wr=[epsc])
    hT16 = A.buf("hT16", [128, 8, G], BF16); rstd = A.buf("rstd", [128, G])
    xg2 = A.buf("xg2", [128, 8, G]); xgs = [xg, xg2]

    A.mark()
    xtok = A.buf("xtok", [128, 1024]); xtr = A.buf("xtr", [128, 8, 128])
    for b in range(NB):
        k.dma(xtok[:], xin[b * 128:(b + 1) * 128, :], xtok, xin)
        for half in range(2):
            pg = nextpg()
            for j in range(4):
                tr(pg[:, j * 128:(j + 1) * 128], xtok[:, (half * 4 + j) * 128:(half * 4 + j + 1) * 128], identf, [xtok], [pg])
            k.op('dve' if half == 0 else 'act',
                 (lambda e, pg=pg, half=half: e.tensor_copy(out=xtr[:, half * 4:half * 4 + 4, :], in_=pg[:, :].rearrange("p (a b) -> p a b", a=4))) if half == 0 else
                 (lambda e, pg=pg, half=half: e.activation(out=xtr[:, half * 4:half * 4 + 4, :], in_=pg[:, :].rearrange("p (a b) -> p a b", a=4), func=AF.Copy)),
                 rd=[pg], wr=[xtr])
        k.dma(xres[:, :, b * 128:(b + 1) * 128], xtr[:], xres, xtr)
    if sample:
        k.dma(xtok[:], xs_in[:], xtok, xs_in)
        for half in range(2):
            pg = nextpg()
            for j in range(4):
                tr(pg[:, j * 128:(j + 1) * 128], xtok[:, (half * 4 + j) * 128:(half * 4 + j + 1) * 128], identf, [xtok], [pg])
            k.op('dve', lambda e, pg=pg, half=half: e.tensor_copy(out=xtr[:, half * 4:half * 4 + 4, :], in_=pg[:, :].rearrange("p (a b) -> p a b", a=4)), rd=[pg], wr=[xtr])
        k.dma(xres_s[:], xtr[:], xres_s, xtr)
    A.release()
    chk('p0')

    def load_w(dst16, src_ap_fn, nchunks, width, gain_col_fn=None):
        npiece = (width + 2047) // 2048
        pw = width // npiece
        assert pw * npiece == width
        n = 0
        for c in range(nchunks):
            for pc in range(npiece):
                st = stg[n % 2]
                k.dma(st[:, 0:pw], src_ap_fn(c)[:, pc * pw:(pc + 1) * pw], st, w_any)
                use_act = (n % 2 == 1)
                n += 1
                if use_act:
                    if gain_col_fn is None:
                        k.op('act', lambda e, st=st, c=c, pc=pc: e.activation(out=dst16[:, c, pc * pw:(pc + 1) * pw], in_=st[:, 0:pw], func=AF.Copy), rd=[st], wr=[dst16])
                    else:
                        k.op('act', lambda e, st=st, c=c, pc=pc: e.activation(out=dst16[:, c, pc * pw:(pc + 1) * pw], in_=st[:, 0:pw], func=AF.Copy, scale=gain_col_fn(c)), rd=[st, colp], wr=[dst16])
                elif gain_col_fn is None:
                    k.op('dve', lambda e, st=st, c=c, pc=pc: e.tensor_copy(out=dst16[:, c, pc * pw:(pc + 1) * pw], in_=st[:, 0:pw]), rd=[st], wr=[dst16])
                else:
                    k.op('dve', lambda e, st=st, c=c, pc=pc: e.tensor_scalar(out=dst16[:, c, pc * pw:(pc + 1) * pw], in0=st[:, 0:pw], scalar1=gain_col_fn(c), scalar2=None, op0=ALU.mult), rd=[st, colp], wr=[dst16])
    w_any = Buf(k, "w_any", None); w_any.is_dram = True

    for l in range(L):
        barrier(k)
        A.mark()
        Win16 = A.buf("Win16", [128, 8, NCW], BF16); Wout16 = A.buf("Wout16", [128, 8, 1024], BF16)
        glu16 = A.buf("glu16", [128, 2, 256], BF16); Bre16 = A.buf("Bre16", [128, 2, 1024], BF16); Bim16 = A.buf("Bim16", [128, 2, 1024], BF16)
        CRe = A.buf("CRe", [128, 8, 128], BF16); nCRe = A.buf("nCRe", [128, 8, 128], BF16); nCIm = A.buf("nCIm", [128, 8, 128], BF16)
        k.dma(colp[:], colp_d[l], colp, colp_d)
        k.dma(rowp[:], rowp_d[l, 0:1, :].to_broadcast([128, 16]), rowp, rowp_d)
        load_w(Win16, lambda c: w_in_d[l, c * 128:(c + 1) * 128, :], 8, NCW, lambda c: colp[:, c:c + 1])
        load_w(Wout16, lambda c: w_out_d[l, c * 128:(c + 1) * 128, :], 8, 1024, lambda c: colp[:, 8 + c:9 + c])
        load_w(glu16, lambda c: glu_d[l, c * 128:(c + 1) * 128, :], 2, 256)
        load_w(Bre16, lambda c: bre_d[l, c * 128:(c + 1) * 128, :], 2, 1024)
        load_w(Bim16, lambda c: bim_d[l, c * 128:(c + 1) * 128, :], 2, 1024)
        chk('w')
        sp = lambda n, w=8: A.buf(n, [128, w])
        dtv = sp("dtv"); rho = sp("rho"); turns = sp("turns"); ti = A.buf("ti", [128, 8], I32); tf = sp("tf")
        sn = sp("sn"); cs_ = sp("cs_"); abr = sp("abr"); abi = sp("abi"); den = sp("den"); fre = sp("fre"); fim = sp("fim"); t8a = sp("t8a"); t8b = sp("t8b")
        aRe = colp[:, 24:32]; aIm = colp[:, 32:40]; lgdt = colp[:, 40:48]
        k.op('act', lambda e: e.activation(out=dtv[:], in_=lgdt, func=AF.Exp), rd=[colp], wr=[dtv])
        k.op('dve', lambda e: e.tensor_tensor(out=rho[:], in0=dtv[:], in1=aRe, op=ALU.mult), rd=[dtv, colp], wr=[rho])
        k.op('act', lambda e: e.activation(out=rho[:], in_=rho[:], func=AF.Exp), rd=[rho], wr=[rho])
        k.op('dve', lambda e: e.tensor_tensor(out=turns[:], in0=dtv[:], in1=aIm, op=ALU.mult), rd=[dtv, colp], wr=[turns])
        k.op('dve', lambda e: e.tensor_scalar(out=turns[:], in0=turns[:], scalar1=1.0 / (2 * np.pi), scalar2=None, op0=ALU.mult), rd=[turns], wr=[turns])
        k.op('dve', lambda e: e.tensor_copy(out=ti[:], in_=turns[:]), rd=[turns], wr=[ti])
        k.op('dve', lambda e: e.tensor_copy(out=tf[:], in_=ti[:]), rd=[ti], wr=[tf])
        thr = sp("thr")
        k.op('dve', lambda e: e.tensor_tensor(out=thr[:], in0=turns[:], in1=tf[:], op=ALU.subtract), rd=[turns, tf], wr=[thr])
        k.op('dve', lambda e: e.tensor_copy(out=turns[:], in_=thr[:]), rd=[thr], wr=[turns])
        sincos(k, turns, sn, cs_, ti, tf, None)
        k.op('dve', lambda e: e.tensor_tensor(out=abr[:], in0=rho[:], in1=cs_[:], op=ALU.mult), rd=[rho, cs_], wr=[abr])
        k.op('dve', lambda e: e.tensor_tensor(out=abi[:], in0=rho[:], in1=sn[:], op=ALU.mult), rd=[rho, sn], wr=[abi])
        k.op('dve', lambda e: e.tensor_tensor(out=den[:], in0=aRe, in1=aRe, op=ALU.mult), rd=[colp], wr=[den])
        k.op('dve', lambda e: e.tensor_tensor(out=t8a[:], in0=aIm, in1=aIm, op=ALU.mult), rd=[colp], wr=[t8a])
        k.op('dve', lambda e: e.tensor_tensor(out=den[:], in0=den[:], in1=t8a[:], op=ALU.add), rd=[den, t8a], wr=[den])
        k.op('dve', lambda e: e.reciprocal(out=den[:], in_=den[:]), rd=[den], wr=[den])
        xr = sp("xr")
        k.op('dve', lambda e: e.tensor_scalar(out=xr[:], in0=abr[:], scalar1=-1.0, scalar2=None, op0=ALU.add), rd=[abr], wr=[xr])
        k.op('dve', lambda e: e.tensor_tensor(out=t8a[:], in0=xr[:], in1=aRe, op=ALU.mult), rd=[xr, colp], wr=[t8a])
        k.op('dve', lambda e: e.tensor_tensor(out=t8b[:], in0=abi[:], in1=aIm, op=ALU.mult), rd=[abi, colp], wr=[t8b])
        k.op('dve', lambda e: e.tensor_tensor(out=t8a[:], in0=t8a[:], in1=t8b[:], op=ALU.add), rd=[t8a, t8b], wr=[t8a])
        k.op('dve', lambda e: e.tensor_tensor(out=fre[:], in0=t8a[:], in1=den[:], op=ALU.mult), rd=[t8a, den], wr=[fre])
        k.op('dve', lambda e: e.tensor_tensor(out=t8a[:], in0=abi[:], in1=aRe, op=ALU.mult), rd=[abi, colp], wr=[t8a])
        k.op('dve', lambda e: e.tensor_tensor(out=t8b[:], in0=xr[:], in1=aIm, op=ALU.mult), rd=[xr, colp], wr=[t8b])
        k.op('dve', lambda e: e.tensor_tensor(out=t8a[:], in0=t8a[:], in1=t8b[:], op=ALU.subtract), rd=[t8a, t8b], wr=[t8a])
        k.op('dve', lambda e: e.tensor_tensor(out=fim[:], in0=t8a[:], in1=den[:], op=ALU.mult), rd=[t8a, den], wr=[fim])
        finv_re = sp("finv_re"); finv_im = sp("finv_im"); nfinv_im = sp("nfinv_im"); nabi = sp("nabi"); nfim = sp("nfim")
        k.op('dve', lambda e: e.tensor_tensor(out=t8a[:], in0=fre[:], in1=fre[:], op=ALU.mult), rd=[fre], wr=[t8a])
        k.op('dve', lambda e: e.tensor_tensor(out=t8b[:], in0=fim[:], in1=fim[:], op=ALU.mult), rd=[fim], wr=[t8b])
        k.op('dve', lambda e: e.tensor_tensor(out=t8a[:], in0=t8a[:], in1=t8b[:], op=ALU.add), rd=[t8a, t8b], wr=[t8a])
        k.op('dve', lambda e: e.reciprocal(out=t8a[:], in_=t8a[:]), rd=[t8a], wr=[t8a])
        k.op('dve', lambda e: e.tensor_tensor(out=finv_re[:], in0=fre[:], in1=t8a[:], op=ALU.mult), rd=[fre, t8a], wr=[finv_re])
        k.op('dve', lambda e: e.tensor_tensor(out=nfinv_im[:], in0=fim[:], in1=t8a[:], op=ALU.mult), rd=[fim, t8a], wr=[nfinv_im])
        k.op('dve', lambda e: e.tensor_scalar(out=finv_im[:], in0=nfinv_im[:], scalar1=-1.0, scalar2=None, op0=ALU.mult), rd=[nfinv_im], wr=[finv_im])
        k.op('dve', lambda e: e.tensor_scalar(out=nabi[:], in0=abi[:], scalar1=-1.0, scalar2=None, op0=ALU.mult), rd=[abi], wr=[nabi])
        k.op('dve', lambda e: e.tensor_scalar(out=nfim[:], in0=fim[:], scalar1=-1.0, scalar2=None, op0=ALU.mult), rd=[fim], wr=[nfim])
        cosT = A.buf("cosT", [128, 8, G]); sinT = A.buf("sinT", [128, 8, G])
        cG = sp("cG"); sG = sp("sG"); tg = sp("tg")
        negsink = A.buf("negsink", [128, 8]); Atab = A.buf("Atab", [128, 4])
        A.mark()
        cst_re = A.buf("cst_re", [128, 8, 128]); cst_im = A.buf("cst_im", [128, 8, 128]); ctmp = A.buf("ctmp", [128, 8, 128]); ctmp2 = A.buf("ctmp2", [128, 8, 128])
        k.dma(cst_re[:], cre_d[l], cst_re, cre_d); k.dma(cst_im[:], cim_d[l], cst_im, cim_d)
        bc = lambda b_: b_[:].unsqueeze(2).to_broadcast([128, 8, 128])
        k.op('dve', lambda e: e.tensor_tensor(out=ctmp[:], in0=cst_re[:], in1=bc(fre), op=ALU.mult), rd=[cst_re, fre], wr=[ctmp])
        k.op('pool', lambda e: e.tensor_tensor(out=ctmp2[:], in0=cst_im[:], in1=bc(fim), op=ALU.mult), rd=[cst_im, fim], wr=[ctmp2])
        k.op('dve', lambda e: e.tensor_tensor(out=CRe[:], in0=ctmp[:], in1=ctmp2[:], op=ALU.subtract), rd=[ctmp, ctmp2], wr=[CRe])
        k.op('dve', lambda e: e.tensor_tensor(out=nCRe[:], in0=ctmp2[:], in1=ctmp[:], op=ALU.subtract), rd=[ctmp, ctmp2], wr=[nCRe])
        k.op('dve', lambda e: e.tensor_tensor(out=ctmp[:], in0=cst_re[:], in1=bc(fim), op=ALU.mult), rd=[cst_re, fim], wr=[ctmp])
        k.op('pool', lambda e: e.tensor_tensor(out=ctmp2[:], in0=cst_im[:], in1=bc(fre), op=ALU.mult), rd=[cst_im, fre], wr=[ctmp2])
        k.op('dve', lambda e: e.tensor_tensor(out=ctmp[:], in0=ctmp[:], in1=ctmp2[:], op=ALU.add), rd=[ctmp, ctmp2], wr=[ctmp])
        k.op('dve', lambda e: e.tensor_scalar(out=nCIm[:], in0=ctmp[:], scalar1=-1.0, scalar2=None, op0=ALU.mult), rd=[ctmp], wr=[nCIm])
        tb_t = A.buf("tb_t", [128, G]); tb_i = A.buf("tb_i", [128, G], I32); tb_f = A.buf("tb_f", [128, G])
        tbs = A.buf("tbs", [128, G]); tbc = A.buf("tbc", [128, G])
        for j in range(8):
            k.op('dve', lambda e, j=j: e.tensor_scalar(out=tb_t[:], in0=iota[:], scalar1=thr[:, j:j + 1], scalar2=None, op0=ALU.mult), rd=[iota, thr], wr=[tb_t])
            sincos(k, tb_t, tbs, tbc, tb_i, tb_f, None)
            k.op('pool', lambda e, j=j: e.tensor_copy(out=sinT[:, j, :], in_=tbs[:]), rd=[tbs], wr=[sinT])
            k.op('pool', lambda e, j=j: e.tensor_copy(out=cosT[:, j, :], in_=tbc[:]), rd=[tbc], wr=[cosT])
        k.op('dve', lambda e: e.tensor_scalar(out=tg[:], in0=thr[:], scalar1=float(G), scalar2=None, op0=ALU.mult), rd=[thr], wr=[tg])
        sincos(k, tg, sG, cG, ti, tf, None)
        k.op('dve', lambda e: e.tensor_scalar(out=negsink[:], in0=rowp[:, 0:8], scalar1=-1.0, scalar2=None, op0=ALU.mult), rd=[rowp], wr=[negsink])
        k.op('act', lambda e: e.activation(out=Atab[:], in_=rowp[:, 8:12], func=AF.Exp), rd=[rowp], wr=[Atab])
        k.op('dve', lambda e: e.tensor_scalar(out=Atab[:], in0=Atab[:], scalar1=-1.0, scalar2=None, op0=ALU.mult), rd=[Atab], wr=[Atab])

        A.release(); barrier(k)
        chk('prep')
        qk32 = A.buf("qk32", [128, 6, G]); qT16 = A.buf("qT16", [128, 4, G], BF16)
        kdT = [A.buf("kdT%d" % i, [128, 128 + G], BF16) for i in range(2)]; rk32 = A.buf("rk32", [128, 2, G])
        Vtok = A.buf("Vtok", [128, 1 + G // 128, 128], BF16); vT32 = A.buf("vT32", [128, G])
        u32 = A.buf("u32", [128, 2, G]); u16 = A.buf("u16", [128, 2, G], BF16); sz = A.buf("sz", [128, 2, G])
        xpre = A.buf("xpre", [128, 4, 8 + G]); xc32 = A.buf("xc32", [128, 4, G]); xc16 = A.buf("xc16", [128, 4, G], BF16)
        dtT = A.buf("dtT", [4, G]); ropc = A.buf("ropc", [128, G]); rops = A.buf("rops", [128, G]); rt1 = A.buf("rt1", [128, G])
        mixT = A.buf("mixT", [128, 8, G], BF16)
        bur = A.buf("bur", [128, G]); bui = A.buf("bui", [128, G]); wr_ = A.buf("wr_", [128, G]); wi_ = A.buf("wi_", [128, G])
        s5a = A.buf("s5a", [128, G]); s5b = A.buf("s5b", [128, G]); gr = A.buf("gr", [128, G]); gi_ = A.buf("gi_", [128, G])
        prodf = A.buf("prodf", [128, 2 * 4 * G])
        prod = prodf.t.bitcast(BF16).rearrange("p (a b c) -> p a b c", a=4, b=4)
        sq16 = Buf(k, "sq16m", prodf.t.bitcast(BF16)[:, 0:8 * G].rearrange("p (a b) -> p a b", a=8)); sq16.lw = prodf.lw; sq16.rd = prodf.rd
        gend_r = sp("gend_r"); gend_i = sp("gend_i"); gin_r = sp("gin_r"); gin_i = sp("gin_i")
        ys = A.buf("ys", [128, 2, G]); y5 = A.buf("y5", [128, 2, G]); y5h = A.buf("y5h", [128, 2, G], BF16); sg = A.buf("sg", [128, 2, G])
        P16 = [A.buf("P16_%d" % i, [128, 2, 256], BF16) for i in range(2)]; PTs = [A.buf("PTs%d" % i, [128, 4, 128], BF16) for i in range(2)]
        mx = A.buf("mx", [128, 8]); ngm = A.buf("ngm", [128, 8]); rs = A.buf("rs", [128, 8]); es = A.buf("es", [128, 8]); rden = A.buf("rden", [128, 8])
        On = A.buf("On", [128, 8, 64]); junk = A.buf("junk", [128, 512]); ss1 = A.buf("ss1", [128, 1]); o16 = A.buf("o16", [128, 512], BF16)
        dtk = A.buf("dtk", [128, 4]); atok = A.buf("atok", [128, 4]); xd16 = A.buf("xd16", [128, 4, 64], BF16); Btk = A.buf("Btk", [128, 128])
        LA = A.buf("LA", [128, 4, 128]); LT = A.buf("LT", [128, 4, 128]); ecs = A.buf("ecs", [128, 8]); SL16 = A.buf("SL16", [128, 4, 128], BF16)
        Bd16 = A.buf("Bd16", [128, 4, 128], BF16); Yd = A.buf("Yd", [128, 256]); Yt = A.buf("Yt", [128, 4, 64]); ycg = A.buf("ycg", [128, 2, 128])
        hT32 = A.buf("hT32", [128, 4, 64]); hTh = A.buf("hTh", [128, 4, 64], BF16); htmp = A.buf("htmp", [128, 4, 64])
        sq2 = A.buf("sq2", [128, 2, 128], BF16); rstd2 = A.buf("rstd2", [128, 128])
        k.op('pool', lambda e: e.memset(hT32[:], 0.0), wr=[hT32]); k.op('pool', lambda e: e.memset(hTh[:], 0.0), wr=[hTh])
        k.op('pool', lambda e: e.memset(gin_r[:], 0.0), wr=[gin_r]); k.op('pool', lambda e: e.memset(gin_i[:], 0.0), wr=[gin_i])
        k.op('pool', lambda e: e.memset(xpre[:], 0.0), wr=[xpre])
        for i in range(2): k.op('pool', lambda e, i=i: e.memset(kdT[i][:], 0.0), wr=[kdT[i]])
        k.op('pool', lambda e: e.memset(Vtok[:], 0.0), wr=[Vtok])

        def do_norm(gt, xb):
            rmsnorm_T(xb, 8, gt, 1024.0, hT16, sq16, rstd)
        def do_proj(gt, nvalid):
            def proj(ot, rows=128):
                pg = nextpg()
                for kt in range(8):
                    mm(pg[0:rows, 0:gt], Win16[:, kt, ot * 128:ot * 128 + rows], hT16[:, kt, 0:gt], [Win16, hT16], [pg], start=(kt == 0), stop=(kt == 7))
                return pg
            for i in range(6):
                pg = proj(i)
                k.op('act' if i % 2 else 'dve', (lambda e, pg=pg, i=i: e.activation(out=qk32[:, i, 0:gt], in_=pg[:, 0:gt], func=AF.Copy)) if i % 2 else
                     (lambda e, pg=pg, i=i: e.tensor_copy(out=qk32[:, i, 0:gt], in_=pg[:, 0:gt])), rd=[pg], wr=[qk32])
            chk('qk')
            pg = proj(6)
            k.op('act', lambda e, pg=pg: e.activation(out=vT32[:, 0:gt], in_=pg[:, 0:gt], func=AF.Copy), rd=[pg], wr=[vT32])
            yield
            chk('v')
            for m in range(2):
                pg = proj(7 + m)
                k.op('dve', lambda e, pg=pg, m=m: e.tensor_copy(out=u32[:, m, 0:gt], in_=pg[:, 0:gt]), rd=[pg], wr=[u32])
                yield
                k.op('act', lambda e, m=m: e.activation(out=u16[:, m, 0:gt], in_=u32[:, m, 0:gt], func=AF.Copy), rd=[u32], wr=[u16])
                yield
            chk('u')
            for m in range(2):
                pg = proj(9 + m)
                k.op('act', lambda e, pg=pg, m=m: e.activation(out=sz[:, m, 0:gt], in_=pg[:, 0:gt], func=AF.Silu), rd=[pg], wr=[sz])
                yield
            chk('z')
            for j in range(4):
                pg = proj(11 + j)
                k.op('dve', lambda e, pg=pg, j=j: e.tensor_copy(out=xpre[:, j, 8:8 + gt], in_=pg[:, 0:gt]), rd=[pg], wr=[xpre])
                yield
            chk('x')
            pg = proj(15, rows=4)
            k.op('act', lambda e, pg=pg: e.activation(out=dtT[:, 0:gt], in_=pg[0:4, 0:gt], func=AF.Exp, bias=colp[0:4, 74:75]), rd=[pg, colp], wr=[dtT])
            yield
            k.op('act', lambda e: e.activation(out=dtT[:, 0:gt], in_=dtT[:, 0:gt], func=AF.Ln, bias=onesf[0:4, 0:1]), rd=[dtT, onesf], wr=[dtT])
            yield
            if nvalid < gt:
                k.op('dve', lambda e: e.memset(dtT[:, nvalid:gt], 0.0), wr=[dtT])
                yield
        def do_rope(gt, sample=False):
            for i in range(6):
                pg = nextpg()
                mm(pg[:, 0:gt], rotm[:], qk32[:, i, 0:gt], [rotm, qk32], [pg])
                k.op('pool', lambda e, i=i: e.tensor_tensor(out=rt1[:, 0:gt], in0=qk32[:, i, 0:gt], in1=ropc[:, 0:gt], op=ALU.mult), rd=[qk32, ropc], wr=[rt1])
                yield
                k.op('dve', lambda e, pg=pg: e.tensor_tensor(out=junk[:, 0:gt], in0=pg[:, 0:gt], in1=rops[:, 0:gt], op=ALU.mult), rd=[pg, rops], wr=[junk])
                yield
                if i < 4 and sample:
                    k.op('pool', lambda e, i=i: e.tensor_tensor(out=qk32[:, i, 0:gt], in0=rt1[:, 0:gt], in1=junk[:, 0:gt], op=ALU.add), rd=[rt1, junk], wr=[qk32])
                    yield
                elif i < 4:
                    k.op('pool', lambda e, i=i: e.tensor_tensor(out=qT16[:, i, 0:gt], in0=rt1[:, 0:gt], in1=junk[:, 0:gt], op=ALU.add), rd=[rt1, junk], wr=[qT16])
                    yield
                else:
                    k.op('pool', lambda e, i=i: e.tensor_tensor(out=rk32[:, i - 4, 0:gt], in0=rt1[:, 0:gt], in1=junk[:, 0:gt], op=ALU.add), rd=[rt1, junk], wr=[rk32])
                    yield
                    k.op('act', lambda e, i=i: e.activation(out=kdT[i - 4][:, 128:128 + gt], in_=rk32[:, i - 4, 0:gt], func=AF.Copy), rd=[rk32], wr=[kdT[i - 4]])
                    yield
        def attn_den():
            k.op('dve', lambda e: e.tensor_tensor(out=es[:], in0=ngm[:], in1=negsink[:], op=ALU.subtract), rd=[ngm, negsink], wr=[es])
            k.op('act', lambda e: e.activation(out=es[:], in_=es[:], func=AF.Exp), rd=[es], wr=[es])
            k.op('dve', lambda e: e.tensor_tensor(out=rden[:], in0=rs[:], in1=es[:], op=ALU.add), rd=[rs, es], wr=[rden])
            k.op('dve', lambda e: e.reciprocal(out=rden[:], in_=rden[:]), rd=[rden], wr=[rden])
        def attn_tail(c0):
            k.op('pool', lambda e: e.memset(ss1[:], 0.0), wr=[ss1])
            k.op('act', lambda e: e.activation(out=junk[:, 0:512], in_=On[:].rearrange("p h d -> p (h d)"), func=AF.Square, accum_out=ss1[:, 0:1]), rd=[On], wr=[junk, ss1])
            k.op('act', lambda e: e.activation(out=ss1[:], in_=ss1[:], func=AF.Ln, scale=1.0 / 512, bias=epsc[:, 0:1]), rd=[ss1, epsc], wr=[ss1])
            k.op('act', lambda e: e.activation(out=ss1[:], in_=ss1[:], func=AF.Exp, scale=-0.5), rd=[ss1], wr=[ss1])
            k.op('dve', lambda e: e.tensor_scalar(out=o16[:], in0=On[:].rearrange("p h d -> p (h d)"), scalar1=ss1[:, 0:1], scalar2=None, op0=ALU.mult), rd=[On, ss1], wr=[o16])
            for c in range(4):
                tr(pT[:, c, :], o16[:, c * 128:(c + 1) * 128], identb, [o16], [pT])
            k.op('act', lambda e, c0=c0: e.activation(out=mixT[:, 0:4, c0:c0 + 128], in_=pT[:, 0:4, :], func=AF.Copy), rd=[pT], wr=[mixT])
        def s5_epi(gt):
            k.op('act', lambda e: e.activation(out=y5[:, :, 0:gt], in_=ys[:, :, 0:gt], func=AF.Gelu_apprx_tanh), rd=[ys], wr=[y5])
            k.op('pool', lambda e: e.tensor_copy(out=y5h[:, :, 0:gt], in_=y5[:, :, 0:gt]), rd=[y5], wr=[y5h])
            for m2 in range(2):
                pg = nextpg()
                for m in range(2):
                    mm(pg[:, 0:gt], glu16[:, m, m2 * 128:(m2 + 1) * 128], y5h[:, m, 0:gt], [glu16, y5h], [pg], start=(m == 0), stop=(m == 1))
                k.op('act', lambda e, pg=pg, m2=m2: e.activation(out=sg[:, m2, 0:gt], in_=pg[:, 0:gt], func=AF.Sigmoid, bias=colp[:, 50 + m2:51 + m2]), rd=[pg, colp], wr=[sg])
            k.op('pool', lambda e: e.tensor_tensor(out=y5[:, :, 0:gt], in0=y5[:, :, 0:gt], in1=sg[:, :, 0:gt], op=ALU.mult), rd=[y5, sg], wr=[y5])
            rmsnorm_T(y5, 2, gt, 256.0, mixT, sq16, rstd, dst_ap=mixT[:, 4:6, 0:gt])
        def ssd_epi(c0):
            pg2 = nextpg()
            for m in range(2):
                tr(pg2[:, m * 128:(m + 1) * 128], Yt[:, 2 * m:2 * m + 2, :].rearrange("p h d -> p (h d)"), identf, [Yt], [pg2])
            for m in range(2):
                k.op('dve', lambda e, m=m, pg2=pg2, c0=c0: e.scalar_tensor_tensor(out=ycg[:, m, :], in0=xc32[:, m, c0:c0 + 128], scalar=colp[:, 72 + m:73 + m], in1=pg2[:, m * 128:(m + 1) * 128], op0=ALU.mult, op1=ALU.add), rd=[xc32, colp, pg2], wr=[ycg])
            k.op('pool', lambda e, c0=c0: e.tensor_tensor(out=ycg[:], in0=ycg[:], in1=sz[:, :, c0:c0 + 128], op=ALU.mult), rd=[ycg, sz], wr=[ycg])
            k.op('act', lambda e: e.activation(out=sq2[:], in_=ycg[:], func=AF.Square), rd=[ycg], wr=[sq2])
            pg3 = nextpg()
            for m in range(2):
                mm(pg3[:, 0:128], onesb[:], sq2[:, m, :], [onesb, sq2], [pg3], start=(m == 0), stop=(m == 1))
            k.op('act', lambda e, pg3=pg3: e.activation(out=rstd2[:], in_=pg3[:, 0:128], func=AF.Ln, scale=1.0 / 256, bias=epsc[:, 0:1]), rd=[pg3, epsc], wr=[rstd2])
            k.op('act', lambda e: e.activation(out=rstd2[:], in_=rstd2[:], func=AF.Exp, scale=-0.5), rd=[rstd2], wr=[rstd2])
            k.op('dve', lambda e, c0=c0: e.tensor_tensor(out=mixT[:, 6:8, c0:c0 + 128], in0=ycg[:], in1=rstd2[:].unsqueeze(1).to_broadcast([128, 2, 128]), op=ALU.mult), rd=[ycg, rstd2], wr=[mixT])
        def wout(gt, xb=None):
            xb = xg if xb is None else xb
            for dt_ in range(8):
                pg = nextpg()
                for kt in range(8):
                    mm(pg[:, 0:gt], Wout16[:, kt, dt_ * 128:(dt_ + 1) * 128], mixT[:, kt, 0:gt], [Wout16, mixT], [pg], start=(kt == 0), stop=(kt == 7))
                k.op('dve', lambda e, pg=pg, dt_=dt_: e.tensor_tensor(out=xb[:, dt_, 0:gt], in0=xb[:, dt_, 0:gt], in1=pg[:, 0:gt], op=ALU.add), rd=[xb, pg], wr=[xb])
                yield

        def proj_chain(gidx_):
            t0, gt = groups[gidx_]; nblk = gt // 128; nvalid = min(gt, T - t0)
            if gidx_ > 0:
                do_norm(gt, xgs[gidx_ % 2])
                yield
            k.dma(ropc[:, 0:gt], ropec_d[:, t0:t0 + gt], ropc, ropec_d); k.dma(rops[:, 0:gt], ropes_d[:, t0:t0 + gt], rops, ropes_d)
            for _ in do_proj(gt, nvalid): yield
            for _ in do_rope(gt): yield
            k.dma(kscr[0:64, t0:t0 + gt], rk32[0:64, 0, 0:gt], kscr, rk32); k.dma(kscr[64:128, t0:t0 + gt], rk32[0:64, 1, 0:gt], kscr, rk32)
            k.dma(vscr[:, t0:t0 + gt], vT32[:, 0:gt], vscr, vT32)
            for bi in range(nblk):
                pg = nextpg()
                tr(pg[:, 0:128], vT32[:, bi * 128:(bi + 1) * 128], identf, [vT32], [pg])
                k.op('dve', lambda e, pg=pg, bi=bi: e.tensor_copy(out=Vtok[:, 1 + bi, :], in_=pg[:, 0:128]), rd=[pg], wr=[Vtok])
            yield
        def wout_chain(gidx_):
            t0, gt = groups[gidx_]; xb = xgs[gidx_ % 2]
            for _ in wout(gt, xb):
                yield
            k.dma(xres[:, :, t0:t0 + gt], xb[:, :, 0:gt], xres, xb)
            yield
        def load_g(gidx_):
            t0_, gt_ = groups[gidx_]; xb_ = xgs[gidx_ % 2]
            k.dma(xb_[:, :, 0:gt_], xres[:, :, t0_:t0_ + gt_], xb_, xres)
        load_g(0)
        do_norm(groups[0][1], xgs[0])
        if len(groups) > 1: load_g(1)
        run_chains([('P', proj_chain(0))])
        for gidx, (t0, gt) in enumerate(groups):
            nblk = gt // 128
            nvalid = min(gt, T - t0)
            xb = xgs[gidx % 2]
            chk('rope')
            def chain_A():
                for bi in range(nblk):
                    c0 = bi * 128
                    first = 1 if (gidx == 0 and bi == 0) else 0
                    k.op('pool', lambda e: e.memset(rs[:], 0.0), wr=[rs])
                    yield
                    for hp in range(4):
                        gi = hp // 2; ps = pS[0]; p16 = P16[hp % 2]; pts = PTs[hp % 2]
                        for hh in range(2):
                            r0 = hh * 64
                            mm(ps[:, hh, :], qT16[r0:r0 + 64, hp, c0:c0 + 128], kdT[gi][r0:r0 + 64, c0:c0 + 256], [qT16, kdT[gi]], [ps], start=True, stop=False)
                            mm(ps[:, hh, :], identb[:], maskb[:, first, :], [identb, maskb], [ps], start=False, stop=True)
                        k.op('dve', lambda e, ps=ps, hp=hp: e.tensor_reduce(out=mx[:, 2 * hp:2 * hp + 2], in_=ps[:], axis=AX.X, op=ALU.max), rd=[ps], wr=[mx])
                        yield
                        k.op('dve', lambda e, hp=hp: e.scalar_tensor_tensor(out=ngm[:, 2 * hp:2 * hp + 2], in0=mx[:, 2 * hp:2 * hp + 2], scalar=-0.125, in1=negsink[:, 2 * hp:2 * hp + 2], op0=ALU.mult, op1=ALU.min), rd=[mx, negsink], wr=[ngm])
                        yield
                        for hh in range(2):
                            h = 2 * hp + hh
                            k.op('act', lambda e, ps=ps, p16=p16, hh=hh, h=h: e.activation(out=p16[:, hh, :], in_=ps[:, hh, :], func=AF.Exp, bias=ngm[:, h:h + 1], scale=0.125, accum_out=rs[:, h:h + 1]), rd=[ps, ngm], wr=[p16, rs])
                            yield
                        yield
                        for hh in range(2):
                            for half in range(2):
                                tr(pT[:, (hp % 2) * 4 + hh * 2 + half, :], p16[:, hh, half * 128:(half + 1) * 128], identb, [p16], [pT])
                        k.op('dve', lambda e, hp=hp, pts=pts: e.tensor_copy(out=pts[:], in_=pT[:, (hp % 2) * 4:(hp % 2) * 4 + 4, :]), rd=[pT], wr=[pts])
                        yield
                        yield
                        for hh in range(2):
                            h = 2 * hp + hh
                            for half in range(2):
                                mm(pO[:, h * 64:(h + 1) * 64], pts[:, hh * 2 + half, :], Vtok[:, bi + half, gi * 64:(gi + 1) * 64], [pts, Vtok], [pO], start=(half == 0), stop=(half == 1))
                    yield
                    attn_den()
                    k.op('dve', lambda e: e.tensor_tensor(out=On[:], in0=pO[:, :].rearrange("p (h d) -> p h d", h=8), in1=rden[:].unsqueeze(2).to_broadcast([128, 8, 64]), op=ALU.mult), rd=[pO, rden], wr=[On])
                    yield
                    attn_tail(c0)
                for i in range(2):
                    k.op('pool', lambda e, i=i: e.tensor_copy(out=kdT[i][:, 0:128], in_=kdT[i][:, gt:gt + 128]), rd=[kdT[i]], wr=[kdT[i]])
                    yield
                k.op('pool', lambda e: e.tensor_copy(out=Vtok[:, 0, :], in_=Vtok[:, nblk, :]), rd=[Vtok], wr=[Vtok])
                yield
                yield
            def chain_B():
                for j in range(8):
                    jq = j % 4
                    pg1 = nextpg()
                    mm(pg1[:, 0:gt], Bre16[:, j // 4, j * 128:(j + 1) * 128], u16[:, j // 4, 0:gt], [Bre16, u16], [pg1])
                    mm(pg1[:, 256:256 + gt], Bim16[:, j // 4, j * 128:(j + 1) * 128], u16[:, j // 4, 0:gt], [Bim16, u16], [pg1])
                    k.op('act', lambda e, pg1=pg1: e.activation(out=bur[:, 0:gt], in_=pg1[:, 0:gt], func=AF.Copy), rd=[pg1], wr=[bur])
                    yield
                    k.op('act', lambda e, pg1=pg1: e.activation(out=bui[:, 0:gt], in_=pg1[:, 256:256 + gt], func=AF.Copy), rd=[pg1], wr=[bui])
                    yield
                    cj = cosT[:, j, 0:gt]; sj = sinT[:, j, 0:gt]
                    k.op('dve', lambda e, cj=cj: e.tensor_tensor(out=s5a[:, 0:gt], in0=bur[:, 0:gt], in1=cj, op=ALU.mult), rd=[bur, cosT], wr=[s5a])
                    yield
                    k.op('pool', lambda e, sj=sj: e.tensor_tensor(out=s5b[:, 0:gt], in0=bui[:, 0:gt], in1=sj, op=ALU.mult), rd=[bui, sinT], wr=[s5b])
                    yield
                    k.op('dve', lambda e: e.tensor_tensor(out=wr_[:, 0:gt], in0=s5a[:, 0:gt], in1=s5b[:, 0:gt], op=ALU.add), rd=[s5a, s5b], wr=[wr_])
                    yield
                    yield
                    k.op('dve', lambda e, cj=cj: e.tensor_tensor(out=s5a[:, 0:gt], in0=bui[:, 0:gt], in1=cj, op=ALU.mult), rd=[bui, cosT], wr=[s5a])
                    yield
                    k.op('pool', lambda e, sj=sj: e.tensor_tensor(out=s5b[:, 0:gt], in0=bur[:, 0:gt], in1=sj, op=ALU.mult), rd=[bur, sinT], wr=[s5b])
                    yield
                    k.op('dve', lambda e: e.tensor_tensor(out=wi_[:, 0:gt], in0=s5a[:, 0:gt], in1=s5b[:, 0:gt], op=ALU.subtract), rd=[s5a, s5b], wr=[wi_])
                    yield
                    rb = rho[:, j:j + 1].to_broadcast([128, gt])
                    k.op('dve', lambda e, rb=rb, j=j: e.tensor_tensor_scan(out=gr[:, 0:gt], data0=rb, data1=wr_[:, 0:gt], initial=gin_r[:, j:j + 1], op0=ALU.mult, op1=ALU.add), rd=[rho, wr_, gin_r], wr=[gr])
                    yield
                    k.op('dve', lambda e, rb=rb, j=j: e.tensor_tensor_scan(out=gi_[:, 0:gt], data0=rb, data1=wi_[:, 0:gt], initial=gin_i[:, j:j + 1], op0=ALU.mult, op1=ALU.add), rd=[rho, wi_, gin_i], wr=[gi_])
                    yield
                    yield
                    ec = nvalid - 1 if nvalid < gt else gt - 1
                    k.op('pool', lambda e, j=j, ec=ec: e.tensor_copy(out=gend_r[:, j:j + 1], in_=gr[:, ec:ec + 1]), rd=[gr], wr=[gend_r])
                    yield
                    k.op('pool', lambda e, j=j, ec=ec: e.tensor_copy(out=gend_i[:, j:j + 1], in_=gi_[:, ec:ec + 1]), rd=[gi_], wr=[gend_i])
                    yield
                    k.op('dve', lambda e, j=j, jq=jq, cj=cj: e.tensor_tensor(out=prod[:, jq, 0, 0:gt], in0=gr[:, 0:gt], in1=cj, op=ALU.mult), rd=[gr, cosT], wr=[prodf])
                    yield
                    k.op('pool', lambda e, j=j, jq=jq, sj=sj: e.tensor_tensor(out=prod[:, jq, 1, 0:gt], in0=gi_[:, 0:gt], in1=sj, op=ALU.mult), rd=[gi_, sinT], wr=[prodf])
                    yield
                    k.op('dve', lambda e, j=j, jq=jq, sj=sj: e.tensor_tensor(out=prod[:, jq, 2, 0:gt], in0=gr[:, 0:gt], in1=sj, op=ALU.mult), rd=[gr, sinT], wr=[prodf])
                    yield
                    k.op('pool', lambda e, j=j, jq=jq, cj=cj: e.tensor_tensor(out=prod[:, jq, 3, 0:gt], in0=gi_[:, 0:gt], in1=cj, op=ALU.mult), rd=[gi_, cosT], wr=[prodf])
                    yield
                    yield
                    if jq == 3:
                        m = j // 4
                        pgc = nextpg()
                        n = 0
                        for jj in range(4):
                            for q, Cm in enumerate([CRe, nCRe, nCIm, nCIm]):
                                mm(pgc[:, 0:gt], Cm[:, 4 * m + jj, :], prod[:, jj, q, 0:gt], [Cm, prodf], [pgc], start=(n == 0), stop=(n == 15)); n += 1
                        k.op('dve', lambda e, pgc=pgc, m=m: e.scalar_tensor_tensor(out=ys[:, m, 0:gt], in0=u32[:, m, 0:gt], scalar=colp[:, 48 + m:49 + m], in1=pgc[:, 0:gt], op0=ALU.mult, op1=ALU.add), rd=[u32, colp, pgc], wr=[ys])
                        yield
                k.op('dve', lambda e: e.tensor_tensor(out=t8a[:], in0=cG[:], in1=gend_r[:], op=ALU.mult), rd=[cG, gend_r], wr=[t8a])
                yield
                k.op('dve', lambda e: e.tensor_tensor(out=t8b[:], in0=sG[:], in1=gend_i[:], op=ALU.mult), rd=[sG, gend_i], wr=[t8b])
                yield
                k.op('dve', lambda e: e.tensor_tensor(out=gin_r[:], in0=t8a[:], in1=t8b[:], op=ALU.subtract), rd=[t8a, t8b], wr=[gin_r])
                yield
                k.op('dve', lambda e: e.tensor_tensor(out=t8a[:], in0=sG[:], in1=gend_r[:], op=ALU.mult), rd=[sG, gend_r], wr=[t8a])
                yield
                k.op('dve', lambda e: e.tensor_tensor(out=t8b[:], in0=cG[:], in1=gend_i[:], op=ALU.mult), rd=[cG, gend_i], wr=[t8b])
                yield
                k.op('dve', lambda e: e.tensor_tensor(out=gin_i[:], in0=t8a[:], in1=t8b[:], op=ALU.add), rd=[t8a, t8b], wr=[gin_i])
                yield
                s5_epi(gt)
                yield
            def chain_C():
                for j in range(4):
                    for tap in range(4):
                        wcol = colp[:, 52 + j * 4 + tap:53 + j * 4 + tap]
                        src = xpre[:, j, 5 + tap:5 + tap + gt]
                        if tap == 0:
                            k.op('pool', lambda e, src=src, wcol=wcol, j=j: e.tensor_scalar(out=xc32[:, j, 0:gt], in0=src, scalar1=wcol, scalar2=colp[:, 68 + j:69 + j], op0=ALU.mult, op1=ALU.add), rd=[xpre, colp], wr=[xc32])
                            yield
                        else:
                            k.op('dve', lambda e, src=src, wcol=wcol, j=j: e.scalar_tensor_tensor(out=xc32[:, j, 0:gt], in0=src, scalar=wcol, in1=xc32[:, j, 0:gt], op0=ALU.mult, op1=ALU.add), rd=[xpre, colp, xc32], wr=[xc32])
                            yield
                if nvalid == gt:
                    k.op('pool', lambda e: e.tensor_copy(out=xpre[:, :, 5:8], in_=xpre[:, :, 5 + gt:8 + gt]), rd=[xpre], wr=[xpre])
                    yield
                k.op('act', lambda e: e.activation(out=xc32[:, :, 0:gt], in_=xc32[:, :, 0:gt], func=AF.Silu), rd=[xc32], wr=[xc32])
                yield
                k.op('act', lambda e: e.activation(out=xc16[:, :, 0:gt], in_=xc32[:, :, 0:gt], func=AF.Copy), rd=[xc32], wr=[xc16])
                yield
                for bi in range(nblk):
                    c0 = bi * 128
                    pg = nextpg()
                    for m in range(2):
                        tr(pg[:, m * 128:(m + 1) * 128], xc32[:, m, c0:c0 + 128], identf, [xc32], [pg])
                    tr(pg[:, 256:384], xc32[:, 2, c0:c0 + 128], identf, [xc32], [pg])
                    k.op('pe', lambda e, pg=pg, c0=c0: e.transpose(pg[:, 384:388], dtT[0:4, c0:c0 + 128], identf[0:4, 0:4]), rd=[dtT, identf], wr=[pg])
                    yield
                    k.op('dve', lambda e, pg=pg: e.tensor_copy(out=dtk[:], in_=pg[:, 384:388]), rd=[pg], wr=[dtk])
                    yield
                    k.op('dve', lambda e: e.tensor_tensor(out=atok[:], in0=dtk[:], in1=Atab[:], op=ALU.mult), rd=[dtk, Atab], wr=[atok])
                    yield
                    k.op('dve', lambda e, pg=pg: e.tensor_tensor(out=xd16[:], in0=pg[:, 0:256].rearrange("p (h d) -> p h d", h=4), in1=dtk[:].unsqueeze(2).to_broadcast([128, 4, 64]), op=ALU.mult), rd=[pg, dtk], wr=[xd16])
                    yield
                    k.op('dve', lambda e, pg=pg: e.tensor_copy(out=Btk[:], in_=pg[:, 256:384]), rd=[pg], wr=[Btk])
                    yield
                    k.op('pool', lambda e: e.tensor_tensor(out=LA[:], in0=slt[:].unsqueeze(1).to_broadcast([128, 4, 128]), in1=atok[:].unsqueeze(2).to_broadcast([128, 4, 128]), op=ALU.mult), rd=[slt, atok], wr=[LA])
                    yield
                    yield
                    pm = pM
                    pmv = pm[:, :].rearrange("p (h i) -> p h i", h=4)
                    for h in range(4):
                        mm(pmv[:, h, :], LA[:, h, :], umat[:], [LA, umat], [pm], start=True, stop=False)
                        mm(pmv[:, h, :], identf[:], negm[:], [identf, negm], [pm], start=False, stop=True)
                    k.op('act', lambda e, pmv=pmv: e.activation(out=LT[:], in_=pmv, func=AF.Exp), rd=[pm], wr=[LT])
                    yield
                    pgs = nextpg()
                    mm(pgs[:, 0:4], umat[:], atok[:], [umat, atok], [pgs], start=True, stop=True)
                    mm(pgs[:, 4:8], onesf[:], atok[:], [onesf, atok], [pgs], start=True, stop=True)
                    k.op('act', lambda e, pgs=pgs: e.activation(out=ecs[:], in_=pgs[:, 0:8], func=AF.Exp), rd=[pgs], wr=[ecs])
                    yield
                    yield
                    psc = pX[:, 0:256].rearrange("p (a b) -> p a b", a=2)
                    for gi in range(2):
                        mm(psc[:, gi, :], xc16[gi * 64:(gi + 1) * 64, 2, c0:c0 + 128], xc16[gi * 64:(gi + 1) * 64, 3, c0:c0 + 128], [xc16], [pX])
                    k.op('dve', lambda e, psc=psc: e.tensor_tensor(out=SL16[:].rearrange("p (g r) i -> p g r i", g=2), in0=psc.unsqueeze(2).to_broadcast([128, 2, 2, 128]), in1=LT[:].rearrange("p (g r) i -> p g r i", g=2), op=ALU.mult), rd=[pX, LT], wr=[SL16])
                    yield
                    yield
                    py = pS[1][:, :, :].rearrange("p a b -> p (a b)")
                    for h in range(4):
                        mm(py[:, h * 64:(h + 1) * 64], SL16[:, h, :], xd16[:, h, :], [SL16, xd16], [pS[1]])
                    for h in range(4):
                        g = h // 2
                        mm(py[:, 256 + h * 64:256 + (h + 1) * 64], xc16[g * 64:(g + 1) * 64, 3, c0:c0 + 128], hTh[g * 64:(g + 1) * 64, h, :], [xc16, hTh], [pS[1]])
                    k.op('dve', lambda e: e.tensor_copy(out=Yd[:], in_=py[:, 0:256]), rd=[pS[1]], wr=[Yd])
                    yield
                    k.op('dve', lambda e: e.tensor_tensor(out=Yt[:], in0=py[:, 256:512].rearrange("p (h d) -> p h d", h=4), in1=ecs[:, 0:4].unsqueeze(2).to_broadcast([128, 4, 64]), op=ALU.mult), rd=[pS[1], ecs], wr=[Yt])
                    yield
                    k.op('pool', lambda e: e.tensor_tensor(out=Yt[:], in0=Yt[:], in1=Yd[:].rearrange("p (h d) -> p h d", h=4), op=ALU.add), rd=[Yt, Yd], wr=[Yt])
                    yield
                    yield
                    ssd_epi(c0)
                    yield
                    k.op('pool', lambda e: e.tensor_tensor(out=Bd16[:], in0=Btk[:].unsqueeze(1).to_broadcast([128, 4, 128]), in1=LT[:, :, 127:128].to_broadcast([128, 4, 128]), op=ALU.mult), rd=[Btk, LT], wr=[Bd16])
                    yield
                    pst = pX[:, 256:512].rearrange("p (h d) -> p h d", h=4)
                    for h in range(4):
                        mm(pst[:, h, :], Bd16[:, h, :], xd16[:, h, :], [Bd16, xd16], [pX])
                    k.op('dve', lambda e: e.tensor_tensor(out=htmp[:], in0=hT32[:], in1=ecs[:, 4:8].unsqueeze(2).to_broadcast([128, 4, 64]), op=ALU.mult), rd=[hT32, ecs], wr=[htmp])
                    yield
                    k.op('dve', lambda e, pst=pst: e.tensor_tensor(out=hT32[:], in0=htmp[:], in1=pst, op=ALU.add), rd=[htmp, pX], wr=[hT32])
                    yield
                    k.op('act', lambda e: e.activation(out=hTh[:], in_=hT32[:], func=AF.Copy), rd=[hT32], wr=[hTh])
                    yield
                yield
            run_chains([('A', chain_A()), ('B', chain_B()), ('C', chain_C())])
            chk('ssd')
            chs = [('W', wout_chain(gidx))]
            if gidx + 1 < len(groups):
                chs.append(('P', proj_chain(gidx + 1)))
            run_chains(chs)
            if gidx + 2 < len(groups):
                load_g(gidx + 2)
            if dbg == ('mix', l) :
                k.dma(dbg_o[:, :, t0:t0 + gt], xb[:, :, 0:gt], dbg_o, xb)
            if gidx == len(groups) - 1:
                ec = nvalid - 1
                hr = dtv; hi = turns
                k.op('dve', lambda e: e.tensor_tensor(out=t8a[:], in0=cosT[:, :, ec], in1=gend_r[:], op=ALU.mult), rd=[cosT, gend_r], wr=[t8a])
                k.op('dve', lambda e: e.tensor_tensor(out=t8b[:], in0=sinT[:, :, ec], in1=gend_i[:], op=ALU.mult), rd=[sinT, gend_i], wr=[t8b])
                k.op('dve', lambda e: e.tensor_tensor(out=hr[:], in0=t8a[:], in1=t8b[:], op=ALU.subtract), rd=[t8a, t8b], wr=[hr])
                k.op('dve', lambda e: e.tensor_tensor(out=t8a[:], in0=sinT[:, :, ec], in1=gend_r[:], op=ALU.mult), rd=[sinT, gend_r], wr=[t8a])
                k.op('dve', lambda e: e.tensor_tensor(out=t8b[:], in0=cosT[:, :, ec], in1=gend_i[:], op=ALU.mult), rd=[cosT, gend_i], wr=[t8b])
                k.op('dve', lambda e: e.tensor_tensor(out=hi[:], in0=t8a[:], in1=t8b[:], op=ALU.add), rd=[t8a, t8b], wr=[hi])
                ore = tf; oim = den
                k.op('dve', lambda e: e.tensor_tensor(out=t8a[:], in0=fre[:], in1=hr[:], op=ALU.mult), rd=[fre, hr], wr=[t8a])
                k.op('dve', lambda e: e.tensor_tensor(out=t8b[:], in0=fim[:], in1=hi[:], op=ALU.mult), rd=[fim, hi], wr=[t8b])
                k.op('dve', lambda e: e.tensor_tensor(out=ore[:], in0=t8a[:], in1=t8b[:], op=ALU.subtract), rd=[t8a, t8b], wr=[ore])
                k.op('dve', lambda e: e.tensor_tensor(out=t8a[:], in0=fre[:], in1=hi[:], op=ALU.mult), rd=[fre, hi], wr=[t8a])
                k.op('dve', lambda e: e.tensor_tensor(out=t8b[:], in0=fim[:], in1=hr[:], op=ALU.mult), rd=[fim, hr], wr=[t8b])
                k.op('dve', lambda e: e.tensor_tensor(out=oim[:], in0=t8a[:], in1=t8b[:], op=ALU.add), rd=[t8a, t8b], wr=[oim])
                jv = junk[:, 0:512]
                pg = nextpg()
                k.op('pe', lambda e, pg=pg: e.transpose(pg[0:8, 0:128], ore[:], identf[:]), rd=[ore, identf], wr=[pg])
                k.op('pe', lambda e, pg=pg: e.transpose(pg[0:8, 128:256], oim[:], identf[:]), rd=[oim, identf], wr=[pg])
                k.op('dve', lambda e, pg=pg: e.tensor_copy(out=junk[0:8, 0:256], in_=pg[0:8, 0:256]), rd=[pg], wr=[junk])
                k.dma(s5re_p[l], junk[0:8, 0:128], s5re_p, junk); k.dma(s5im_p[l], junk[0:8, 128:256], s5im_p, junk)
                pg = nextpg()
                for j in range(4):
                    k.op('pe', lambda e, pg=pg, j=j: e.transpose(pg[0:3, j * 128:(j + 1) * 128], xpre[:, j, 8 + nvalid - 3:8 + nvalid], identf[:]), rd=[xpre, identf], wr=[pg])
                k.op('dve', lambda e, pg=pg: e.tensor_copy(out=junk[0:3, 0:512], in_=pg[0:3, 0:512]), rd=[pg], wr=[junk])
                k.dma(conv_p[l], junk[0:3, 0:512], conv_p, junk)
                pg = nextpg()
                for h in range(4):
                    g = h // 2
                    k.op('pe', lambda e, pg=pg, h=h, g=g: e.transpose(pg[0:64, h * 64:(h + 1) * 64], hT32[g * 64:(g + 1) * 64, h, :], identf[g * 64:(g + 1) * 64, g * 64:(g + 1) * 64]), rd=[hT32, identf], wr=[pg])
                k.op('dve', lambda e, pg=pg: e.tensor_copy(out=junk[0:64, 0:256], in_=pg[0:64, 0:256]), rd=[pg], wr=[junk])
                k.dma(ssd_p[l].rearrange("h p n -> p h n"), junk[0:64, 0:256].rearrange("p (h n) -> p h n", h=4), ssd_p, junk)
        if sample:
            gt = 128
            k.dma(xg[:, :, 0:128], xres_s[:], xg, xres_s)
            k.dma(ropc[:, 0:128], ropec_s_d[:], ropc, ropec_s_d); k.dma(rops[:, 0:128], ropes_s_d[:], rops, ropes_s_d)
            do_norm(128, xg)
            for _ in do_proj(128, 128): pass
            for _ in do_rope(128, sample=True): pass
            st0 = stg[0]; st1 = stg[1]
            S0v = st0[:, 0:2048].rearrange("p (a b) -> p a b", a=32); T1v = st1[:, 0:2048].rearrange("p (a b) -> p a b", a=32)
            pg = nextpg()
            for i in range(4):
                tr(pg[:, i * 128:(i + 1) * 128], qk32[:, i, 0:128], identf, [qk32], [pg])
            k.op('dve', lambda e, pg=pg: e.tensor_copy(out=junk[:, 0:512], in_=pg[:, 0:512]), rd=[pg], wr=[junk])
            pg = nextpg()
            for gi in range(2):
                k.op('pe', lambda e, pg=pg, gi=gi: e.transpose(pg[:, gi * 64:(gi + 1) * 64], rk32[0:64, gi, 0:128], identf[0:64, 0:64]), rd=[rk32, identf], wr=[pg])
            tr(pg[:, 128:256], vT32[:, 0:128], identf, [vT32], [pg])
            k.op('dve', lambda e, pg=pg: e.tensor_copy(out=ycg[:], in_=pg[:, 0:256].rearrange("p (a b) -> p a b", a=2)), rd=[pg], wr=[ycg])
            k.dma(k_s_d[l][:, 127, :], ycg[:, 0, :], k_s_d, ycg); k.dma(v_s_d[l][:, 127, :], ycg[:, 1, :], v_s_d, ycg)
            k.dma(k_s_d[l][:, 0:127, :], cache_k_d[l][:, 1:128, :], k_s_d, cache_k_d); k.dma(v_s_d[l][:, 0:127, :], cache_v_d[l][:, 1:128, :], v_s_d, cache_v_d)
            Sv = prodf[:, 0:1056].rearrange("p (h c) -> p h c", h=8)
            hb = []
            for nm, base in (("st0a", st0), ("st0b", st0), ("st1a", st1), ("st1b", st1)):
                off = 0 if nm.endswith("a") else 1024
                b_ = Buf(k, nm + "_%d" % l, base.t[:, off:off + 1024]); b_.lw = list(base.lw); b_.rd = list(base.rd)
                b_.ldsem = None
                hb.append(b_)
            Kh = [hb[0], hb[1]]; Th = [hb[2], hb[3]]
            kv3 = lambda b_: b_[:, 0:1024].rearrange("p (a b) -> p a b", a=16)
            stp = 0
            for gi in range(2):
                for c in range(8):
                    Kb = Kh[stp % 2]
                    k.dma(kv3(Kb), cache_k_d[l][:, 16 * c:16 * c + 16, gi * 64:(gi + 1) * 64], Kb, cache_k_d)
                    for r in range(4):
                        h = 4 * gi + r
                        Tb = Th[(stp * 4 + r) % 2]
                        k.op('pool' if r % 2 else 'dve', lambda e, h=h, Kb=Kb, Tb=Tb: e.tensor_tensor(out=kv3(Tb), in0=kv3(Kb), in1=junk[:, h * 64:(h + 1) * 64].unsqueeze(1).to_broadcast([128, 16, 64]), op=ALU.mult), rd=[Kb, junk], wr=[Tb])
                        k.op('dve', lambda e, h=h, c=c, Tb=Tb: e.tensor_reduce(out=Sv[:, h, 16 * c:16 * c + 16], in_=kv3(Tb), axis=AX.X, op=ALU.add), rd=[Tb], wr=[prodf])
                    stp += 1
            k.op('pool', lambda e: e.tensor_tensor(out=On[:].rearrange("p (g r) d -> p g r d", g=2), in0=junk[:, 0:512].rearrange("p (g r d) -> p g r d", g=2, r=4), in1=ycg[:, 0, :].rearrange("p (g d) -> p g d", g=2).unsqueeze(2).to_broadcast([128, 2, 4, 64]), op=ALU.mult), rd=[junk, ycg], wr=[On])
            k.op('dve', lambda e: e.tensor_reduce(out=Sv[:, :, 128], in_=On[:], axis=AX.X, op=ALU.add), rd=[On], wr=[prodf])
            k.op('dve', lambda e: e.tensor_reduce(out=mx[:], in_=Sv[:, :, 0:129], axis=AX.X, op=ALU.max), rd=[prodf], wr=[mx])
            k.op('dve', lambda e: e.scalar_tensor_tensor(out=ngm[:], in0=mx[:], scalar=-0.125, in1=negsink[:], op0=ALU.mult, op1=ALU.min), rd=[mx, negsink], wr=[ngm])
            k.op('pool', lambda e: e.memset(rs[:], 0.0), wr=[rs])
            for h in range(8):
                k.op('act', lambda e, h=h: e.activation(out=Sv[:, h, 0:129], in_=Sv[:, h, 0:129], func=AF.Exp, bias=ngm[:, h:h + 1], scale=0.125, accum_out=rs[:, h:h + 1]), rd=[prodf, ngm], wr=[prodf, rs])
            attn_den()
            k.op('pool', lambda e: e.memset(On[:], 0.0), wr=[On])
            for gi in range(2):
                for c in range(8):
                    Kb = Kh[stp % 2]
                    k.dma(kv3(Kb), cache_v_d[l][:, 16 * c:16 * c + 16, gi * 64:(gi + 1) * 64], Kb, cache_v_d)
                    for r in range(4):
                        h = 4 * gi + r
                        Tb = Th[(stp * 4 + r) % 2]
                        k.op('pool' if r % 2 else 'dve', lambda e, h=h, c=c, Kb=Kb, Tb=Tb: e.tensor_tensor(out=kv3(Tb), in0=kv3(Kb), in1=Sv[:, h, 16 * c:16 * c + 16].unsqueeze(2).to_broadcast([128, 16, 64]), op=ALU.mult), rd=[Kb, prodf], wr=[Tb])
                        k.op('dve', lambda e, Tb=Tb, r=r: e.tensor_reduce(out=htmp[:, r, :], in_=kv3(Tb).rearrange("p k d -> p d k"), axis=AX.X, op=ALU.add), rd=[Tb], wr=[htmp])
                    k.op('dve', lambda e, gi=gi: e.tensor_tensor(out=On[:, 4 * gi:4 * gi + 4, :], in0=On[:, 4 * gi:4 * gi + 4, :], in1=htmp[:], op=ALU.add), rd=[On, htmp], wr=[On])
                    stp += 1
            for b_ in hb:
                base = st0 if b_.name.startswith("st0") else st1
                for ev in b_.lw + b_.rd:
                    if ev not in base.rd: base.rd.append(ev)
            for h in range(8):
                g = h // 4
                k.op('dve', lambda e, h=h, g=g: e.scalar_tensor_tensor(out=On[:, h, :], in0=ycg[:, 1, g * 64:(g + 1) * 64], scalar=Sv[:, h, 128:129], in1=On[:, h, :], op0=ALU.mult, op1=ALU.add), rd=[ycg, prodf, On], wr=[On])
            k.op('dve', lambda e: e.tensor_tensor(out=On[:], in0=On[:], in1=rden[:].unsqueeze(2).to_broadcast([128, 8, 64]), op=ALU.mult), rd=[On, rden], wr=[On])
            attn_tail(0)
            h0r = st1[:, 0:1024].rearrange("p (a b) -> p a b", a=8); h0i = st1[:, 1024:2048].rearrange("p (a b) -> p a b", a=8)
            for src_d, dv in ((st_s5re_d, h0r), (st_s5im_d, h0i)):
                k.dma(st0[:, 0:1024], src_d[l], st0, src_d)
                for half in range(2):
                    pg = nextpg()
                    for jj in range(4):
                        j = half * 4 + jj
                        tr(pg[:, jj * 128:(jj + 1) * 128], st0[:, j * 128:(j + 1) * 128], identf, [st0], [pg])
                    k.op('dve', lambda e, pg=pg, dv=dv, half=half: e.tensor_copy(out=dv[:, half * 4:half * 4 + 4, :], in_=pg[:, :].rearrange("p (a b) -> p a b", a=4)), rd=[pg], wr=[st1])
            sc = lambda b_, j: b_[:, j:j + 1]
            for j in range(8):
                jq = j % 4
                pg1 = nextpg(); pg2 = nextpg()
                mm(pg1[:, 0:gt], Bre16[:, j // 4, j * 128:(j + 1) * 128], u16[:, j // 4, 0:gt], [Bre16, u16], [pg1])
                mm(pg2[:, 0:gt], Bim16[:, j // 4, j * 128:(j + 1) * 128], u16[:, j // 4, 0:gt], [Bim16, u16], [pg2])
                k.op('act', lambda e, pg1=pg1: e.activation(out=bur[:, 0:gt], in_=pg1[:, 0:gt], func=AF.Copy), rd=[pg1], wr=[bur])
                k.op('act', lambda e, pg2=pg2: e.activation(out=bui[:, 0:gt], in_=pg2[:, 0:gt], func=AF.Copy), rd=[pg2], wr=[bui])
                V = lambda b_: b_[:, 0:gt]
                ts = lambda o, i0, s1, rd_, wr_b: k.op('dve', lambda e: e.tensor_scalar(out=o, in0=i0, scalar1=s1, scalar2=None, op0=ALU.mult), rd=rd_, wr=[wr_b])
                stt = lambda o, i0, s1, i1, rd_, wr_b: k.op('dve', lambda e: e.scalar_tensor_tensor(out=o, in0=i0, scalar=s1, in1=i1, op0=ALU.mult, op1=ALU.add), rd=rd_, wr=[wr_b])
                ts(V(s5a), h0r[:, j, :], sc(finv_re, j), [st1, finv_re], s5a)
                stt(V(gr), h0i[:, j, :], sc(nfinv_im, j), V(s5a), [st1, nfinv_im, s5a], gr)
                ts(V(s5a), h0i[:, j, :], sc(finv_re, j), [st1, finv_re], s5a)
                stt(V(gi_), h0r[:, j, :], sc(finv_im, j), V(s5a), [st1, finv_im, s5a], gi_)
                stt(V(s5a), V(gr), sc(abr, j), V(bur), [gr, abr, bur], s5a)
                stt(V(wr_), V(gi_), sc(nabi, j), V(s5a), [gi_, nabi, s5a], wr_)
                stt(V(s5b), V(gi_), sc(abr, j), V(bui), [gi_, abr, bui], s5b)
                stt(V(wi_), V(gr), sc(abi, j), V(s5b), [gr, abi, s5b], wi_)
                k.op('pool', lambda e, jq=jq: e.tensor_copy(out=prod[:, jq, 0, 0:gt], in_=wr_[:, 0:gt]), rd=[wr_], wr=[prodf])
                k.op('pool', lambda e, jq=jq: e.tensor_copy(out=prod[:, jq, 1, 0:gt], in_=wi_[:, 0:gt]), rd=[wi_], wr=[prodf])
                ts(V(s5a), V(wr_), sc(fre, j), [wr_, fre], s5a)
                stt(h0r[:, j, :], V(wi_), sc(nfim, j), V(s5a), [wi_, nfim, s5a], st1)
                ts(V(s5a), V(wi_), sc(fre, j), [wi_, fre], s5a)
                stt(h0i[:, j, :], V(wr_), sc(fim, j), V(s5a), [wr_, fim, s5a], st1)
                if jq == 3:
                    m = j // 4
                    pgc = nextpg()
                    n = 0
                    for jj in range(4):
                        for q, Cm in enumerate([CRe, nCIm]):
                            mm(pgc[:, 0:gt], Cm[:, 4 * m + jj, :], prod[:, jj, q, 0:gt], [Cm, prodf], [pgc], start=(n == 0), stop=(n == 7)); n += 1
                    k.op('dve', lambda e, pgc=pgc, m=m: e.scalar_tensor_tensor(out=ys[:, m, 0:gt], in0=u32[:, m, 0:gt], scalar=colp[:, 48 + m:49 + m], in1=pgc[:, 0:gt], op0=ALU.mult, op1=ALU.add), rd=[u32, colp, pgc], wr=[ys])
            s5_epi(128)
            for dv, dst_d in ((h0r, s5re_s_d), (h0i, s5im_s_d)):
                for half in range(2):
                    pg = nextpg()
                    for jj in range(4):
                        tr(pg[:, jj * 128:(jj + 1) * 128], dv[:, half * 4 + jj, :], identf, [st1], [pg])
                    k.op('dve', lambda e, pg=pg, half=half: e.tensor_copy(out=st0[:, half * 512:(half + 1) * 512], in_=pg[:, :]), rd=[pg], wr=[st0])
                k.dma(dst_d[l], st0[:, 0:1024], dst_d, st0)
            k.dma(st0[:, 0:1536], st_conv_d[l], st0, st_conv_d)
            planes = st1[:, 0:1536].rearrange("p (t j b) -> p t j b", t=3, j=4)
            for tap in range(3):
                pg = nextpg()
                for j in range(4):
                    tr(pg[:, j * 128:(j + 1) * 128], st0[:, tap * 512 + j * 128:tap * 512 + (j + 1) * 128], identf, [st0], [pg])
                k.op('dve', lambda e, pg=pg, tap=tap: e.tensor_copy(out=planes[:, tap], in_=pg[:, :].rearrange("p (a b) -> p a b", a=4)), rd=[pg], wr=[st1])
            for j in range(4):
                wc = lambda tap: colp[:, 52 + j * 4 + tap:53 + j * 4 + tap]
                k.op('pool', lambda e, j=j, w0=wc(0): e.tensor_scalar(out=xc32[:, j, 0:gt], in0=planes[:, 0, j], scalar1=w0, scalar2=colp[:, 68 + j:69 + j], op0=ALU.mult, op1=ALU.add), rd=[st1, colp], wr=[xc32])
                for tap in (1, 2):
                    k.op('dve', lambda e, j=j, tap=tap, w=wc(tap): e.scalar_tensor_tensor(out=xc32[:, j, 0:gt], in0=planes[:, tap, j], scalar=w, in1=xc32[:, j, 0:gt], op0=ALU.mult, op1=ALU.add), rd=[st1, colp, xc32], wr=[xc32])
                k.op('dve', lambda e, j=j, w=wc(3): e.scalar_tensor_tensor(out=xc32[:, j, 0:gt], in0=xpre[:, j, 8:8 + gt], scalar=w, in1=xc32[:, j, 0:gt], op0=ALU.mult, op1=ALU.add), rd=[xpre, colp, xc32], wr=[xc32])
            k.op('act', lambda e: e.activation(out=xc32[:, :, 0:gt], in_=xc32[:, :, 0:gt], func=AF.Silu), rd=[xc32], wr=[xc32])
            k.op('act', lambda e: e.activation(out=xc16[:, :, 0:gt], in_=xc32[:, :, 0:gt], func=AF.Copy), rd=[xc32], wr=[xc16])
            k.dma(conv_s_d[l][:, 0:1024], st_conv_d[l][:, 512:1536], conv_s_d, st_conv_d)
            pg = nextpg()
            for j in range(4):
                tr(pg[:, j * 128:(j + 1) * 128], xpre[:, j, 8:8 + 128], identf, [xpre], [pg])
            k.op('dve', lambda e, pg=pg: e.tensor_copy(out=junk[:, 0:512], in_=pg[:, :]), rd=[pg], wr=[junk])
            k.dma(conv_s_d[l][:, 1024:1536], junk[:, 0:512], conv_s_d, junk)
            pg = nextpg()
            for j in range(4):
                tr(pg[:, j * 128:(j + 1) * 128], xc32[:, j, 0:128], identf, [xc32], [pg])
            xtk = LA[:].rearrange("p a b -> p (a b)")
            k.op('dve', lambda e, pg=pg: e.tensor_copy(out=xtk, in_=pg[:, :]), rd=[pg], wr=[LA])
            pg = nextpg()
            k.op('pe', lambda e, pg=pg: e.transpose(pg[:, 0:4], dtT[0:4, 0:128], identf[0:4, 0:4]), rd=[dtT, identf], wr=[pg])
            k.op('dve', lambda e, pg=pg: e.tensor_copy(out=dtk[:], in_=pg[:, 0:4]), rd=[pg], wr=[dtk])
            k.op('dve', lambda e: e.tensor_tensor(out=atok[:], in0=dtk[:], in1=Atab[:], op=ALU.mult), rd=[dtk, Atab], wr=[atok])
            k.op('act', lambda e: e.activation(out=ecs[:, 0:4], in_=atok[:], func=AF.Exp), rd=[atok], wr=[ecs])
            k.op('dve', lambda e: e.tensor_tensor(out=Yd[:].rearrange("p (h d) -> p h d", h=4), in0=xtk[:, 0:256].rearrange("p (h d) -> p h d", h=4), in1=dtk[:].unsqueeze(2).to_broadcast([128, 4, 64]), op=ALU.mult), rd=[LA, dtk], wr=[Yd])
            for h in range(4):
                g = h // 2
                for ph in range(2):
                    o0 = h * 4096 + ph * 2048
                    k.dma(st0[:, 0:2048], st_ssd_d[l][:, o0:o0 + 2048], st0, st_ssd_d)
                    k.op('dve', lambda e, h=h, ph=ph, g=g: e.tensor_tensor(out=T1v, in0=Yd[:, h * 64 + ph * 32:h * 64 + ph * 32 + 32].unsqueeze(2).to_broadcast([128, 32, 64]), in1=xtk[:, 256 + g * 64:256 + (g + 1) * 64].unsqueeze(1).to_broadcast([128, 32, 64]), op=ALU.mult), rd=[Yd, LA], wr=[st1])
                    k.op('dve', lambda e, h=h: e.scalar_tensor_tensor(out=S0v, in0=S0v, scalar=ecs[:, h:h + 1], in1=T1v, op0=ALU.mult, op1=ALU.add), rd=[st0, ecs, st1], wr=[st0])
                    k.dma(ssd_s_d[l][:, o0:o0 + 2048], st0[:, 0:2048], ssd_s_d, st0)
                    k.op('dve', lambda e, g=g: e.tensor_tensor(out=T1v, in0=S0v, in1=xtk[:, 384 + g * 64:384 + (g + 1) * 64].unsqueeze(1).to_broadcast([128, 32, 64]), op=ALU.mult), rd=[st0, LA], wr=[st1])
                    k.op('dve', lambda e, h=h, ph=ph: e.tensor_reduce(out=Yt[:, h, ph * 32:(ph + 1) * 32], in_=T1v, axis=AX.X, op=ALU.add), rd=[st1], wr=[Yt])
            ssd_epi(0)
            for _ in wout(128): pass
            k.dma(xres_s[:], xg[:, :, 0:128], xres_s, xg)
        chk('mixgroups')
        Onf = On[:].rearrange("p h d -> p (h d)")
        k.dma(Onf[:, 0:128], kscr[:, T - 128:T], On, kscr); k.dma(Onf[:, 128:256], vscr[:, T - 128:T], On, vscr)
        pg = nextpg()
        for a in range(2):
            tr(pg[:, a * 128:(a + 1) * 128], Onf[:, a * 128:(a + 1) * 128], identf, [On], [pg])
        k.op('dve', lambda e, pg=pg: e.tensor_copy(out=junk[:, 0:256], in_=pg[:, 0:256]), rd=[pg], wr=[junk])
        k.dma(k_p[l], junk[:, 0:128], k_p, junk); k.dma(v_p[l], junk[:, 128:256], v_p, junk)
        A.release()
        barrier(k)
        chk('mix')
        A.mark()
        Wg16 = A.buf("Wg16", [128, 8, 2816], BF16); Wu16 = A.buf("Wu16", [128, 8, 2816], BF16); Wd16 = A.buf("Wd16", [128, 22, 1024], BF16)
        load_w(Wg16, lambda c: w_g_d[l, c * 128:(c + 1) * 128, :], 8, 2816, lambda c: colp[:, 16 + c:17 + c])
        load_w(Wu16, lambda c: w_u_d[l, c * 128:(c + 1) * 128, :], 8, 2816, lambda c: colp[:, 16 + c:17 + c])
        load_w(Wd16, lambda c: w_d_d[l, c * 128:(c + 1) * 128, :], 22, 1024)
        sq16 = A.buf("sq16f", [128, 8, G], BF16); hTs = [hT16, A.buf("hT16b", [128, 8, G], BF16)]
        actf = A.buf("actf", [128, 11 * G]); actT = Buf(k, "actT", actf.t.bitcast(BF16).rearrange("p (a b) -> p a b", a=22)); actT.lw = actf.lw; actT.rd = actf.rd; sgt = [A.buf("sgt%d" % i, [128, G]) for i in range(2)]
        yT = Buf(k, "yT", actf.t[:, 0:8 * G].rearrange("p (a b) -> p a b", a=8)); yT.lw = actf.lw; yT.rd = actf.rd; yo = A.buf("yo", [128, 1024])
        ffn_items = [('p', t0, gt) for (t0, gt) in groups] + ([('s', 0, 128)] if sample else [])
        def ffn_src(it):
            kind_, t0_, gt_ = it
            return (xres, xres[:, :, t0_:t0_ + gt_]) if kind_ == 'p' else (xres_s, xres_s[:])
        for fi, (kind, t0, gt) in enumerate(ffn_items):
            xsrc, xsl = ffn_src(ffn_items[fi])
            xg = xgs[fi % 2]
            if fi == 0:
                k.dma(xg[:, :, 0:gt], xsl, xg, xsrc)
            if fi + 1 < len(ffn_items):
                xsrcn, xsln = ffn_src(ffn_items[fi + 1]); xgn = xgs[(fi + 1) % 2]
                k.dma(xgn[:, :, 0:ffn_items[fi + 1][2]], xsln, xgn, xsrcn)
            hTc = hTs[fi % 2]
            if fi == 0:
                rmsnorm_T(xg, 8, gt, 1024.0, hTc, sq16, rstd)
            for ht in range(22):
                pg1 = nextpg()
                for kt in range(8):
                    mm(pg1[:, 0:gt], Wg16[:, kt, ht * 128:(ht + 1) * 128], hTc[:, kt, 0:gt], [Wg16, hTc], [pg1], start=(kt == 0), stop=(kt == 7))
                st_ = sgt[ht % 2]
                k.op('act', lambda e, pg1=pg1, st_=st_: e.activation(out=st_[:, 0:gt], in_=pg1[:, 0:gt], func=AF.Silu), rd=[pg1], wr=[st_])
                pg2 = nextpg()
                for kt in range(8):
                    mm(pg2[:, 0:gt], Wu16[:, kt, ht * 128:(ht + 1) * 128], hTc[:, kt, 0:gt], [Wu16, hTc], [pg2], start=(kt == 0), stop=(kt == 7))
                k.op('dve', lambda e, pg2=pg2, st_=st_, ht=ht: e.tensor_tensor(out=actT[:, ht, 0:gt], in0=st_[:, 0:gt], in1=pg2[:, 0:gt], op=ALU.mult), rd=[st_, pg2], wr=[actT])
            if fi + 1 < len(ffn_items):
                rmsnorm_T(xgn, 8, ffn_items[fi + 1][2], 1024.0, hTs[(fi + 1) % 2], sq16, rstd)
            for dt_ in range(8):
                pg = nextpg()
                for ht in range(22):
                    mm(pg[:, 0:gt], Wd16[:, ht, dt_ * 128:(dt_ + 1) * 128], actT[:, ht, 0:gt], [Wd16, actT], [pg], start=(ht == 0), stop=(ht == 21))
                k.op('dve', lambda e, pg=pg, dt_=dt_: e.tensor_tensor(out=xg[:, dt_, 0:gt], in0=xg[:, dt_, 0:gt], in1=pg[:, 0:gt], op=ALU.add), rd=[xg, pg], wr=[xg])
            if l < L - 1:
                k.dma(xsl, xg[:, :, 0:gt], xsrc, xg)
            else:
                k.op('act', lambda e: e.activation(out=sq16[:, :, 0:gt], in_=xg[:, :, 0:gt], func=AF.Square), rd=[xg], wr=[sq16])
                pg = nextpg()
                for i in range(8):
                    mm(pg[:, 0:gt], onesb[:], sq16[:, i, 0:gt], [onesb, sq16], [pg], start=(i == 0), stop=(i == 7))
                k.op('act', lambda e, pg=pg: e.activation(out=rstd[:, 0:gt], in_=pg[:, 0:gt], func=AF.Ln, scale=1.0 / 1024, bias=epsc[:, 0:1]), rd=[pg, epsc], wr=[rstd])
                k.op('act', lambda e: e.activation(out=rstd[:, 0:gt], in_=rstd[:, 0:gt], func=AF.Exp, scale=-0.5), rd=[rstd], wr=[rstd])
                for i in range(8):
                    k.op('dve', lambda e, i=i: e.scalar_tensor_tensor(out=yT[:, i, 0:gt], in0=xg[:, i, 0:gt], scalar=lnf[:, i:i + 1], in1=rstd[:, 0:gt], op0=ALU.mult, op1=ALU.mult), rd=[xg, lnf, rstd], wr=[yT])
                for bi in range(gt // 128):
                    tok0 = t0 + bi * 128
                    lo = max(tok0, 16); hi_ = min(tok0 + 128, T)
                    if kind == 's': lo, hi_ = 0, 128
                    if hi_ <= lo: continue
                    for half in range(2):
                        pg = nextpg()
                        for j in range(4):
                            tr(pg[:, j * 128:(j + 1) * 128], yT[:, half * 4 + j, bi * 128:(bi + 1) * 128], identf, [yT], [pg])
                        k.op('dve' if half == 0 else 'act',
                             (lambda e, pg=pg, half=half: e.tensor_copy(out=yo[:, half * 512:(half + 1) * 512], in_=pg[:, :])) if half == 0 else
                             (lambda e, pg=pg, half=half: e.activation(out=yo[:, half * 512:(half + 1) * 512], in_=pg[:, :], func=AF.Copy)), rd=[pg], wr=[yo])
                    if kind == 's':
                        k.dma(y_sample[:], yo[:], y_sample, yo)
                    else:
                        k.dma(y_prompt[lo - 16:hi_ - 16, :], yo[lo - tok0:hi_ - tok0, :], y_prompt, yo)
        xg = xgs[0]
        A.release()
    k.finish()
    k.close()
    print('arena high-water', A.hw, 'of', A.words)
    return k

def prep_inputs(inp, b, seq, L=4):
    f32=np.float32
    T=seq+16; NB=(T+127)//128; TP=NB*128
    d={}
    xin=np.zeros((TP,1024),f32); xin[:16]=inp['meta_tokens']; xin[16:T]=inp['x_prompt'][b,:seq]
    d['xin']=xin
    d['ident']=np.eye(128,dtype=f32)
    R=np.zeros((128,128),f32)
    for m in range(128):
        if m%64<32: R[m+32,m]=-1.0
        else: R[m-32,m]=1.0
    d['rotm']=R
    half=32
    inv=(10000.0**(-np.arange(half,dtype=np.float32)/half)).astype(f32)
    pos=np.arange(TP,dtype=f32)
    ang=(pos[None,:]*inv[:,None]).astype(f32)
    d['ropec']=np.tile(np.cos(ang).astype(f32),(4,1)); d['ropes']=np.tile(np.sin(ang).astype(f32),(4,1))
    NEG=-240000.0
    i=np.arange(128)[:,None]; j=np.arange(128)[None,:]
    m0=np.concatenate([np.where(j>=i,0.0,NEG),np.where(j<=i,0.0,NEG)],axis=1).astype(f32)
    m1=m0.copy(); m1[:,:128]=NEG
    d['amask']=np.stack([m0,m1])
    kk=np.arange(128)[:,None]; jj=np.arange(128)[None,:]
    d['slt']=(kk>jj).astype(f32); d['umat']=(kk<=jj).astype(f32)
    d['negm']=np.where(jj<kk,-30000.0,0.0).astype(f32)
    d['iota']=np.tile(np.arange(G,dtype=f32)[None,:],(128,1))
    w=inp['w_in'][:L]
    d['w_in_x']=np.ascontiguousarray(np.concatenate([w[:,:,0:512],w[:,:,512:576],w[:,:,512:576],w[:,:,576:640],w[:,:,576:640],w[:,:,640:1796]],axis=2))
    d['w_out']=np.ascontiguousarray(inp['w_out'][:L]); d['w_gate']=np.ascontiguousarray(inp['w_gate'][:L]); d['w_up']=np.ascontiguousarray(inp['w_up'][:L]); d['w_down']=np.ascontiguousarray(inp['w_down'][:L])
    d['glu_w']=np.ascontiguousarray(inp['s5_glu_w'][:L])
    bre=np.zeros((L,256,1024),f32); bim=np.zeros((L,256,1024),f32)
    for g in range(16):
        bre[:,g*16:(g+1)*16,g*64:(g+1)*64]=np.transpose(inp['s5_b_re'][:L,g],(0,2,1))
        bim[:,g*16:(g+1)*16,g*64:(g+1)*64]=np.transpose(inp['s5_b_im'][:L,g],(0,2,1))
    d['bblk_re']=bre; d['bblk_im']=bim
    cre=np.zeros((L,128,8,128),f32); cim=np.zeros((L,128,8,128),f32)
    for g in range(16):
        jch=g//2; r0=(g%2)*64; c0=(g%8)*16
        cre[:,r0:r0+64,jch,c0:c0+16]=np.transpose(inp['s5_c_re'][:L,g],(0,2,1))
        cim[:,r0:r0+64,jch,c0:c0+16]=np.transpose(inp['s5_c_im'][:L,g],(0,2,1))
    d['cpad_re']=cre; d['cpad_im']=cim
    NC=76
    cp=np.zeros((L,128,NC),f32)
    col=lambda v: np.transpose(v.reshape(L,-1,128),(0,2,1))
    cp[:,:,0:8]=col(inp['ln1_g'][:L])
    gmix=np.concatenate([inp['attn_out_g'][:L],inp['s5_out_g'][:L],inp['ssd_norm_g'][:L]],axis=1)
    cp[:,:,8:16]=col(gmix); cp[:,:,16:24]=col(inp['ln2_g'][:L])
    chan=lambda v: np.transpose(v.reshape(L,8,2*64),(0,2,1))
    cp[:,:,24:32]=chan(inp['s5_a_re'][:L]); cp[:,:,32:40]=chan(inp['s5_a_im'][:L])
    cp[:,:,40:48]=chan(np.repeat(inp['s5_log_dt'][:L,:,None],64,axis=2))
    cp[:,:,48:50]=col(inp['s5_d'][:L]); cp[:,:,50:52]=col(inp['s5_glu_b'][:L])
    cw=inp['ssd_conv_w'][:L]
    for j in range(4):
        for tap in range(4):
            cp[:,:,52+j*4+tap]=cw[:,tap,j*128:(j+1)*128]
    cp[:,:,68:72]=col(inp['ssd_conv_b'][:L])
    cp[:,:,72:74]=col(np.repeat(inp['ssd_d'][:L],64,axis=1))
    cp[:,0:4,74]=inp['ssd_dt_bias'][:L]
    d['colpack']=cp
    rp=np.zeros((L,1,16),f32); rp[:,0,0:8]=inp['attn_sinks'][:L]; rp[:,0,8:12]=inp['ssd_a_log'][:L]
    d['rowpack']=rp
    d['lnf_cols']=np.ascontiguousarray(inp['lnf_g'].reshape(8,128).T)
    d['xs_in']=np.ascontiguousarray(inp['x_sample'][:,0,:])
    d['cache_k']=np.ascontiguousarray(inp['cache_k'][:L]).reshape(L,128,128,128); d['cache_v']=np.ascontiguousarray(inp['cache_v'][:L]).reshape(L,128,128,128)
    d['st_s5re']=np.ascontiguousarray(inp['state_s5_re'][:L]).reshape(L,128,1024); d['st_s5im']=np.ascontiguousarray(inp['state_s5_im'][:L]).reshape(L,128,1024)
    d['st_conv']=np.ascontiguousarray(inp['state_ssd_conv'][:L]).reshape(L,128,1536); d['st_ssd']=np.ascontiguousarray(inp['state_ssd'][:L]).reshape(L,128,16384)
    angs=(np.float32(8192.0)*inv).astype(f32)
    d['ropec_s']=np.tile(np.tile(np.cos(angs).astype(f32),4)[:,None],(1,128)).astype(f32); d['ropes_s']=np.tile(np.tile(np.sin(angs).astype(f32),4)[:,None],(1,128)).astype(f32)
    return d


_CACHE = {}

def kernel(**inputs):
    from concourse.bass_utils import run_bass_kernel_spmd
    inp = {k_: np.asarray(v) for k_, v in inputs.items()}
    seq = inp['x_prompt'].shape[1]; L = inp['w_in'].shape[0]; B = inp['x_prompt'].shape[0]
    key = (seq, L)
    if key not in _CACHE:
        _CACHE[key] = build(seq, L)
    kb = _CACHE[key]
    per_seq = [prep_inputs(inp, b, seq, L) for b in range(B)]
    for b in range(1, B):
        for n in per_seq[0]:
            if n != 'xin':
                per_seq[b][n] = per_seq[0][n]
    n_cores = 8
    in_maps = [per_seq[c % B] for c in range(n_cores)]
    res = run_bass_kernel_spmd(kb.nc, in_maps, core_ids=list(range(n_cores))).results
    f32 = np.float32
    y_prompt = np.stack([res[b]['y_prompt'] for b in range(B)]).astype(f32)
    st = lambda n: np.stack([res[b][n] for b in range(B)], axis=1)
    k_p = st('k_p').reshape(L, B, 128, 2, 64); v_p = st('v_p').reshape(L, B, 128, 2, 64)
    s5re = st('s5re_p').reshape(L, B, 16, 64); s5im = st('s5im_p').reshape(L, B, 16, 64)
    conv_p = st('conv_p'); ssd_p = st('ssd_p')
    DB = inp['x_sample'].shape[0]
    r0 = res[0]
    y_sample = r0['y_sample'].reshape(DB, 1, 1024).astype(f32)
    k_s = r0['k_s'].reshape(L, DB, 128, 2, 64); v_s = r0['v_s'].reshape(L, DB, 128, 2, 64)
    s5re_s = r0['s5re_s'].reshape(L, DB, 16, 64); s5im_s = r0['s5im_s'].reshape(L, DB, 16, 64)
    conv_s = r0['conv_s'].reshape(L, DB, 3, 512); ssd_s = r0['ssd_s'].reshape(L, DB, 4, 64, 64)
    return (y_prompt, y_sample, k_p, v_p, s5re, s5im, conv_p, ssd_p, k_s, v_s, s5re_s, s5im_s, conv_s, ssd_s)
```
